# Optimizing a Trainium2 kernel written in Bass

```python
import math
import jax
import jax.numpy as jnp
from jax import lax
import numpy as np

D_MODEL = 1024
BATCH = 4
SEQ = 4096
DEPTH = 2

GRID_W = 64
CTX_LEN = 256
HEAD_DIM = 64
N_BRANCH = 4
BRANCH_WIDTH = D_MODEL // N_BRANCH
A_HEADS = BRANCH_WIDTH // HEAD_DIM
B_HEADS = BRANCH_WIDTH // HEAD_DIM
B_GROUPS = 2
B_STATE = 128
B_CONV = 5
C_HEADS = BRANCH_WIDTH // HEAD_DIM
NA_WIN_R = 8
NA_WIN_C = 16
NA_QBLK = 16
NA_KSPAN = 32
D_HEADS = BRANCH_WIDTH // HEAD_DIM
D_KV_HEADS = 2
D_KV_WIDTH = D_KV_HEADS * HEAD_DIM
Q_BLOCK = 128
ROPE_BASE = 10000.0
SCAN_CHUNK = 64
LN_EPS = 1e-6
FFN_HIDDEN = -(-8 * D_MODEL // (3 * 256)) * 256
SSD_CONV_CH = BRANCH_WIDTH + 2 * B_GROUPS * B_STATE
IN_SPLITS = ((BRANCH_WIDTH,) * 5
             + (BRANCH_WIDTH, SSD_CONV_CH, B_HEADS, B_HEADS)
             + (BRANCH_WIDTH,) * 3
             + (BRANCH_WIDTH, D_KV_WIDTH, D_KV_WIDTH)
             + (N_BRANCH * D_MODEL,))
IN_COLS = sum(IN_SPLITS)
IN_OFFSETS = [int(o) for o in np.cumsum(IN_SPLITS)[:-1]]
DEEPNORM_ALPHA = (2.0 * DEPTH) ** 0.25
DEEPNORM_BETA = (8.0 * DEPTH) ** -0.25

kernel_name = "hybrid_gated_dit_block"


def layer_norm(x):
    xf = x.astype(jnp.float32)
    mu = jnp.mean(xf, -1, keepdims=True)
    var = jnp.mean(jnp.square(xf - mu), -1, keepdims=True)
    return ((xf - mu) * lax.rsqrt(var + LN_EPS)).astype(x.dtype)


def rms_norm(x, w):
    xf = x.astype(jnp.float32)
    return (xf * lax.rsqrt(jnp.mean(xf * xf, -1, keepdims=True) + LN_EPS)).astype(x.dtype) * w


def modulate(x, shift, scale):
    return layer_norm(x) * (1.0 + scale) + shift


def post_norm(x, y, g, b):
    return layer_norm(DEEPNORM_ALPHA * x + y) * g + b


def split_heads(a, n):
    return a.reshape(a.shape[0], a.shape[1], n, -1)


def head_major(a, n):
    return split_heads(a, n).transpose(0, 2, 1, 3)


def axial_rope_angles(T):
    t = jnp.arange(T)
    nf = HEAD_DIM // 4
    inv_freq = ROPE_BASE ** (-jnp.arange(nf, dtype=jnp.float32) / nf)
    row = (t // GRID_W).astype(jnp.float32)[:, None] * inv_freq
    col = (t % GRID_W).astype(jnp.float32)[:, None] * inv_freq
    return row, col


def rope_half(x, ang):
    x1, x2 = jnp.split(x, 2, axis=-1)
    cos = jnp.cos(ang)[:, None, :].astype(x.dtype)
    sin = jnp.sin(ang)[:, None, :].astype(x.dtype)
    return jnp.concatenate([x1 * cos - x2 * sin, x2 * cos + x1 * sin], axis=-1)


def rope_2d(x, ang_r, ang_c):
    xr, xc = jnp.split(x, 2, axis=-1)
    return jnp.concatenate([rope_half(xr, ang_r), rope_half(xc, ang_c)], axis=-1)


def dwconv_centred(x, w, b):
    K = w.shape[0]
    y = lax.conv_general_dilated(x, w[:, None, :], window_strides=(1,), padding=[(K // 2, K // 2)],
                                 dimension_numbers=('NWC', 'WIO', 'NWC'), feature_group_count=x.shape[-1])
    return y + b


def chunked_scan(q, k, v, logf, s0):
    Bsz, H, T, _ = q.shape
    n = T // SCAN_CHUNK
    out_dtype = v.dtype

    def chunks(a):
        return a.astype(jnp.float32).reshape(Bsz, H, n, SCAN_CHUNK, a.shape[-1]).transpose(2, 0, 1, 3, 4)

    tri = jnp.tril(jnp.ones((SCAN_CHUNK, SCAN_CHUNK), dtype=bool))[:, :, None]
    scalar = logf.shape[-1] == 1

    def step(S, inp):
        qc, kc, vc, gc = inp
        b = jnp.cumsum(gc, axis=2)
        decay = jnp.exp(jnp.where(tri, b[:, :, :, None, :] - b[:, :, None, :, :], -jnp.inf))
        if scalar:
            scores = jnp.einsum('bhtk,bhsk->bhts', qc, kc) * decay[..., 0]
        else:
            scores = jnp.einsum('bhtk,bhsk,bhtsk->bhts', qc, kc, decay)
        o = jnp.einsum('bhts,bhsv->bhtv', scores, vc) + jnp.einsum('bhtk,bhkv->bhtv', qc * jnp.exp(b), S)
        b_end = b[:, :, -1:, :]
        S = S * jnp.exp(b_end[:, :, 0, :])[..., None] + jnp.einsum('bhsk,bhsv->bhkv', kc * jnp.exp(b_end - b), vc)
        return S, o

    S, o = lax.scan(step, s0, (chunks(q), chunks(k), chunks(v), chunks(logf)))
    return o.transpose(1, 2, 0, 3, 4).reshape(Bsz, H, T, v.shape[-1]).astype(out_dtype), S


def bidir_scan(q, dirs, qc, dirs_c):
    o_lat, o_ctx = [], []
    for rev, ((k, v, g), (kc, vc, gc)) in zip((False, True), zip(dirs, dirs_c)):
        fl = (lambda a: jnp.flip(a, axis=2)) if rev else (lambda a: a)
        s0 = jnp.zeros(q.shape[:2] + (k.shape[-1], v.shape[-1]), jnp.float32)
        oc, s_ctx = chunked_scan(fl(qc), fl(kc), fl(vc), fl(gc), s0)
        ol, _ = chunked_scan(fl(q), fl(k), fl(v), fl(g), s_ctx)
        o_lat.append(fl(ol))
        o_ctx.append(fl(oc))
    return o_lat[0] + o_lat[1], o_ctx[0] + o_ctx[1]


def hgrn_lower_bound(lb_param, l):
    sm = jax.nn.softmax(lb_param.astype(jnp.float32), axis=0)
    return jnp.cumsum(sm, axis=0)[l] - sm[0]


def hgrn2_inputs(q, f_fwd, f_bwd, v, lb_fwd, lb_bwd):
    dirs = []
    for f, lb in ((f_fwd, lb_fwd), (f_bwd, lb_bwd)):
        fg = lb + (1.0 - lb) * jax.nn.sigmoid(f.astype(jnp.float32))
        dirs.append((head_major(1.0 - fg, A_HEADS), head_major(v, A_HEADS), head_major(jnp.log(fg), A_HEADS)))
    return head_major(q, A_HEADS), dirs


def hgrn2_out(o, g, w):
    o = rms_norm(o.transpose(0, 2, 1, 3), w.reshape(A_HEADS, HEAD_DIM))
    return o.reshape(o.shape[0], o.shape[1], BRANCH_WIDTH) * jax.nn.silu(g)


def ssd_inputs(xbc, dt_fwd, dt_bwd, conv_w, conv_b, dt_bias, a_log):
    xbc = jax.nn.silu(dwconv_centred(xbc, conv_w, conv_b))
    xs, bm, cm = jnp.split(xbc, [BRANCH_WIDTH, BRANCH_WIDTH + B_GROUPS * B_STATE], axis=-1)
    rep = B_HEADS // B_GROUPS

    def group_heads(a):
        return jnp.repeat(split_heads(a, B_GROUPS), rep, axis=2).transpose(0, 2, 1, 3)

    xh = split_heads(xs, B_HEADS)
    k = group_heads(bm)
    dirs = []
    for dt_raw, bias, alog in ((dt_fwd, dt_bias[0], a_log[0]), (dt_bwd, dt_bias[1], a_log[1])):
        dt = jax.nn.softplus(dt_raw + bias)
        v = (xh * dt[..., None]).transpose(0, 2, 1, 3)
        g = (dt * -jnp.exp(alog)).transpose(0, 2, 1)[..., None]
        dirs.append((k, v, g))
    return group_heads(cm), dirs, xh


def ssd_out(y, xh, z, d_skip, w):
    y = y.transpose(0, 2, 1, 3) + xh * d_skip[:, None]
    y = y.reshape(xh.shape[0], xh.shape[1], BRANCH_WIDTH) * jax.nn.silu(z)
    return rms_norm(y, w)


def attend_dense(q, k, v):
    Bsz, S, Hq, d = q.shape
    Hkv = k.shape[2]
    qg = q.reshape(Bsz, S, Hkv, Hq // Hkv, d)
    s = jnp.einsum('bqkgd,bskd->bkgqs', qg, k).astype(jnp.float32) * d ** -0.5
    p = jax.nn.softmax(s, axis=-1).astype(v.dtype)
    return jnp.einsum('bkgqs,bskd->bqkgd', p, v).reshape(Bsz, S, Hq * d)


def na_latent(q, k, v, kc, vc, rpb):
    Bsz, T, H, d = q.shape
    rows = T // GRID_W
    wr = min(NA_WIN_R, rows)
    ncb = GRID_W // NA_QBLK
    r = np.arange(rows)
    row_idx = np.clip(r - wr // 2, 0, rows - wr)[:, None] + np.arange(wr)
    qcol = np.arange(GRID_W).reshape(ncb, NA_QBLK)
    kcol = np.clip(qcol[:, :1] - NA_WIN_C // 2, 0, GRID_W - NA_KSPAN) + np.arange(NA_KSPAN)
    win0 = np.clip(qcol - NA_WIN_C // 2, 0, GRID_W - NA_WIN_C)[:, :, None]
    in_win = (kcol[:, None, :] >= win0) & (kcol[:, None, :] < win0 + NA_WIN_C)
    dcol = np.clip(kcol[:, None, :] - qcol[:, :, None] + NA_WIN_C - 1, 0, 2 * NA_WIN_C - 2)
    drow = row_idx - r[:, None] + NA_WIN_R - 1
    bias = rpb[:, drow[:, None, None, :, None], dcol[None, :, :, None, :]]
    bias = jnp.where(in_win[None, None, :, :, None, :], bias, -jnp.inf)
    ridx = row_idx[:, :, None, None]
    cidx = kcol[None, None]
    kg = k.reshape(Bsz, rows, GRID_W, H, d)[:, ridx, cidx]
    vg = v.reshape(Bsz, rows, GRID_W, H, d)[:, ridx, cidx]
    qg = q.reshape(Bsz, rows, ncb, NA_QBLK, H, d)
    scale = d ** -0.5
    s_win = jnp.einsum('brjqhd,brwjkhd->bhrjqwk', qg, kg).astype(jnp.float32) * scale + bias
    s_ctx = jnp.einsum('brjqhd,bchd->bhrjqc', qg, kc).astype(jnp.float32) * scale
    nw = wr * NA_KSPAN
    p = jax.nn.softmax(jnp.concatenate([s_win.reshape(s_win.shape[:5] + (nw,)), s_ctx], axis=-1), axis=-1).astype(v.dtype)
    o = (jnp.einsum('bhrjqwk,brwjkhd->brjqhd', p[..., :nw].reshape(s_win.shape), vg)
         + jnp.einsum('bhrjqc,bchd->brjqhd', p[..., nw:], vc))
    return o.reshape(Bsz, T, H * d)


def gqa_latent(q, k, v, kc, vc):
    Bsz, T, Hq, d = q.shape
    Hkv = k.shape[2]
    k_all = jnp.concatenate([k, kc], axis=1)
    v_all = jnp.concatenate([v, vc], axis=1)
    qb = q.reshape(Bsz, T // Q_BLOCK, Q_BLOCK, Hkv, Hq // Hkv, d).transpose(1, 0, 2, 3, 4, 5)

    def block(qi):
        s = jnp.einsum('bqkgd,bskd->bkgqs', qi, k_all).astype(jnp.float32) * d ** -0.5
        p = jax.nn.softmax(s, axis=-1).astype(v_all.dtype)
        return jnp.einsum('bkgqs,bskd->bqkgd', p, v_all)

    o = lax.map(block, qb)
    return o.transpose(1, 0, 2, 3, 4, 5).reshape(Bsz, T, Hq * d)


def gated_merge(branches, gate_logits, w_branch, w_out):
    br = jnp.stack(branches, axis=2)
    proj = jnp.einsum('btnw,nwd->btnd', br, w_branch)
    g = jax.nn.sigmoid(gate_logits.reshape(proj.shape))
    return jnp.sum(g * proj, axis=2) @ w_out


def token_mixers(h, hc, w_in, lb_f, lb_b, hgrn_norm, conv_w, conv_b, dt_bias, a_log, d_skip, ssd_norm,
                 rpb, q_norm, k_norm, w_branch, w_out, ang_r, ang_c, ctx_out):
    (a_q, a_ff, a_fb, a_v, a_g, b_z, b_xbc, b_dtf, b_dtb, c_q, c_k, c_v, d_q, d_k, d_v, gate) = jnp.split(h @ w_in, IN_OFFSETS, axis=-1)
    (ac_q, ac_ff, ac_fb, ac_v, ac_g, bc_z, bc_xbc, bc_dtf, bc_dtb, cc_q, cc_k, cc_v, dc_q, dc_k, dc_v, gate_c) = jnp.split(hc @ w_in, IN_OFFSETS, axis=-1)
    qa, dirs_a = hgrn2_inputs(a_q, a_ff, a_fb, a_v, lb_f, lb_b)
    qac, dirs_ac = hgrn2_inputs(ac_q, ac_ff, ac_fb, ac_v, lb_f, lb_b)
    oa, oac = bidir_scan(qa, dirs_a, qac, dirs_ac)
    qb, dirs_b, xb = ssd_inputs(b_xbc, b_dtf, b_dtb, conv_w, conv_b, dt_bias, a_log)
    qbc, dirs_bc, xbc_ = ssd_inputs(bc_xbc, bc_dtf, bc_dtb, conv_w, conv_b, dt_bias, a_log)
    ob, obc = bidir_scan(qb, dirs_b, qbc, dirs_bc)
    kc_na, vc_na = split_heads(cc_k, C_HEADS), split_heads(cc_v, C_HEADS)
    y_c = na_latent(split_heads(c_q, C_HEADS), split_heads(c_k, C_HEADS), split_heads(c_v, C_HEADS), kc_na, vc_na, rpb)
    kc_g, vc_g = rms_norm(split_heads(dc_k, D_KV_HEADS), k_norm), split_heads(dc_v, D_KV_HEADS)
    q_g = rope_2d(rms_norm(split_heads(d_q, D_HEADS), q_norm), ang_r, ang_c)
    k_g = rope_2d(rms_norm(split_heads(d_k, D_KV_HEADS), k_norm), ang_r, ang_c)
    y_d = gqa_latent(q_g, k_g, split_heads(d_v, D_KV_HEADS), kc_g, vc_g)
    y = gated_merge([hgrn2_out(oa, a_g, hgrn_norm), ssd_out(ob, xb, b_z, d_skip, ssd_norm), y_c, y_d], gate, w_branch, w_out)
    if not ctx_out:
        return y, None
    yc_c = attend_dense(split_heads(cc_q, C_HEADS), kc_na, vc_na)
    yc_d = attend_dense(rms_norm(split_heads(dc_q, D_HEADS), q_norm), kc_g, vc_g)
    yc = gated_merge([hgrn2_out(oac, ac_g, hgrn_norm), ssd_out(obc, xbc_, bc_z, d_skip, ssd_norm), yc_c, yc_d], gate_c, w_branch, w_out)
    return y, yc


def swiglu(h, w_up, w_down):
    g, u = jnp.split(h @ w_up, 2, axis=-1)
    return (jax.nn.silu(g) * u) @ w_down


def setup_inputs(seed: int = 0) -> dict:
    key = jax.random.key(seed)
    ks = jax.random.split(key, 26)
    L = DEPTH

    def nrm(k, shape, scale):
        return jax.random.normal(k, shape, jnp.float32) * scale

    def gain(k, shape):
        return 1.0 + nrm(k, shape, 0.02)

    dt = jnp.exp(jax.random.uniform(ks[12], (L, 2, B_HEADS), jnp.float32, math.log(1e-3), math.log(1e-1)))
    return {
        "x": nrm(ks[0], (BATCH, SEQ, D_MODEL), 1.0),
        "c": nrm(ks[1], (BATCH, D_MODEL), 1.0),
        "ctx": nrm(ks[2], (BATCH, CTX_LEN, D_MODEL), 1.0),
        "c_ctx": nrm(ks[3], (D_MODEL,), 1.0),
        "ada_w": nrm(ks[4], (L, D_MODEL, 6 * D_MODEL), 0.5 * D_MODEL ** -0.5),
        "ada_b": nrm(ks[5], (L, 6 * D_MODEL), 0.02),
        "w_in": nrm(ks[6], (L, D_MODEL, IN_COLS), D_MODEL ** -0.5),
        "hgrn_lb": nrm(ks[7], (2, L, BRANCH_WIDTH), 0.5),
        "hgrn_norm": gain(ks[8], (L, BRANCH_WIDTH)),
        "ssd_conv_w": nrm(ks[9], (L, B_CONV, SSD_CONV_CH), B_CONV ** -0.5),
        "ssd_conv_b": nrm(ks[10], (L, SSD_CONV_CH), 0.02),
        "ssd_dt_bias": dt + jnp.log(-jnp.expm1(-dt)),
        "ssd_a_log": jnp.log(jax.random.uniform(ks[11], (L, 2, B_HEADS), jnp.float32, 1.0, 16.0)),
        "ssd_d": gain(ks[13], (L, B_HEADS)),
        "ssd_norm": gain(ks[14], (L, BRANCH_WIDTH)),
        "na_rpb": nrm(ks[15], (L, C_HEADS, 2 * NA_WIN_R - 1, 2 * NA_WIN_C - 1), 0.1),
        "q_norm": gain(ks[16], (L, HEAD_DIM)),
        "k_norm": gain(ks[17], (L, HEAD_DIM)),
        "w_branch": nrm(ks[18], (L, N_BRANCH, BRANCH_WIDTH, D_MODEL), BRANCH_WIDTH ** -0.5),
        "w_out": nrm(ks[19], (L, D_MODEL, D_MODEL), DEEPNORM_BETA * D_MODEL ** -0.5),
        "ln1_g": gain(ks[20], (L, D_MODEL)),
        "ln1_b": nrm(ks[21], (L, D_MODEL), 0.02),
        "ffn_w_up": nrm(ks[22], (L, D_MODEL, 2 * FFN_HIDDEN), D_MODEL ** -0.5),
        "ffn_w_down": nrm(ks[23], (L, FFN_HIDDEN, D_MODEL), DEEPNORM_BETA * FFN_HIDDEN ** -0.5),
        "ln2_g": gain(ks[24], (L, D_MODEL)),
        "ln2_b": nrm(ks[25], (L, D_MODEL), 0.02),
    }


def reference(x, c, ctx, c_ctx, ada_w, ada_b, w_in, hgrn_lb, hgrn_norm, ssd_conv_w, ssd_conv_b, ssd_dt_bias,
              ssd_a_log, ssd_d, ssd_norm, na_rpb, q_norm, k_norm, w_branch, w_out, ln1_g, ln1_b,
              ffn_w_up, ffn_w_down, ln2_g, ln2_b):
    ang_r, ang_c = axial_rope_angles(x.shape[1])
    xc = ctx
    for l in range(DEPTH):
        last = l == DEPTH - 1
        mod = jnp.split(jax.nn.silu(c) @ ada_w[l] + ada_b[l], 6, axis=-1)
        shift1, scale1, gate1, shift2, scale2, gate2 = [m[:, None, :] for m in mod]
        shift1c, scale1c, gate1c, shift2c, scale2c, gate2c = jnp.split(jax.nn.silu(c_ctx) @ ada_w[l] + ada_b[l], 6, axis=-1)
        h = modulate(x, shift1, scale1)
        hc = modulate(xc, shift1c, scale1c)
        y, yc = token_mixers(h, hc, w_in[l], hgrn_lower_bound(hgrn_lb[0], l), hgrn_lower_bound(hgrn_lb[1], l),
                             hgrn_norm[l], ssd_conv_w[l], ssd_conv_b[l], ssd_dt_bias[l], ssd_a_log[l], ssd_d[l],
                             ssd_norm[l], na_rpb[l], q_norm[l], k_norm[l], w_branch[l], w_out[l], ang_r, ang_c,
                             not last)
        x = post_norm(x, gate1 * y, ln1_g[l], ln1_b[l])
        x = post_norm(x, gate2 * swiglu(modulate(x, shift2, scale2), ffn_w_up[l], ffn_w_down[l]), ln2_g[l], ln2_b[l])
        if not last:
            xc = post_norm(xc, gate1c * yc, ln1_g[l], ln1_b[l])
            xc = post_norm(xc, gate2c * swiglu(modulate(xc, shift2c, scale2c), ffn_w_up[l], ffn_w_down[l]), ln2_g[l], ln2_b[l])
    return x
```

```python
import types
import numpy as np
from contextlib import ExitStack
import concourse.bass as bass
import concourse.mybir as mybir
from concourse.bass_utils import run_bass_kernel_spmd

F32 = mybir.dt.float32
BF16 = mybir.dt.bfloat16
AF = mybir.ActivationFunctionType
ALU = mybir.AluOpType
AX = mybir.AxisListType

D = 1024
NCTX = 256
NLAT = 4096
NT = NCTX + NLAT
NTILE = NT // 128
DEPTH = 2
EPS = 1e-6
ALPHA = (2.0 * DEPTH) ** 0.25
FH = 2816
FM_COLS = 256 * 3 + 768 + 256 + 256
FM_DT = 8
TM_COLS = 256 * 5 + 128 + 128 + 4096
O_AQ, O_AFF, O_AFB, O_AV, O_AG = 0, 256, 512, 768, 1024
O_BZ, O_XBC, O_DTF, O_DTB = 1280, 1536, 2304, 2308
O_CQ, O_CK, O_CV = 2312, 2568, 2824
O_DQ, O_DK, O_DV = 3080, 3336, 3464
O_GATE = 3592
T_AV, T_AG, T_BZ, T_CV, T_DQ, T_DK, T_DV, T_GATE = 0, 256, 512, 768, 1024, 1280, 1408, 1536
R_AQ, R_AFF, R_AFB, R_XBC, R_CQ, R_CK, R_DT = 0, 256, 512, 768, 1536, 1792, 2048


def na_plan_and_blocks():
    blocks = {}
    plan = []
    for i in range(32):
        starts = [min(max(2 * i + a - 4, 0), 56) for a in range(2)]
        lo = min(starts) // 2
        hi = (max(starts) + 7) // 2
        lst = []
        for j in range(lo, hi + 1):
            ids = []
            for a in range(2):
                qr = 2 * i + a
                key = []
                for half in range(2):
                    kr = 2 * j + half
                    valid = starts[a] <= kr < starts[a] + 8
                    key.append((valid, kr - qr if valid else 0))
                key = tuple(key)
                if key not in blocks:
                    blocks[key] = len(blocks)
                ids.append(blocks[key])
            lst.append((j, ids[0], ids[1]))
        plan.append(lst)
    return plan, blocks


NA_PLAN, NA_BLOCKS = na_plan_and_blocks()
NA_NBLK = len(NA_BLOCKS)
NEG = -30000.0


class Buf:
    __slots__ = ("name", "w", "r", "sem", "dcount", "kind")

    def __init__(self, name):
        self.name = name
        self.w = None
        self.r = {}
        self.sem = None
        self.dcount = 0
        self.kind = None


def _freeze(f):
    if f is None or f.__closure__ is None:
        return f
    cells = []
    for c in f.__closure__:
        try:
            cells.append(types.CellType(c.cell_contents))
        except ValueError:
            cells.append(c)
    return types.FunctionType(f.__code__, f.__globals__, f.__name__, f.__defaults__, tuple(cells))


class Sched:
    ENGS = ("pe", "act", "dve", "pool", "sp")

    def __init__(self, nc, es):
        self.nc = nc
        self.es = es
        self.lists = {e: [] for e in self.ENGS}
        self.esem = {e: es.enter_context(nc.semaphore("sem_" + e)) for e in ("pe", "act", "dve", "pool")}
        self.ecount = {e: 0 for e in ("pe", "act", "dve", "pool")}
        self.waited = {e: {} for e in self.ENGS}
        self.nsem = 0
        self.nrot = 0
        self.free = {"sw": [], "hw": []}
        self.sem_bufs = []
        self.dsems = {}
        self.sem_owner = {}
        for e, s in self.esem.items():
            self.sem_owner[id(s)] = e

    def _deps(self, eng, reads, writes):
        need = {}

        def add(tok):
            if tok is None:
                return
            sem, val = tok
            k = id(sem)
            if k not in need or need[k][1] < val:
                need[k] = (sem, val)

        for b in reads:
            add(b.w)
        for b in writes:
            add(b.w)
            for t in b.r.values():
                add(t)
        waits = []
        for k, (sem, val) in need.items():
            if eng == "pe" and self.sem_owner.get(k) == "pe":
                continue
            if self.waited[eng].get(k, 0) >= val:
                continue
            self.waited[eng][k] = val
            waits.append((sem, val))
        return waits

    def _mark(self, tok, reads, writes):
        k = id(tok[0])
        for b in reads:
            b.r[k] = tok
        for b in writes:
            b.w = tok
            b.r = {}

    def op(self, eng, thunk, reads=(), writes=()):
        waits = self._deps(eng, reads, writes)
        self.ecount[eng] += 1
        tok = (self.esem[eng], self.ecount[eng])
        self.lists[eng].append((waits, _freeze(thunk), tok[0], 1))
        self._mark(tok, reads, writes)

    def dma(self, queue, thunk, reads, writes, sembuf=None):
        waits = self._deps(queue, reads, writes)
        sb = sembuf if sembuf is not None else writes[0]
        kind = "sw" if queue == "pool" else "hw"
        if sb.sem is not None and sb.kind != kind:
            raise AssertionError("buffer %s written by both DMA queue kinds" % sb.name)
        if sb.sem is None:
            sb.kind = kind
            if self.free[kind]:
                sb.sem, sb.dcount = self.free[kind].pop()
            else:
                sb.sem = self.es.enter_context(self.nc.semaphore("dsem%d" % self.nsem))
                sb.dcount = 0
                self.nsem += 1
            self.sem_bufs.append(sb)
        sb.dcount += 16
        tok = (sb.sem, sb.dcount)
        self.dsems[id(sb.sem)] = tok
        self.lists[queue].append((waits, _freeze(thunk), tok[0], 16))
        self._mark(tok, reads, writes)

    def barrier(self):
        toks = [(self.esem[e], self.ecount[e]) for e in self.ecount if self.ecount[e] > 0]
        toks += list(self.dsems.values())
        for eng in self.ENGS:
            waits = []
            for (sem, val) in toks:
                k = id(sem)
                if eng == "pe" and self.sem_owner.get(k) == "pe":
                    continue
                if self.waited[eng].get(k, 0) >= val:
                    continue
                self.waited[eng][k] = val
                waits.append((sem, val))
            if waits:
                self.lists[eng].append((waits, None, None, 0))
        for b in self.sem_bufs:
            self.free[b.kind].append((b.sem, b.dcount))
            b.sem = None
        self.sem_bufs = []
        for e in list(self.ecount):
            if self.ecount[e] > 12000:
                ns = self.es.enter_context(self.nc.semaphore("sem_%s_%d" % (e, self.nrot)))
                self.nrot += 1
                self.esem[e] = ns
                self.ecount[e] = 0
                self.sem_owner[id(ns)] = e

    def final_wait(self, eng, bufs):
        waits = self._deps(eng, bufs, ())
        self.lists[eng].append((waits, None, None, 0))

    def emit(self):
        nc = self.nc
        lists = self.lists

        def run(engname, e):
            for waits, thunk, sem, inc in lists[engname]:
                for (s, v) in waits:
                    e.wait_ge(s, v)
                if thunk is not None:
                    ins = thunk(e)
                    ins.then_inc(sem, inc)

        with nc.Block() as block:
            @block.tensor
            def _(e):
                run("pe", e)

            @block.vector
            def _(e):
                run("dve", e)

            @block.scalar
            def _(e):
                run("act", e)

            @block.gpsimd
            def _(e):
                run("pool", e)

            @block.sync
            def _(e):
                run("sp", e)


class Ctx:
    pass


def build_program(n_layers=DEPTH, stop_after=None, debug=(), skip=()):
    nc = bass.Bass("TRN2", target_bir_lowering=False)
    es = ExitStack()
    with es:
        S = Sched(nc, es)
        g = Ctx()
        g.nc, g.S, g.es = nc, S, es

        def dram_in(name, shape, dt=F32):
            return nc.dram_tensor(name, list(shape), dt, kind="ExternalInput").ap()

        def dram_scratch(name, shape, dt=F32):
            kind = "ExternalOutput" if name in debug else "Internal"
            return nc.dram_tensor(name, list(shape), dt, kind=kind).ap()

        SB_WORDS = 53000
        big = es.enter_context(nc.sbuf_tensor("bigsb", [128, SB_WORDS], F32))
        sbtop = [0]

        def sb(name, shape, dt=F32):
            shape = list(shape)
            nel = 1
            for s_ in shape[1:]:
                nel *= s_
            esz = 4 if dt == F32 else 2
            nw = (nel * esz + 3) // 4
            nw = (nw + 7) // 8 * 8
            off = sbtop[0]
            assert off + nw <= SB_WORDS, "SBUF overflow at %s: %d + %d" % (name, off, nw)
            sbtop[0] = off + nw
            ap = big[0:shape[0], off:off + nw]
            if dt != F32:
                ap = ap.bitcast(dt)
            ap = ap[:, 0:nel]
            if len(shape) == 3:
                ap = ap.rearrange("p (a b) -> p a b", a=shape[1])
            elif len(shape) == 4:
                ap = ap.rearrange("p (a b c) -> p a b c", a=shape[1], b=shape[2])
            return ap

        class Scope:
            def __enter__(self_):
                self_.mark = sbtop[0]
                return self_

            def __exit__(self_, *a):
                S.barrier()
                sbtop[0] = self_.mark
                return False

        def ps(name, shape, dt=F32):
            return es.enter_context(nc.psum_tensor(name, list(shape), dt))

        xin = dram_in("xin", [NT, D])
        cvec = dram_in("cvec", [128, 16])
        ada_w = dram_in("ada_w", [DEPTH, D, 6 * D])
        ada_b = dram_in("ada_b", [DEPTH, 6 * D])
        w_fm = dram_in("w_fm", [DEPTH, D, FM_COLS + 128])
        w_tm = dram_in("w_tm", [DEPTH, D, TM_COLS])
        w_br = dram_in("w_br", [DEPTH, D, D])
        w_o = dram_in("w_o", [DEPTH, D, D])
        w_up = dram_in("w_up", [DEPTH, D, 2 * FH])
        w_dn = dram_in("w_dn", [DEPTH, FH, D])
        lnp = dram_in("lnp", [DEPTH, 4, D])
        ident_in = dram_in("ident", [128, 128])
        rope_cs = dram_in("rope_cs", [NT, 128])
        qkw = dram_in("qkw", [DEPTH, 384])
        natab_in = dram_in("natab", [DEPTH, 128, NA_NBLK * 256])
        hmask_in = dram_in("hmask", [4, 128, 128])
        lbp_in = dram_in("lbp", [128, 8])
        hnorm_in = dram_in("hnorm", [DEPTH, 256])
        convp_in = dram_in("convp", [DEPTH, 128, 6, 6])
        dtp_in = dram_in("dtp", [DEPTH, 8, 4])
        dsk_in = dram_in("dsk", [DEPTH, 256])
        snorm_in = dram_in("snorm", [DEPTH, 256])
        selr_in = dram_in("selr", [8, 8, 128])
        negm_in = dram_in("negm", [2, 128, 128])
        out = nc.dram_tensor("out", [NLAT, D], F32, kind="ExternalOutput").ap()

        XRES = dram_scratch("XRES", [NT, D])
        X1 = dram_scratch("X1", [NT, D])
        MODD = dram_scratch("MODD", [2, 6 * D])
        PT = dram_scratch("PT", [FM_COLS + 128, NT])
        PTOK = dram_scratch("PTOK", [NT, TM_COLS])
        BR = dram_scratch("BR", [NT, D])
        ACTT = dram_scratch("ACTT", [FH, NT], BF16)
        XS = dram_scratch("XS", [NT, 256])
        XCT = dram_scratch("XCT", [512, NT], BF16)
        BTOK = dram_scratch("BTOK", [NT, 256], BF16)
        B_XS, B_XCT, B_BTOK = Buf("XS"), Buf("XCT"), Buf("BTOK")
        OA_dr = [dram_scratch("OA0", [NT, 256]), dram_scratch("OA1", [NT, 256])]
        B_OA = Buf("OA")
        DBG = {}
        for nm, shp in (("DBG_tokq", [128, NTILE * 32]), ("DBG_cum", [8, NT]), ("DBG_ddb", [128, 8 * (NT // 64)]),
                        ("DBG_oacc", [128, NTILE * 256]), ("DBG_h1", [128, 5 * NT]), ("DBG_dd", [128, NT // 64])):
            if nm in debug:
                DBG[nm] = (nc.dram_tensor(nm, shp, F32, kind="ExternalOutput").ap(), Buf(nm))

        def dbg_dump(nm, src_ap, bsrc, col0=0, ncol=None, eng="sp"):
            if nm not in DBG:
                return
            dst, bd = DBG[nm]
            n = ncol if ncol is not None else dst.shape[1]
            S.dma(eng, lambda e: e.dma_start(out=dst[0:src_ap.shape[0], col0:col0 + n], in_=src_ap), [bsrc], [bd])
        B_xin, B_XRES, B_X1, B_MODD, B_PT, B_PTOK, B_BR, B_ACTT, B_out = (
            Buf(n) for n in ("xin", "XRES", "X1", "MODD", "PT", "PTOK", "BR", "ACTT", "out"))
        B_const = Buf("constin")

        PS = ps("PS", [128, 4096], F32)
        psum = [PS[:, i * 512:(i + 1) * 512] for i in range(8)]
        B_ps = [Buf("ps%d" % i) for i in range(8)]
        rr = [0, 0]

        def nb(lo=0, hi=8):
            rr[0] += 1
            return lo + rr[0] % (hi - lo)

        def nb2(lo=0, hi=8):
            rr[1] += 1
            return lo + 2 * (rr[1] % ((hi - lo) // 2))

        ident_f = sb("ident_f", [128, 128], F32)
        ident_b = sb("ident_b", [128, 128], BF16)
        B_identf, B_identb = Buf("identf"), Buf("identb")
        S.dma("sp", lambda e: e.dma_start(out=ident_f[:], in_=ident_in), [B_const], [B_identf])
        S.op("dve", lambda e: e.tensor_copy(out=ident_b[:], in_=ident_f[:]), [B_identf], [B_identb])
        eps_t = sb("eps_t", [128, 1], F32)
        B_eps = Buf("eps")
        S.op("pool", lambda e: e.memset(eps_t[:], EPS), [], [B_eps])

        def load_bcast(name, src_row_ap, n, bsrc):
            t = sb(name, [128, n], F32)
            b = Buf(name)
            S.dma("sp", lambda e: e.dma_start(out=t[:], in_=src_row_ap.partition_broadcast(128)), [bsrc], [b])
            return t, b

        def load_weight_bf16(name, src3, kc, ncols, stg, bstg, chunk=512):
            wt = sb(name, [128, kc, ncols], BF16)
            bw = Buf(name)
            for ci, c0 in enumerate(range(0, ncols, chunk)):
                cn = min(chunk, ncols - c0)
                st, bs = stg[ci % len(stg)], bstg[ci % len(stg)]
                stv = st[:, 0:kc * cn].rearrange("p (k n) -> p k n", k=kc)
                S.dma("sp", lambda e, stv=stv, c0=c0, cn=cn: e.dma_start(out=stv, in_=src3[:, :, c0:c0 + cn]), [B_const], [bs])
                S.op("pool", lambda e, stv=stv, c0=c0, cn=cn: e.tensor_copy(out=wt[:, :, c0:c0 + cn], in_=stv), [bs], [bw])
            return wt, bw

        def stage_mod(l):
            csb = sb("csb", [128, 16], F32)
            sil = sb("sil", [128, 16], F32)
            B_c, B_sil = Buf("c"), Buf("sil")
            S.dma("sp", lambda e: e.dma_start(out=csb[:], in_=cvec), [B_const], [B_c])
            S.op("act", lambda e: e.activation(out=sil[:], in_=csb[:], func=AF.Silu), [B_c], [B_sil])
            adab = sb("adab", [1, 6 * D], F32)
            B_adab = Buf("adab")
            S.dma("sp", lambda e: e.dma_start(out=adab[:], in_=ada_b[l:l + 1, :]), [B_const], [B_adab])
            wst = [sb("adaw%d" % i, [128, 8, 512], F32) for i in range(2)]
            B_wst = [Buf("adaw%d" % i) for i in range(2)]
            modrow = sb("modrow", [1, 2, 6 * D], F32)
            B_modrow = Buf("modrow")
            for gi in range(12):
                w = wst[gi % 2]
                bw = B_wst[gi % 2]
                src = ada_w[l].rearrange("(kc p) n -> p kc n", p=128)[:, :, gi * 512:(gi + 1) * 512]
                S.dma("sp", lambda e, w=w, src=src: e.dma_start(out=w[:], in_=src), [B_const], [bw])
                for which in range(2):
                    pb = nb()
                    for kc in range(8):
                        S.op("pe", lambda e, pb=pb, kc=kc, w=w, which=which: e.matmul(
                            psum[pb][0:1, :], lhsT=sil[:, which * 8 + kc: which * 8 + kc + 1], rhs=w[:, kc, :],
                            start=(kc == 0), stop=(kc == 7)), [B_sil, bw], [B_ps[pb]])
                    S.op("dve", lambda e, pb=pb, which=which, gi=gi: e.tensor_tensor(
                        out=modrow[0:1, which, gi * 512:(gi + 1) * 512], in0=psum[pb][0:1, :],
                        in1=adab[0:1, gi * 512:(gi + 1) * 512], op=ALU.add), [B_ps[pb], B_adab], [B_modrow])
            S.dma("sp", lambda e: e.dma_start(out=MODD.rearrange("(o a) n -> o a n", o=1), in_=modrow[:]),
                  [B_modrow], [B_MODD])

        def load_mod_pair(tag, idx, plus_one=False):
            res = []
            for which in range(2):
                t, b = load_bcast("modb_%s_%d" % (tag, which), MODD[which, idx * D:(idx + 1) * D], D, B_MODD)
                if plus_one:
                    S.op("pool", lambda e, t=t: e.tensor_scalar_add(out=t[:], in0=t[:], scalar1=1.0), [b], [b])
                res.append((t, b))
            return res

        def make_ln_scr(tag):
            return dict(st=sb(tag + "st", [128, 2, 6], F32), mv=sb(tag + "mv", [128, 2], F32),
                        rstd=sb(tag + "rs", [128, 1], F32), bst=Buf("st"), bmv=Buf("mv"), brs=Buf("rs"))

        def ln_stats(xt, bx, scr):
            st, mv, rstd = scr["st"], scr["mv"], scr["rstd"]
            bst, bmv, brs = scr["bst"], scr["bmv"], scr["brs"]
            for hlf in range(2):
                S.op("dve", lambda e, hlf=hlf: e.bn_stats(out=st[:, hlf, :], in_=xt[:, hlf * 512:(hlf + 1) * 512]),
                     [bx], [bst])
            S.op("dve", lambda e: e.bn_aggr(out=mv[:], in_=st[:]), [bst], [bmv])
            S.op("act", lambda e: e.activation(out=rstd[:], in_=mv[:, 1:2], func=AF.Sqrt, bias=eps_t[:], scale=1.0),
                 [bmv, B_eps], [brs])
            S.op("dve", lambda e: e.reciprocal(out=rstd[:], in_=rstd[:]), [brs], [brs])

        def transpose8(src_bf, bsrc, dst3, bdst, eng="act"):
            pb = nb()
            pt = psum[pb].bitcast(BF16)
            for kc in range(8):
                S.op("pe", lambda e, kc=kc, pt=pt: e.transpose(
                    out=pt[:, kc * 128:(kc + 1) * 128], in_=src_bf[:, kc * 128:(kc + 1) * 128], identity=ident_b[:]),
                    [bsrc, B_identb], [B_ps[pb]])
            if eng == "act":
                S.op("act", lambda e, pt=pt: e.copy(out=dst3, in_=pt[:, 0:1024].rearrange("p (k t) -> p k t", k=8)),
                     [B_ps[pb]], [bdst])
            else:
                S.op("dve", lambda e, pt=pt: e.tensor_copy(out=dst3, in_=pt[:, 0:1024].rearrange("p (k t) -> p k t", k=8)),
                     [B_ps[pb]], [bdst])

        def stage_ln_mod(l, src, bsrc, shift_idx, scale_idx, hT, B_hT, tiles):
            mods = load_mod_pair("sh", shift_idx)
            modsc = load_mod_pair("sc", scale_idx, plus_one=True)
            NB = 2
            xts = [sb("lnx%d" % i, [128, D], F32) for i in range(NB)]
            bxs = [Buf("lnx") for i in range(NB)]
            hts = [sb("lnh%d" % i, [128, D], BF16) for i in range(NB)]
            bhs = [Buf("lnh") for i in range(NB)]
            scr = make_ln_scr("ln")
            xn = sb("lnxn", [128, D], F32)
            bxn = Buf("xn")
            for n, tt in enumerate(tiles):
                xt, bx, ht, bh = xts[n % NB], bxs[n % NB], hts[n % NB], bhs[n % NB]
                which = 1 if tt < 2 else 0
                S.dma("sp", lambda e, xt=xt, tt=tt: e.dma_start(out=xt[:], in_=src[tt * 128:(tt + 1) * 128, :]),
                      [bsrc], [bx])
                ln_stats(xt, bx, scr)
                S.op("dve", lambda e, xt=xt: e.tensor_scalar(out=xn[:], in0=xt[:], scalar1=scr["mv"][:, 0:1],
                                                             scalar2=scr["rstd"][:], op0=ALU.subtract, op1=ALU.mult),
                     [bx, scr["bmv"], scr["brs"]], [bxn])
                S.op("pool", lambda e, which=which: e.tensor_tensor(out=xn[:], in0=xn[:], in1=modsc[which][0][:],
                                                                    op=ALU.mult), [bxn, modsc[which][1]], [bxn])
                S.op("pool", lambda e, which=which, ht=ht: e.tensor_tensor(out=ht[:], in0=xn[:], in1=mods[which][0][:],
                                                                           op=ALU.add), [bxn, mods[which][1]], [bh])
                transpose8(ht, bh, hT[:, :, tt * 128:(tt + 1) * 128], B_hT[tt])

        def stage_inproj(l, hT, B_hT):
            NB = 2
            wst = [sb("wst%d" % i, [128, 8, 512], F32) for i in range(NB)]
            bwst = [Buf("wst") for i in range(NB)]
            wbf = [sb("wbf%d" % i, [128, 8, 512], BF16) for i in range(NB)]
            bwbf = [Buf("wbf") for i in range(NB)]
            NE = 6
            ev = [sb("ipev%d" % i, [128, 512], F32) for i in range(NE)]
            bev = [Buf("ipev") for i in range(NE)]
            cnt = [0]
            for gi in range(TM_COLS // 512):
                w, bw, wb, bwb = wst[gi % NB], bwst[gi % NB], wbf[gi % NB], bwbf[gi % NB]
                src = w_tm[l].rearrange("(kc p) n -> p kc n", p=128)[:, :, gi * 512:(gi + 1) * 512]
                S.dma("sp", lambda e, w=w, src=src: e.dma_start(out=w[:], in_=src), [B_const], [bw])
                S.op("pool", lambda e, w=w, wb=wb: e.tensor_copy(out=wb[:], in_=w[:]), [bw], [bwb])
                is_gate = gi * 512 >= T_GATE
                for tt in range(NTILE):
                    i = cnt[0]
                    cnt[0] += 1
                    pb = nb()
                    for kc in range(8):
                        S.op("pe", lambda e, pb=pb, kc=kc, wb=wb, tt=tt: e.matmul(
                            psum[pb], lhsT=hT[:, kc, tt * 128:(tt + 1) * 128], rhs=wb[:, kc, :],
                            start=(kc == 0), stop=(kc == 7)), [B_hT[tt], bwb], [B_ps[pb]])
                    evt, bevt = ev[i % NE], bev[i % NE]
                    if is_gate:
                        S.op("act", lambda e, pb=pb, evt=evt: e.activation(out=evt[:], in_=psum[pb], func=AF.Sigmoid),
                             [B_ps[pb]], [bevt])
                    elif i % 2 == 0:
                        S.op("dve", lambda e, pb=pb, evt=evt: e.tensor_copy(out=evt[:], in_=psum[pb]),
                             [B_ps[pb]], [bevt])
                    else:
                        S.op("act", lambda e, pb=pb, evt=evt: e.copy(out=evt[:], in_=psum[pb]),
                             [B_ps[pb]], [bevt])
                    S.dma("pool", lambda e, evt=evt, tt=tt, gi=gi: e.dma_start(
                        out=PTOK[tt * 128:(tt + 1) * 128, gi * 512:(gi + 1) * 512], in_=evt[:]), [bevt], [B_PTOK])
            TG = [(i * 512, 512) for i in range(NT // 512)] + ([(NT // 512 * 512, NT % 512)] if NT % 512 else [])
            ngrp = FM_COLS // 512 + 1
            for gi in range(ngrp):
                ncol = 512 if gi < FM_COLS // 512 else 128
                w, bw, wb, bwb = wst[gi % NB], bwst[gi % NB], wbf[gi % NB], bwbf[gi % NB]
                src = w_fm[l].rearrange("(kc p) n -> p kc n", p=128)[:, :, gi * 512:gi * 512 + ncol]
                S.dma("sp", lambda e, w=w, src=src, ncol=ncol: e.dma_start(out=w[:, :, 0:ncol], in_=src), [B_const], [bw])
                S.op("pool", lambda e, w=w, wb=wb, ncol=ncol: e.tensor_copy(out=wb[:, :, 0:ncol], in_=w[:, :, 0:ncol]),
                     [bw], [bwb])
                for mi in range(ncol // 128):
                    for (t0, tn) in TG:
                        i = cnt[0]
                        cnt[0] += 1
                        pb = nb()
                        tts = list(range(t0 // 128, (t0 + tn) // 128))
                        for kc in range(8):
                            S.op("pe", lambda e, pb=pb, kc=kc, wb=wb, mi=mi, t0=t0, tn=tn: e.matmul(
                                psum[pb][:, 0:tn], lhsT=wb[:, kc, mi * 128:(mi + 1) * 128], rhs=hT[:, kc, t0:t0 + tn],
                                start=(kc == 0), stop=(kc == 7)), [B_hT[t] for t in tts] + [bwb], [B_ps[pb]])
                        evt, bevt = ev[i % NE], bev[i % NE]
                        if i % 2 == 0:
                            S.op("dve", lambda e, pb=pb, evt=evt, tn=tn: e.tensor_copy(out=evt[:, 0:tn], in_=psum[pb][:, 0:tn]),
                                 [B_ps[pb]], [bevt])
                        else:
                            S.op("act", lambda e, pb=pb, evt=evt, tn=tn: e.copy(out=evt[:, 0:tn], in_=psum[pb][:, 0:tn]),
                                 [B_ps[pb]], [bevt])
                        r0 = gi * 512 + mi * 128
                        S.dma("pool", lambda e, evt=evt, r0=r0, t0=t0, tn=tn: e.dma_start(
                            out=PT[r0:r0 + 128, t0:t0 + tn], in_=evt[:, 0:tn]), [bevt], [B_PT])

        def post_norm_tile(xsrc_rows, bxsrc, ypb, gate_t, gate_b, g_t, g_b, b_t, b_b, dst_rows, bdst, tiles):
            xt, bx, tt_, bt, scr = tiles["xt"], tiles["bx"], tiles["t"], tiles["bt"], tiles["scr"]
            S.dma("sp", lambda e: e.dma_start(out=xt[:], in_=xsrc_rows), [bxsrc], [bx])
            yv = PS[:, ypb * 512:(ypb + 2) * 512]
            S.op("dve", lambda e: e.tensor_tensor(out=tt_[:], in0=yv, in1=gate_t[:], op=ALU.mult),
                 [B_ps[ypb], B_ps[ypb + 1], gate_b], [bt])
            S.op("act", lambda e: e.activation(out=xt[:], in_=xt[:], func=AF.Copy, scale=ALPHA), [bx], [bx])
            S.op("dve", lambda e: e.tensor_tensor(out=tt_[:], in0=tt_[:], in1=xt[:], op=ALU.add), [bx, bt], [bt])
            ln_stats(tt_, bt, scr)
            S.op("dve", lambda e: e.tensor_scalar(out=tt_[:], in0=tt_[:], scalar1=scr["mv"][:, 0:1],
                                                  scalar2=scr["rstd"][:], op0=ALU.subtract, op1=ALU.mult),
                 [bt, scr["bmv"], scr["brs"]], [bt])
            S.op("pool", lambda e: e.tensor_tensor(out=tt_[:], in0=tt_[:], in1=g_t[:], op=ALU.mult), [bt, g_b], [bt])
            S.op("pool", lambda e: e.tensor_tensor(out=xt[:], in0=tt_[:], in1=b_t[:], op=ALU.add), [bt, b_b], [bx])
            S.dma("pool", lambda e: e.dma_start(out=dst_rows, in_=xt[:]), [bx], [bdst])

        def stage_merge(l, tiles, xsrc, bxsrc):
            stg = [sb("mstg%d" % i, [128, 8 * 512], F32) for i in range(2)]
            bstg = [Buf("mstg") for i in range(2)]
            wbr, bwbr = load_weight_bf16("wbr", w_br[l].rearrange("(kc p) n -> p kc n", p=128), 8, D, stg, bstg)
            wo, bwo = load_weight_bf16("wo", w_o[l].rearrange("(kc p) n -> p kc n", p=128), 8, D, stg, bstg)
            gate1 = load_mod_pair("g1", 2)
            lg, blg = load_bcast("ln1g", lnp[l, 0, :], D, B_const)
            lb, blb = load_bcast("ln1b", lnp[l, 1, :], D, B_const)
            NB = 2
            brt = [sb("brt%d" % i, [128, D], F32) for i in range(NB)]
            bbrt = [Buf("brt") for i in range(NB)]
            gt = [sb("gt%d" % i, [128, 4 * D], F32) for i in range(NB)]
            bgt = [Buf("gt") for i in range(NB)]
            brb = sb("brb", [128, D], BF16)
            bbrb = Buf("brb")
            brT = sb("brT", [128, 8, 128], BF16)
            bbrT = Buf("brT")
            mg = sb("mg", [128, D], F32)
            bmg = Buf("mg")
            tmp = [sb("mtmp%d" % i, [128, 512], F32) for i in range(2)]
            btmp = [Buf("mtmp") for i in range(2)]
            mb = sb("mb", [128, D], BF16)
            bmb = Buf("mb")
            mT = sb("mT", [128, 8, 128], BF16)
            bmT = Buf("mT")
            pn = dict(xt=sb("pnx", [128, D], F32), bx=Buf("pnx"), t=sb("pnt", [128, D], F32), bt=Buf("pnt"),
                      scr=make_ln_scr("pn"))
            for n, tt in enumerate(tiles):
                rows = slice(tt * 128, (tt + 1) * 128)
                which = 1 if tt < 2 else 0
                b_, bb_, g_, bg_ = brt[n % NB], bbrt[n % NB], gt[n % NB], bgt[n % NB]
                S.dma("sp", lambda e, b_=b_, rows=rows: e.dma_start(out=b_[:], in_=BR[rows, :]), [B_BR], [bb_])
                S.dma("sp", lambda e, g_=g_, rows=rows: e.dma_start(out=g_[:], in_=PTOK[rows, T_GATE:T_GATE + 4 * D]),
                      [B_PTOK], [bg_])
                S.op("pool", lambda e, b_=b_: e.tensor_copy(out=brb[:], in_=b_[:]), [bb_], [bbrb])
                transpose8(brb, bbrb, brT[:], bbrT)
                k = 0
                for nbr in range(4):
                    for cg in range(2):
                        pb = nb()
                        for kh in range(2):
                            S.op("pe", lambda e, pb=pb, nbr=nbr, kh=kh, cg=cg: e.matmul(
                                psum[pb], lhsT=brT[:, 2 * nbr + kh, :], rhs=wbr[:, 2 * nbr + kh, cg * 512:(cg + 1) * 512],
                                start=(kh == 0), stop=(kh == 1)), [bbrT, bwbr], [B_ps[pb]])
                        gsl = g_[:, nbr * D + cg * 512: nbr * D + (cg + 1) * 512]
                        if nbr == 0:
                            S.op("dve", lambda e, pb=pb, cg=cg, gsl=gsl: e.tensor_tensor(
                                out=mg[:, cg * 512:(cg + 1) * 512], in0=psum[pb], in1=gsl, op=ALU.mult),
                                [B_ps[pb], bg_], [bmg])
                        else:
                            tm_, btm_ = tmp[k % 2], btmp[k % 2]
                            k += 1
                            S.op("dve", lambda e, pb=pb, gsl=gsl, tm_=tm_: e.tensor_tensor(
                                out=tm_[:], in0=psum[pb], in1=gsl, op=ALU.mult), [B_ps[pb], bg_], [btm_])
                            S.op("pool", lambda e, cg=cg, tm_=tm_: e.tensor_tensor(
                                out=mg[:, cg * 512:(cg + 1) * 512], in0=mg[:, cg * 512:(cg + 1) * 512], in1=tm_[:],
                                op=ALU.add), [btm_, bmg], [bmg])
                S.op("act", lambda e: e.copy(out=mb[:], in_=mg[:]), [bmg], [bmb])
                transpose8(mb, bmb, mT[:], bmT, eng="dve")
                ypb = nb2()
                for cg in range(2):
                    for kc in range(8):
                        S.op("pe", lambda e, ypb=ypb, cg=cg, kc=kc: e.matmul(
                            psum[ypb + cg], lhsT=mT[:, kc, :], rhs=wo[:, kc, cg * 512:(cg + 1) * 512],
                            start=(kc == 0), stop=(kc == 7)), [bmT, bwo], [B_ps[ypb + cg]])
                post_norm_tile(xsrc[rows, :], bxsrc, ypb, gate1[which][0], gate1[which][1], lg, blg, lb, blb,
                               X1[rows, :], B_X1, pn)

        def stage_ffn_up(l, hT, B_hT, t_start):
            NB = 2
            wst = [sb("fwst%d" % i, [128, 8, 512], F32) for i in range(NB)]
            bwst = [Buf("fwst") for i in range(NB)]
            wbf = [sb("fwbf%d" % i, [128, 8, 512], BF16) for i in range(NB)]
            bwbf = [Buf("fwbf") for i in range(NB)]
            sg = [sb("fsg%d" % i, [128, 512], F32) for i in range(2)]
            bsg = [Buf("fsg") for i in range(2)]
            av = [sb("fav%d" % i, [128, 512], BF16) for i in range(3)]
            bav = [Buf("fav") for i in range(3)]
            TG = []
            t0 = t_start
            while t0 < NT:
                tn = min(512, NT - t0)
                TG.append((t0, tn))
                t0 += tn
            cnt = 0
            for gi in range(2 * FH // 512):
                w, bw, wb, bwb = wst[gi % NB], bwst[gi % NB], wbf[gi % NB], bwbf[gi % NB]
                src = w_up[l].rearrange("(kc p) n -> p kc n", p=128)[:, :, gi * 512:(gi + 1) * 512]
                S.dma("sp", lambda e, w=w, src=src: e.dma_start(out=w[:], in_=src), [B_const], [bw])
                S.op("pool", lambda e, w=w, wb=wb: e.tensor_copy(out=wb[:], in_=w[:]), [bw], [bwb])
                for mm in range(2):
                    m = gi * 2 + mm
                    for (t0, tn) in TG:
                        tts = list(range(t0 // 128, (t0 + tn) // 128))
                        pg, pu = nb(), nb()
                        for which, pb in ((0, pg), (1, pu)):
                            c0 = mm * 256 + which * 128
                            for kc in range(8):
                                S.op("pe", lambda e, pb=pb, kc=kc, wb=wb, c0=c0, t0=t0, tn=tn: e.matmul(
                                    psum[pb][:, 0:tn], lhsT=wb[:, kc, c0:c0 + 128], rhs=hT[:, kc, t0:t0 + tn],
                                    start=(kc == 0), stop=(kc == 7)), [B_hT[t] for t in tts] + [bwb], [B_ps[pb]])
                        s_, bs_ = sg[cnt % 2], bsg[cnt % 2]
                        a_, ba_ = av[cnt % 3], bav[cnt % 3]
                        cnt += 1
                        S.op("act", lambda e, pg=pg, s_=s_, tn=tn: e.activation(out=s_[:, 0:tn], in_=psum[pg][:, 0:tn],
                                                                               func=AF.Silu), [B_ps[pg]], [bs_])
                        S.op("dve", lambda e, pu=pu, s_=s_, a_=a_, tn=tn: e.tensor_tensor(
                            out=a_[:, 0:tn], in0=psum[pu][:, 0:tn], in1=s_[:, 0:tn], op=ALU.mult), [B_ps[pu], bs_], [ba_])
                        S.dma("pool", lambda e, a_=a_, m=m, t0=t0, tn=tn: e.dma_start(
                            out=ACTT[m * 128:(m + 1) * 128, t0:t0 + tn], in_=a_[:, 0:tn]), [ba_], [B_ACTT])

        def stage_ffn_down(l, tiles, last):
            KC = FH // 128
            stg = [sb("dstg%d" % i, [128, KC * 256], F32) for i in range(2)]
            bstg = [Buf("dstg") for i in range(2)]
            wd, bwd = load_weight_bf16("wd", w_dn[l].rearrange("(kc p) n -> p kc n", p=128), KC, D, stg, bstg, chunk=256)
            gate2 = load_mod_pair("g2", 5)
            lg, blg = load_bcast("ln2g", lnp[l, 2, :], D, B_const)
            lb, blb = load_bcast("ln2b", lnp[l, 3, :], D, B_const)
            at = [sb("dat%d" % i, [128, KC, 256], BF16) for i in range(2)]
            bat = [Buf("dat") for i in range(2)]
            pn = dict(xt=sb("pnx", [128, D], F32), bx=Buf("pnx"), t=sb("pnt", [128, D], F32), bt=Buf("pnt"),
                      scr=make_ln_scr("pn"))
            for n in range(0, len(tiles), 2):
                tt0 = tiles[n]
                a_, ba_ = at[(n // 2) % 2], bat[(n // 2) % 2]
                for (k0, k1) in ((0, 6), (6, 12), (12, 17), (17, 22)):
                    S.dma("sp", lambda e, a_=a_, tt0=tt0, k0=k0, k1=k1: e.dma_start(
                        out=a_[:, k0:k1, :], in_=ACTT.rearrange("(kc p) t -> p kc t", p=128)[:, k0:k1, tt0 * 128: tt0 * 128 + 256]),
                        [B_ACTT], [ba_])
                for sub in range(2):
                    tt = tt0 + sub
                    rows = slice(tt * 128, (tt + 1) * 128)
                    which = 1 if tt < 2 else 0
                    ypb = nb2()
                    for cg in range(2):
                        for kc in range(KC):
                            S.op("pe", lambda e, ypb=ypb, cg=cg, kc=kc, a_=a_, sub=sub: e.matmul(
                                psum[ypb + cg], lhsT=a_[:, kc, sub * 128:(sub + 1) * 128],
                                rhs=wd[:, kc, cg * 512:(cg + 1) * 512], start=(kc == 0), stop=(kc == KC - 1)),
                                [ba_, bwd], [B_ps[ypb + cg]])
                    if last:
                        dst, bdst = out[(tt - 2) * 128:(tt - 1) * 128, :], B_out
                    else:
                        dst, bdst = XRES[rows, :], B_XRES
                    post_norm_tile(X1[rows, :], B_X1, ypb, gate2[which][0], gate2[which][1], lg, blg, lb, blb,
                                   dst, bdst, pn)
        def stage_gqa(l, ctx_out):
            qT2 = sb("qT2", [128, 2, 2, NT], BF16)
            kT = sb("kT", [128, NT], BF16)
            vall = sb("vall", [128, NTILE, 2, 65], BF16)
            bq, bk, bv = Buf("qT2"), Buf("kT"), Buf("vall")
            S.op("pool", lambda e: e.memset(vall[:], 1.0), [], [bv])
            S.op("pool", lambda e: e.memset(qT2[:], 0.0), [], [bq])
            nw, bnw = load_bcast("qkw", qkw[l, :], 384, B_const)
            NB = 2
            xts = [sb("gx%d" % i, [128, 512], F32) for i in range(NB)]
            bxs = [Buf("gx") for i in range(NB)]
            css = [sb("gcs%d" % i, [128, 128], F32) for i in range(NB)]
            bcs = [Buf("gcs") for i in range(NB)]
            sq = sb("gsq", [128, 384], F32)
            bsq = Buf("gsq")
            ss = sb("gss", [128, 6], F32)
            bss = Buf("gss")
            xn = sb("gxn", [128, 384], F32)
            bxn = Buf("gxn")
            rot = sb("grot", [128, 384], F32)
            brot = Buf("grot")
            t1 = sb("gt1", [128, 384], F32)
            bt1 = Buf("gt1")
            qkb = sb("gqkb", [128, 384], BF16)
            bqkb = Buf("gqkb")
            for tt in range(NTILE):
                rows = slice(tt * 128, (tt + 1) * 128)
                xt, bx, cs, bc = xts[tt % NB], bxs[tt % NB], css[tt % NB], bcs[tt % NB]
                S.dma("sp", lambda e, xt=xt, rows=rows: e.dma_start(out=xt[:], in_=PTOK[rows, T_DQ:T_DQ + 512]), [B_PTOK], [bx])
                S.dma("sp", lambda e, cs=cs, rows=rows: e.dma_start(out=cs[:], in_=rope_cs[rows, :]), [B_const], [bc])
                S.op("dve", lambda e, xt=xt: e.tensor_tensor(out=sq[:], in0=xt[:, 0:384], in1=xt[:, 0:384], op=ALU.mult), [bx], [bsq])
                S.op("dve", lambda e: e.tensor_reduce(out=ss[:], in_=sq[:].rearrange("p (h d) -> p h d", d=64), axis=AX.X, op=ALU.add),
                     [bsq], [bss])
                S.op("act", lambda e: e.activation(out=ss[:], in_=ss[:], func=AF.Sqrt, bias=eps_t[:], scale=1.0 / 64), [bss, B_eps], [bss])
                S.op("dve", lambda e: e.reciprocal(out=ss[:], in_=ss[:]), [bss], [bss])
                S.op("dve", lambda e, xt=xt: e.tensor_tensor(
                    out=xn[:].rearrange("p (h d) -> p h d", d=64), in0=xt[:, 0:384].rearrange("p (h d) -> p h d", d=64),
                    in1=ss[:].unsqueeze(2).to_broadcast([128, 6, 64]), op=ALU.mult), [bx, bss], [bxn])
                S.op("pool", lambda e: e.tensor_tensor(out=xn[:], in0=xn[:], in1=nw[:], op=ALU.mult), [bxn, bnw], [bxn])
                xv = xn[:].rearrange("p (g two x) -> p g two x", two=2, x=16)
                rv = rot[:].rearrange("p (g two x) -> p g two x", two=2, x=16)
                S.op("pool", lambda e, xv=xv, rv=rv: e.tensor_copy(out=rv[:, :, 0, :], in_=xv[:, :, 1, :]), [bxn], [brot])
                S.op("pool", lambda e, xv=xv, rv=rv: e.tensor_copy(out=rv[:, :, 1, :], in_=xv[:, :, 0, :]), [bxn], [brot])
                cb = cs[:, 0:64].unsqueeze(1).to_broadcast([128, 6, 64])
                sbb = cs[:, 64:128].unsqueeze(1).to_broadcast([128, 6, 64])
                S.op("dve", lambda e, cb=cb: e.tensor_tensor(out=t1[:].rearrange("p (h d) -> p h d", d=64),
                                                             in0=xn[:].rearrange("p (h d) -> p h d", d=64), in1=cb, op=ALU.mult),
                     [bxn, bc], [bt1])
                S.op("pool", lambda e, sbb=sbb: e.tensor_tensor(out=rot[:].rearrange("p (h d) -> p h d", d=64),
                                                                in0=rot[:].rearrange("p (h d) -> p h d", d=64), in1=sbb, op=ALU.mult),
                     [brot, bc], [brot])
                S.op("dve", lambda e: e.tensor_tensor(
                    out=qkb[:, 0:256].rearrange("p (j k d) -> p j k d", j=2, k=2),
                    in0=t1[:, 0:256].rearrange("p (k j d) -> p j k d", k=2, j=2),
                    in1=rot[:, 0:256].rearrange("p (k j d) -> p j k d", k=2, j=2), op=ALU.add), [bt1, brot], [bqkb])
                S.op("dve", lambda e: e.tensor_tensor(out=qkb[:, 256:384], in0=t1[:, 256:384], in1=rot[:, 256:384], op=ALU.add),
                     [bt1, brot], [bqkb])
                S.op("act", lambda e, xt=xt, tt=tt: e.copy(out=vall[:, tt, :, 0:64],
                                                           in_=xt[:, 384:512].rearrange("p (k d) -> p k d", k=2)), [bx], [bv])
                pb = nb(4, 8)
                pt = psum[pb].bitcast(BF16)
                for c3 in range(3):
                    S.op("pe", lambda e, c3=c3, pt=pt: e.transpose(out=pt[:, c3 * 128:(c3 + 1) * 128],
                                                                   in_=qkb[:, c3 * 128:(c3 + 1) * 128], identity=ident_b[:]),
                         [bqkb, B_identb], [B_ps[pb]])
                for kh_ in range(2):
                    pr = slice(kh_ * 64, (kh_ + 1) * 64)
                    S.op("act", lambda e, pt=pt, tt=tt, kh_=kh_, pr=pr: e.copy(
                        out=qT2[pr, kh_, :, tt * 128:(tt + 1) * 128], in_=pt[pr, 0:256].rearrange("p (j t) -> p j t", j=2)),
                        [B_ps[pb]], [bq])
                S.op("act", lambda e, pt=pt, tt=tt: e.copy(out=kT[:, tt * 128:(tt + 1) * 128], in_=pt[:, 256:384]), [B_ps[pb]], [bk])
            eb = [sb("geb%d" % i, [128, 512], BF16) for i in range(3)]
            beb = [Buf("geb") for i in range(3)]
            osb = [sb("gosb%d" % i, [128, 2, 256], F32) for i in range(2)]
            bosb = [Buf("gosb") for i in range(2)]
            rc = sb("grc", [128, 1], F32)
            brc = Buf("grc")
            groups = ([(0, [0, 1])] if ctx_out else []) + [(2 + 2 * i, list(range(NTILE))) for i in range(16)]
            cnt = 0
            for gi, (tt0, keys) in enumerate(groups):
                t0 = tt0 * 128
                o_, bo_ = osb[gi % 2], bosb[gi % 2]
                for kh in range(2):
                    for ki, kt in enumerate(keys):
                        spb = nb(4, 8)
                        S.op("pe", lambda e, spb=spb, kh=kh, kt=kt, t0=t0: e.matmul(
                            psum[spb].rearrange("p (j t) -> p j t", j=2), lhsT=kT[:, kt * 128:(kt + 1) * 128],
                            rhs=qT2[:, kh, :, t0:t0 + 256], start=True, stop=True), [bq, bk], [B_ps[spb]])
                        e_, be_ = eb[cnt % 3], beb[cnt % 3]
                        cnt += 1
                        S.op("act", lambda e, spb=spb, e_=e_: e.activation(out=e_[:], in_=psum[spb], func=AF.Exp, scale=0.125),
                             [B_ps[spb]], [be_])
                        for j in range(2):
                            for sub in range(2):
                                ab = j * 2 + sub
                                S.op("pe", lambda e, ab=ab, e_=e_, j=j, sub=sub, kt=kt, kh=kh, ki=ki, nk=len(keys): e.matmul(
                                    psum[ab][:, 0:65], lhsT=e_[:, j * 256 + sub * 128: j * 256 + (sub + 1) * 128],
                                    rhs=vall[:, kt, kh, :], start=(ki == 0), stop=(ki == nk - 1)), [be_, bv], [B_ps[ab]])
                    for j in range(2):
                        for sub in range(2):
                            ab = j * 2 + sub
                            hh = 2 * kh + j
                            S.op("dve", lambda e, ab=ab: e.reciprocal(out=rc[:], in_=psum[ab][:, 64:65]), [B_ps[ab]], [brc])
                            S.op("dve", lambda e, ab=ab, o_=o_, sub=sub, hh=hh: e.tensor_scalar(
                                out=o_[:, sub, hh * 64:(hh + 1) * 64], in0=psum[ab][:, 0:64], scalar1=rc[:], scalar2=None,
                                op0=ALU.mult), [B_ps[ab], brc], [bo_])
                S.dma("pool", lambda e, o_=o_, t0=t0: e.dma_start(
                    out=BR[t0:t0 + 256, 768:1024].rearrange("(s p) c -> p s c", p=128), in_=o_[:]), [bo_], [B_BR])

        def stage_na(l, ctx_out):
            cqT = sb("cqT", [128, 4, NT], BF16)
            ckT = sb("ckT", [128, 2, NT], BF16)
            vna = sb("vna", [128, NTILE, 4, 65], BF16)
            bcq, bck, bvn = Buf("cqT"), Buf("ckT"), Buf("vna")
            S.op("pool", lambda e: e.memset(vna[:], 1.0), [], [bvn])
            S.op("pool", lambda e: e.memset(cqT[:], 0.0), [], [bcq])
            tab = sb("natab", [128, NA_NBLK, 4, 64], F32)
            btab = Buf("natab")
            S.dma("sp", lambda e: e.dma_start(out=tab[:].rearrange("p a b c -> p (a b c)"), in_=natab_in[l]), [B_const], [btab])
            stg = [sb("nstg%d" % i, [128, NT], F32) for i in range(2)]
            bstg = [Buf("nstg") for i in range(2)]
            k = 0
            for (dst, bdst, r0) in ((cqT, bcq, R_CQ), (ckT, bck, R_CK)):
                for hc in range(2):
                    st, bs = stg[k % 2], bstg[k % 2]
                    k += 1
                    S.dma("sp", lambda e, st=st, r0=r0, hc=hc: e.dma_start(out=st[:], in_=PT[r0 + hc * 128: r0 + (hc + 1) * 128, :]),
                          [B_PT], [bs])
                    if r0 == R_CQ:
                        for hh in range(2):
                            S.op("pool", lambda e, st=st, dst=dst, hc=hc, hh=hh: e.tensor_scalar_mul(
                                out=dst[hh * 64:(hh + 1) * 64, 2 * hc + hh, :], in0=st[hh * 64:(hh + 1) * 64, :], scalar1=0.125), [bs], [bdst])
                    else:
                        S.op("pool", lambda e, st=st, dst=dst, hc=hc: e.tensor_copy(out=dst[:, hc, :], in_=st[:]), [bs], [bdst])
            vst = [sb("nvst%d" % i, [128, 256], F32) for i in range(2)]
            bvst = [Buf("nvst") for i in range(2)]
            for tt in range(NTILE):
                v_, bv_ = vst[tt % 2], bvst[tt % 2]
                S.dma("sp", lambda e, v_=v_, tt=tt: e.dma_start(out=v_[:], in_=PTOK[tt * 128:(tt + 1) * 128, T_CV:T_CV + 256]),
                      [B_PTOK], [bv_])
                S.op("act", lambda e, v_=v_, tt=tt: e.copy(out=vna[:, tt, :, 0:64], in_=v_[:].rearrange("p (h d) -> p h d", h=4)),
                     [bv_], [bvn])
            sfp = [sb("nsfp%d" % i, [128, 512], F32) for i in range(2)]
            bsfp = [Buf("nsfp") for i in range(2)]
            eb = [sb("neb%d" % i, [128, 512], BF16) for i in range(3)]
            beb = [Buf("neb") for i in range(3)]
            osb = [sb("nosb%d" % i, [128, 256], F32) for i in range(2)]
            bosb = [Buf("nosb") for i in range(2)]
            rc = sb("nrc", [128, 1], F32)
            brc = Buf("nrc")
            plan = ([(0, [(0, None), (1, None)]), (1, [(0, None), (1, None)])] if ctx_out else [])
            import os as _os
            _mode = _os.environ.get("NA_MODE", "")
            for i in range(32):
                if _mode == "nobias":
                    plan.append((2 + i, [(2 + j, None) for (j, b0, b1) in NA_PLAN[i]] + [(0, None), (1, None)]))
                else:
                    plan.append((2 + i, [(2 + j, (b0, b1)) for (j, b0, b1) in NA_PLAN[i]] + [(0, None), (1, None)]))
            if _mode == "prep":
                plan = []
            if _mode == "ctxonly":
                plan = plan[:2]
            cnt = 0
            for qi, (qt, keylist) in enumerate(plan):
                q0 = qt * 128
                o_, bo_ = osb[qi % 2], bosb[qi % 2]
                for ki, (kt, blk) in enumerate(keylist):
                    spb = nb(4, 8)
                    for h in range(4):
                        hc = h // 2
                        S.op("pe", lambda e, spb=spb, h=h, hc=hc, kt=kt, q0=q0: e.matmul(
                            psum[spb][:, h * 128:(h + 1) * 128], lhsT=ckT[:, hc, kt * 128:(kt + 1) * 128],
                            rhs=cqT[:, h, q0:q0 + 128], start=True, stop=True), [bcq, bck], [B_ps[spb]])
                    e_, be_ = eb[cnt % 3], beb[cnt % 3]
                    if blk is not None:
                        s_, bs_ = sfp[cnt % 2], bsfp[cnt % 2]
                        for a in range(2):
                            for h in range(4):
                                c0 = h * 128 + a * 64
                                S.op("dve", lambda e, spb=spb, s_=s_, a=a, h=h, c0=c0, blk=blk: e.tensor_tensor(
                                    out=s_[:, c0:c0 + 64], in0=psum[spb][:, c0:c0 + 64], in1=tab[:, blk[a], h, :], op=ALU.add),
                                    [B_ps[spb], btab], [bs_])
                        S.op("act", lambda e, s_=s_, e_=e_: e.activation(out=e_[:], in_=s_[:], func=AF.Exp), [bs_], [be_])
                    else:
                        S.op("act", lambda e, spb=spb, e_=e_: e.activation(out=e_[:], in_=psum[spb], func=AF.Exp, scale=1.0),
                             [B_ps[spb]], [be_])
                    cnt += 1
                    for h in range(4):
                        S.op("pe", lambda e, h=h, e_=e_, kt=kt, ki=ki, nk=len(keylist): e.matmul(
                            psum[h][:, 0:65], lhsT=e_[:, h * 128:(h + 1) * 128], rhs=vna[:, kt, h, :],
                            start=(ki == 0), stop=(ki == nk - 1)), [be_, bvn], [B_ps[h]])
                for h in range(4):
                    S.op("dve", lambda e, h=h: e.reciprocal(out=rc[:], in_=psum[h][:, 64:65]), [B_ps[h]], [brc])
                    S.op("dve", lambda e, h=h, o_=o_: e.tensor_scalar(
                        out=o_[:, h * 64:(h + 1) * 64], in0=psum[h][:, 0:64], scalar1=rc[:], scalar2=None, op0=ALU.mult),
                        [B_ps[h], brc], [bo_])
                S.dma("pool", lambda e, o_=o_, q0=q0: e.dma_start(out=BR[q0:q0 + 128, 512:768], in_=o_[:]), [bo_], [B_BR])
        def stage_hgrn(l, ctx_out):
            NCH = NT // 64

            def v3(ap):
                return ap.rearrange("p (c x) -> p c x", x=64)

            def v32(ap):
                return ap.rearrange("p (c x) -> p c x", x=32)

            def v5(ap):
                return ap.rearrange("p (c two x) -> p c two x", two=2, x=32)

            def v4(ap):
                return ap.rearrange("p (t two x) -> p t two x", two=2, x=64)

            cm = sb("hcm", [128, 4, 128], F32)
            bcm = Buf("hcm")
            S.dma("sp", lambda e: e.dma_start(out=cm[:], in_=hmask_in.rearrange("d p t -> p d t")), [B_const], [bcm])
            lbt = sb("hlb", [128, 8], F32)
            lb = sb("hlbv", [128, 4], F32)
            oml = sb("homl", [128, 4], F32)
            blb = Buf("hlb")
            if l > 0:
                S.dma("sp", lambda e: e.dma_start(out=lbt[:], in_=lbp_in), [B_const], [blb])
                lv = lbt[:].rearrange("p (d l h) -> p d l h", d=2, l=2)
                S.op("dve", lambda e: e.tensor_tensor(out=lb[:].rearrange("p (d h) -> p d h", d=2), in0=lv[:, :, 1, :],
                                                      in1=lv[:, :, 0, :], op=ALU.subtract), [blb], [blb])
                S.op("act", lambda e: e.activation(out=lb[:], in_=lb[:], func=AF.Sigmoid), [blb], [blb])
                S.op("dve", lambda e: e.tensor_scalar(out=oml[:], in0=lb[:], scalar1=-1.0, scalar2=1.0, op0=ALU.mult, op1=ALU.add),
                     [blb], [blb])
            for dr in range(2):
                OA = OA_dr[dr]
                for hc in range(2):
                    with Scope():
                        q1 = sb("hq1", [128, 2, NT], BF16)
                        q2 = sb("hq2", [128, 2, NT], BF16)
                        k1 = sb("hk1", [128, NT], BF16)
                        k2 = sb("hk2", [128, NT], BF16)
                        qhA = sb("hqA", [128, 2, NT], BF16)
                        qhB = sb("hqB", [128, 2, NT], BF16)
                        kh = sb("hkh", [128, NT], BF16)
                        bper = Buf("hper")
                        Sbf = sb("hSbf", [128, NCH, 2, 64], BF16)
                        bSbf = Buf("hSbf")
                        dd = sb("hdd", [128, NCH], F32)
                        bdd = Buf("hdd")
                        for t_ in (q1, q2, qhA, qhB):
                            S.op("pool", lambda e, t_=t_: e.memset(t_[:], 0.0), [], [bper])
                        S.op("pool", lambda e: e.memset(k2[:], 0.0), [], [bper])
                        with Scope():
                            A = sb("hA", [128, NT], F32)
                            C = sb("hC", [128, NT], F32)
                            E = sb("hE", [128, NT], F32)
                            Q = sb("hQ", [128, NT], F32)
                            m01 = sb("hm01", [128, NT], BF16)
                            bA, bC, bE, bQ, bm01 = Buf("hA"), Buf("hC"), Buf("hE"), Buf("hQ"), Buf("hm01")
                            S.op("pool", lambda e: e.memset(m01[:], 1.0), [], [bm01])
                            S.op("pool", lambda e: e.memset(v3(m01[:])[:, :, 0:1], 0.0), [], [bm01])
                            RF = R_AFF if dr == 0 else R_AFB
                            S.dma("sp", lambda e: e.dma_start(out=A[:], in_=PT[RF + hc * 128:RF + (hc + 1) * 128, :]), [B_PT], [bA])
                            S.op("act", lambda e: e.activation(out=A[:], in_=A[:], func=AF.Sigmoid), [bA], [bA])
                            if l > 0:
                                ci = dr * 2 + hc
                                S.op("dve", lambda e: e.tensor_scalar(out=A[:], in0=A[:], scalar1=oml[:, ci:ci + 1],
                                                                      scalar2=lb[:, ci:ci + 1], op0=ALU.mult, op1=ALU.add),
                                     [bA, blb], [bA])
                            S.op("act", lambda e: e.activation(out=E[:], in_=A[:], func=AF.Ln), [bA], [bE])
                            S.op("dve", lambda e: e.tensor_scalar(out=A[:], in0=A[:], scalar1=-1.0, scalar2=1.0,
                                                                  op0=ALU.mult, op1=ALU.add), [bA, bE], [bA])
                            S.op("dve", lambda e: e.tensor_tensor_scan(out=C[:], data0=m01[:], data1=E[:], initial=0.0,
                                                                       op0=ALU.mult, op1=ALU.add), [bm01, bE], [bC])
                            if dr == 1:
                                S.op("dve", lambda e: e.tensor_tensor(out=v3(Q[:]), in0=v3(C[:])[:, :, 63:64].to_broadcast([128, NCH, 64]),
                                                                      in1=v3(C[:]), op=ALU.subtract), [bC], [bQ])
                                S.op("dve", lambda e: e.tensor_tensor(out=C[:], in0=Q[:], in1=E[:], op=ALU.add), [bQ, bE], [bC])
                                ti, bi, kv, qv = 0, 32, 1, 0
                            else:
                                ti, bi, kv, qv = 63, 31, 0, 1
                            S.op("act", lambda e: e.activation(out=dd[:], in_=v3(C[:])[:, :, ti], func=AF.Exp), [bC], [bdd])
                            S.op("dve", lambda e: e.tensor_tensor(out=v3(E[:]), in0=v3(C[:])[:, :, ti:ti + 1].to_broadcast([128, NCH, 64]),
                                                                  in1=v3(C[:]), op=ALU.subtract), [bC, bE], [bE])
                            S.op("act", lambda e: e.activation(out=E[:], in_=E[:], func=AF.Exp), [bE], [bE])
                            S.op("dve", lambda e: e.tensor_tensor(out=kh[:], in0=A[:], in1=E[:], op=ALU.mult), [bA, bE], [bper])
                            S.op("dve", lambda e: e.tensor_tensor(out=v32(E[:]), in0=v32(C[:]),
                                                                  in1=v32(C[:])[:, :, 16:17].to_broadcast([128, 2 * NCH, 32]),
                                                                  op=ALU.subtract), [bC, bE, bper], [bE])
                            S.op("act", lambda e: e.activation(out=Q[:], in_=E[:], func=AF.Exp, scale=-1.0), [bE, bQ], [bQ])
                            S.op("dve", lambda e: e.tensor_tensor(out=k1[:], in0=A[:], in1=Q[:], op=ALU.mult), [bA, bQ], [bper])
                            S.op("dve", lambda e: e.tensor_tensor(out=v3(Q[:]), in0=v3(C[:])[:, :, bi:bi + 1].to_broadcast([128, NCH, 64]),
                                                                  in1=v3(C[:]), op=ALU.subtract), [bC, bQ, bper], [bQ])
                            S.op("act", lambda e: e.activation(out=Q[:], in_=Q[:], func=AF.Exp), [bQ], [bQ])
                            S.op("dve", lambda e: e.tensor_tensor(out=v5(k2[:])[:, :, kv, :], in0=v5(A[:])[:, :, kv, :],
                                                                  in1=v5(Q[:])[:, :, kv, :], op=ALU.mult), [bA, bQ], [bper])
                            S.dma("sp", lambda e: e.dma_start(out=Q[:], in_=PT[R_AQ + hc * 128:R_AQ + (hc + 1) * 128, :]), [B_PT, bQ, bper], [bQ])
                            S.op("act", lambda e: e.activation(out=A[:], in_=E[:], func=AF.Exp), [bE, bA, bper], [bA])
                            for hh in range(2):
                                pr = slice(hh * 64, (hh + 1) * 64)
                                S.op("dve", lambda e, pr=pr, hh=hh: e.tensor_tensor(out=q1[pr, hh, :], in0=Q[pr, :], in1=A[pr, :], op=ALU.mult),
                                     [bQ, bA], [bper])
                            S.op("dve", lambda e: e.tensor_tensor(out=v3(E[:]), in0=v3(C[:]),
                                                                  in1=v3(C[:])[:, :, bi:bi + 1].to_broadcast([128, NCH, 64]),
                                                                  op=ALU.subtract), [bC, bE, bper, bA], [bE])
                            S.op("act", lambda e: e.activation(out=A[:], in_=E[:], func=AF.Exp), [bE, bA, bper], [bA])
                            for hh in range(2):
                                pr = slice(hh * 64, (hh + 1) * 64)
                                S.op("dve", lambda e, pr=pr, hh=hh: e.tensor_tensor(
                                    out=v5(q2[pr, hh, :])[:, :, qv, :], in0=v5(Q[pr, :])[:, :, qv, :], in1=v5(A[pr, :])[:, :, qv, :],
                                    op=ALU.mult), [bQ, bA], [bper])
                            S.op("act", lambda e: e.activation(out=E[:], in_=C[:], func=AF.Exp), [bC, bE, bper], [bE])
                            for hh in range(2):
                                pr = slice(hh * 64, (hh + 1) * 64)
                                S.op("dve", lambda e, pr=pr, hh=hh: e.tensor_tensor(
                                    out=v4(qhA[pr, hh, :])[:, :, 0, :], in0=v4(Q[pr, :])[:, :, 0, :], in1=v4(E[pr, :])[:, :, 0, :],
                                    op=ALU.mult), [bQ, bE], [bper])
                                S.op("dve", lambda e, pr=pr, hh=hh: e.tensor_tensor(
                                    out=v4(qhB[pr, hh, :])[:, :, 1, :], in0=v4(Q[pr, :])[:, :, 1, :], in1=v4(E[pr, :])[:, :, 1, :],
                                    op=ALU.mult), [bQ, bE], [bper])
                        with Scope():
                            vall = sb("hv", [128, NTILE, 128], BF16)
                            vz = sb("hvz", [128, NTILE, 2, 128], BF16)
                            bv = Buf("hv")
                            S.op("pool", lambda e: e.memset(vz[:], 0.0), [], [bv])
                            vst = [sb("hvst%d" % i, [128, 128], F32) for i in range(2)]
                            bvst = [Buf("hvst") for i in range(2)]
                            for tt in range(NTILE):
                                v_, bv_ = vst[tt % 2], bvst[tt % 2]
                                S.dma("sp", lambda e, v_=v_, tt=tt: e.dma_start(
                                    out=v_[:], in_=PTOK[tt * 128:(tt + 1) * 128, T_AV + hc * 128:T_AV + (hc + 1) * 128]), [B_PTOK], [bv_])
                                S.op("pool", lambda e, v_=v_, tt=tt: e.tensor_copy(out=vall[:, tt, :], in_=v_[:]), [bv_], [bv])
                                for half in range(2):
                                    pr = slice(half * 64, (half + 1) * 64)
                                    S.op("pool", lambda e, v_=v_, tt=tt, half=half, pr=pr: e.tensor_copy(out=vz[pr, tt, half, :], in_=v_[pr, :]),
                                         [bv_], [bv])
                            Sf = sb("hSf", [128, 128], F32)
                            bSf = Buf("hSf")
                            S.op("pool", lambda e: e.memset(Sf[:], 0.0), [], [bSf])
                            khtok = [sb("hktok%d" % i, [128, 128], BF16) for i in range(2)]
                            bkhtok = [Buf("hktok") for i in range(2)]
                            order = list(range(NTILE)) if dr == 0 else [1, 0] + list(range(NTILE - 1, 1, -1))
                            for n, tt in enumerate(order):
                                pb = nb(4, 8)
                                pt = psum[pb].bitcast(BF16)
                                kk_, bkk_ = khtok[n % 2], bkhtok[n % 2]
                                S.op("pe", lambda e, pt=pt, tt=tt: e.transpose(out=pt[:, 0:128], in_=kh[:, tt * 128:(tt + 1) * 128],
                                                                               identity=ident_b[:]), [bper, B_identb], [B_ps[pb]])
                                S.op("act", lambda e, pt=pt, kk_=kk_: e.copy(out=kk_[:], in_=pt[:, 0:128]), [B_ps[pb]], [bkk_])
                                for half in ((0, 1) if dr == 0 else (1, 0)):
                                    c = tt * 2 + half
                                    S.op("pool", lambda e, c=c: e.tensor_copy(out=Sbf[:, c, :, :].rearrange("p a b -> p (a b)"), in_=Sf[:]),
                                         [bSf], [bSbf])
                                    pu = nb(4, 8)
                                    for hh in range(2):
                                        S.op("pe", lambda e, pu=pu, hh=hh, half=half, kk_=kk_, tt=tt: e.matmul(
                                            psum[pu][:, hh * 64:(hh + 1) * 64], lhsT=kk_[:],
                                            rhs=vz[:, tt, half, hh * 64:(hh + 1) * 64], start=True, stop=True),
                                            [bkk_, bv], [B_ps[pu]])
                                    S.op("dve", lambda e, c=c: e.tensor_scalar_mul(out=Sf[:], in0=Sf[:], scalar1=dd[:, c:c + 1]),
                                         [bSf, bdd, bSbf], [bSf])
                                    S.op("dve", lambda e, pu=pu: e.tensor_tensor(out=Sf[:], in0=Sf[:], in1=psum[pu][:, 0:128], op=ALU.add),
                                         [bSf, B_ps[pu]], [bSf])
                            am = [sb("ham%d" % i, [128, 128], BF16) for i in range(3)]
                            bam = [Buf("ham") for i in range(3)]
                            t1s = [sb("ht1%d" % i, [128, 128], F32) for i in range(2)]
                            bt1s = [Buf("ht1") for i in range(2)]
                            t2s = [sb("ht2%d" % i, [128, 128], F32) for i in range(2)]
                            bt2s = [Buf("ht2") for i in range(2)]
                            ots = [sb("hot%d" % i, [128, 128], F32) for i in range(2)]
                            bots = [Buf("hot") for i in range(2)]
                            k = 0
                            for n, tt in enumerate(range(NTILE) if ctx_out else range(2, NTILE)):
                                tl = slice(tt * 128, (tt + 1) * 128)
                                ot, bot = ots[n % 2], bots[n % 2]
                                for hh in range(2):
                                    po = nb(0, 4)
                                    pa1 = nb(4, 8)
                                    pa2 = nb(4, 8)
                                    S.op("pe", lambda e, pa1=pa1, hh=hh, tl=tl: e.matmul(
                                        psum[pa1][:, 0:128], lhsT=k1[:, tl], rhs=q1[:, hh, tl], start=True, stop=True), [bper], [B_ps[pa1]])
                                    S.op("pe", lambda e, pa2=pa2, hh=hh, tl=tl: e.matmul(
                                        psum[pa2][:, 0:128], lhsT=k2[:, tl], rhs=q2[:, hh, tl], start=True, stop=True), [bper], [B_ps[pa2]])
                                    a_, ba_ = am[k % 3], bam[k % 3]
                                    t1, bt1, t2, bt2 = t1s[k % 2], bt1s[k % 2], t2s[k % 2], bt2s[k % 2]
                                    k += 1
                                    S.op("dve", lambda e, pa1=pa1, t1=t1: e.tensor_tensor(out=t1[:], in0=psum[pa1][:, 0:128], in1=cm[:, dr, :],
                                                                                         op=ALU.mult), [B_ps[pa1], bcm], [bt1])
                                    S.op("dve", lambda e, pa2=pa2, t2=t2: e.tensor_tensor(out=t2[:], in0=psum[pa2][:, 0:128], in1=cm[:, 2 + dr, :],
                                                                                         op=ALU.mult), [B_ps[pa2], bcm], [bt2])
                                    S.op("pool", lambda e, a_=a_, t1=t1, t2=t2: e.tensor_tensor(out=a_[:], in0=t1[:], in1=t2[:], op=ALU.add),
                                         [bt1, bt2], [ba_])
                                    oc = psum[po][:, 0:64]
                                    S.op("pe", lambda e, oc=oc, a_=a_, tt=tt, hh=hh: e.matmul(
                                        oc, lhsT=a_[:], rhs=vall[:, tt, hh * 64:(hh + 1) * 64], start=True, stop=False), [ba_, bv], [B_ps[po]])
                                    S.op("pe", lambda e, oc=oc, tl=tl, tt=tt, hh=hh: e.matmul(
                                        oc, lhsT=qhA[:, hh, tl], rhs=Sbf[:, 2 * tt, hh, :], start=False, stop=False),
                                        [bper, bSbf], [B_ps[po]])
                                    S.op("pe", lambda e, oc=oc, tl=tl, tt=tt, hh=hh: e.matmul(
                                        oc, lhsT=qhB[:, hh, tl], rhs=Sbf[:, 2 * tt + 1, hh, :], start=False, stop=True),
                                        [bper, bSbf], [B_ps[po]])
                                    S.op("act", lambda e, po=po, ot=ot, hh=hh: e.copy(out=ot[:, hh * 64:(hh + 1) * 64], in_=psum[po][:, 0:64]),
                                         [B_ps[po]], [bot])
                                S.dma("pool", lambda e, ot=ot, tl=tl: e.dma_start(out=OA[tl, hc * 128:(hc + 1) * 128], in_=ot[:]), [bot], [B_OA])
            nw, bnw = load_bcast("hnw", hnorm_in[l, :], 256, B_const)
            gts = [sb("hg%d" % i, [128, 256], F32) for i in range(2)]
            bgts = [Buf("hg") for i in range(2)]
            oas = [sb("hoa%d" % i, [128, 2, 256], F32) for i in range(2)]
            boas = [Buf("hoa") for i in range(2)]
            sq = sb("hsq", [128, 256], F32)
            bsq = Buf("hsq")
            ss = sb("hss", [128, 4], F32)
            bss = Buf("hss")
            ob = [sb("hob%d" % i, [128, 256], F32) for i in range(2)]
            bob = [Buf("hob") for i in range(2)]
            for n, tt in enumerate(range(NTILE) if ctx_out else range(2, NTILE)):
                rows = slice(tt * 128, (tt + 1) * 128)
                g_, bg_ = gts[n % 2], bgts[n % 2]
                o_, bo_ = ob[n % 2], bob[n % 2]
                oa2, boa2 = oas[n % 2], boas[n % 2]
                for dr in range(2):
                    S.dma("sp", lambda e, oa2=oa2, dr=dr, rows=rows: e.dma_start(out=oa2[:, dr, :], in_=OA_dr[dr][rows, :]), [B_OA], [boa2])
                oa = oa2[:, 0, :]
                S.op("pool", lambda e, oa2=oa2: e.tensor_tensor(out=oa2[:, 0, :], in0=oa2[:, 0, :], in1=oa2[:, 1, :], op=ALU.add), [boa2], [boa2])
                S.dma("sp", lambda e, g_=g_, rows=rows: e.dma_start(out=g_[:], in_=PTOK[rows, T_AG:T_AG + 256]), [B_PTOK], [bg_])
                S.op("act", lambda e, g_=g_: e.activation(out=g_[:], in_=g_[:], func=AF.Silu), [bg_], [bg_])
                S.op("dve", lambda e, oa=oa: e.tensor_tensor(out=sq[:], in0=oa, in1=oa, op=ALU.mult), [boa2], [bsq])
                S.op("dve", lambda e: e.tensor_reduce(out=ss[:], in_=sq[:].rearrange("p (h d) -> p h d", d=64), axis=AX.X, op=ALU.add),
                     [bsq], [bss])
                S.op("act", lambda e: e.activation(out=ss[:], in_=ss[:], func=AF.Sqrt, bias=eps_t[:], scale=1.0 / 64), [bss, B_eps], [bss])
                S.op("dve", lambda e: e.reciprocal(out=ss[:], in_=ss[:]), [bss], [bss])
                S.op("dve", lambda e, oa=oa, o_=o_: e.tensor_tensor(
                    out=o_[:].rearrange("p (h d) -> p h d", d=64), in0=oa.rearrange("p (h d) -> p h d", d=64),
                    in1=ss[:].unsqueeze(2).to_broadcast([128, 4, 64]), op=ALU.mult), [boa2, bss], [bo_])
                S.op("pool", lambda e, o_=o_: e.tensor_tensor(out=o_[:], in0=o_[:], in1=nw[:], op=ALU.mult), [bo_, bnw], [bo_])
                S.op("pool", lambda e, o_=o_, g_=g_: e.tensor_tensor(out=o_[:], in0=o_[:], in1=g_[:], op=ALU.mult), [bo_, bg_], [bo_])
                S.dma("pool", lambda e, o_=o_, rows=rows: e.dma_start(out=BR[rows, 0:256], in_=o_[:]), [bo_], [B_BR])
        def stage_ssd(l, ctx_out):
            NCH = NT // 64

            def v3(ap):
                return ap.rearrange("p (c x) -> p c x", x=64)

            with Scope():
                cp = sb("cp", [128, 6, 6], F32)
                bcp = Buf("cp")
                S.dma("sp", lambda e: e.dma_start(out=cp[:], in_=convp_in[l]), [B_const], [bcp])
                Xs = [sb("cx%d" % i, [128, NT], F32) for i in range(2)]
                bXs = [Buf("cx") for i in range(2)]
                Ys = [sb("cy%d" % i, [128, NT], F32) for i in range(2)]
                bYs = [Buf("cy") for i in range(2)]
                Yb = [sb("cyb%d" % i, [128, NT], BF16) for i in range(2)]
                bYb = [Buf("cyb") for i in range(2)]
                st32 = [sb("cst%d" % i, [128, 128], F32) for i in range(3)]
                bst32 = [Buf("cst") for i in range(3)]
                st16 = [sb("cstb%d" % i, [128, 128], BF16) for i in range(3)]
                bst16 = [Buf("cstb") for i in range(3)]
                ctmp = sb("ctmp", [128, NT], F32)
                bctmp = Buf("ctmp")
                k3 = 0
                for fc in range(6):
                    X, bX, Y, bY = Xs[fc % 2], bXs[fc % 2], Ys[fc % 2], bYs[fc % 2]
                    S.dma("sp", lambda e, X=X, fc=fc: e.dma_start(out=X[:], in_=PT[R_XBC + fc * 128:R_XBC + (fc + 1) * 128, :]), [B_PT], [bX])
                    S.op("dve", lambda e, X=X, Y=Y, fc=fc: e.tensor_scalar(out=Y[:], in0=X[:], scalar1=cp[:, fc, 2:3], scalar2=cp[:, fc, 5:6],
                                                                           op0=ALU.mult, op1=ALU.add), [bX, bcp], [bY])
                    for (lo, hi) in ((0, NCTX), (NCTX, NT)):
                        for kk in (0, 1, 3, 4):
                            dlt = kk - 2
                            a, b_ = lo + max(0, -dlt), hi - max(0, dlt)
                            S.op("act", lambda e, X=X, fc=fc, kk=kk, a=a, b_=b_, dlt=dlt: e.activation(
                                out=ctmp[:, a:b_], in_=X[:, a + dlt:b_ + dlt], func=AF.Copy, scale=cp[:, fc, kk:kk + 1]), [bX, bcp], [bctmp])
                            S.op("dve", lambda e, Y=Y, a=a, b_=b_: e.tensor_tensor(
                                out=Y[:, a:b_], in0=Y[:, a:b_], in1=ctmp[:, a:b_], op=ALU.add), [bY, bctmp], [bY])
                    S.op("act", lambda e, Y=Y: e.activation(out=Y[:], in_=Y[:], func=AF.Silu), [bY], [bY])
                    if fc < 2:
                        for tt in range(NTILE):
                            pb = nb()
                            s_, bs_ = st32[k3 % 3], bst32[k3 % 3]
                            k3 += 1
                            S.op("pe", lambda e, pb=pb, Y=Y, tt=tt: e.transpose(out=psum[pb][:, 0:128], in_=Y[:, tt * 128:(tt + 1) * 128],
                                                                               identity=ident_f[:]), [bY, B_identf], [B_ps[pb]])
                            if k3 % 2:
                                S.op("act", lambda e, pb=pb, s_=s_: e.copy(out=s_[:], in_=psum[pb][:, 0:128]), [B_ps[pb]], [bs_])
                            else:
                                S.op("dve", lambda e, pb=pb, s_=s_: e.tensor_copy(out=s_[:], in_=psum[pb][:, 0:128]), [B_ps[pb]], [bs_])
                            S.dma("pool", lambda e, s_=s_, tt=tt, fc=fc: e.dma_start(out=XS[tt * 128:(tt + 1) * 128, fc * 128:(fc + 1) * 128],
                                                                                    in_=s_[:]), [bs_], [B_XS])
                    else:
                        yb, byb = Yb[fc % 2], bYb[fc % 2]
                        S.op("pool", lambda e, Y=Y, yb=yb: e.tensor_copy(out=yb[:], in_=Y[:]), [bY], [byb])
                        S.dma("pool", lambda e, yb=yb, fc=fc: e.dma_start(out=XCT[(fc - 2) * 128:(fc - 1) * 128, :], in_=yb[:]), [byb], [B_XCT])
                        if fc < 4:
                            for tt in range(NTILE):
                                pb = nb()
                                pt = psum[pb].bitcast(BF16)
                                s_, bs_ = st16[k3 % 3], bst16[k3 % 3]
                                k3 += 1
                                S.op("pe", lambda e, pt=pt, yb=yb, tt=tt: e.transpose(out=pt[:, 0:128], in_=yb[:, tt * 128:(tt + 1) * 128],
                                                                                     identity=ident_b[:]), [byb, B_identb], [B_ps[pb]])
                                S.op("act", lambda e, pt=pt, s_=s_: e.copy(out=s_[:], in_=pt[:, 0:128]), [B_ps[pb]], [bs_])
                                S.dma("pool", lambda e, s_=s_, tt=tt, fc=fc: e.dma_start(
                                    out=BTOK[tt * 128:(tt + 1) * 128, (fc - 2) * 128:(fc - 1) * 128], in_=s_[:]), [bs_], [B_BTOK])
            tokq = sb("tokq", [128, NTILE, 4, 8], F32)
            btokq = Buf("tokq")
            cumrow = sb("cumrow", [8, NT], F32)
            bcum = Buf("cumrow")
            ddb = sb("ddb", [128, 8, NCH], F32)
            bddb = Buf("ddb")
            selr = sb("selr", [8, 8, 128], F32)
            bselr = Buf("selr")
            S.dma("sp", lambda e: e.dma_start(out=selr[:], in_=selr_in), [B_const], [bselr])
            Sbf = [sb("sSbf%d" % i, [128, NCH, 256], BF16) for i in range(2)]
            bSbf = [Buf("sSbf") for i in range(2)]
            with Scope():
                dtp = sb("dtp", [8, 4], F32)
                bdtp = Buf("dtp")
                S.dma("sp", lambda e: e.dma_start(out=dtp[:], in_=dtp_in[l]), [B_const], [bdtp])
                negA = sb("negA", [8, 1], F32)
                S.op("act", lambda e: e.activation(out=negA[:], in_=dtp[:, 1:2], func=AF.Exp), [bdtp], [bdtp])
                S.op("dve", lambda e: e.tensor_scalar_mul(out=negA[:], in0=negA[:], scalar1=-1.0), [bdtp], [bdtp])
                m01 = sb("sm01", [8, NT], BF16)
                bm01 = Buf("sm01")
                S.op("pool", lambda e: e.memset(m01[:], 1.0), [], [bm01])
                S.op("pool", lambda e: e.memset(v3(m01[:])[:, :, 0:1], 0.0), [], [bm01])
                T0, T1, T2, T3 = (sb("sT%d" % i, [8, NT], F32) for i in range(4))
                b0, b1, b2, b3 = Buf("sT0"), Buf("sT1"), Buf("sT2"), Buf("sT3")
                dch = sb("dch", [8, NCH], F32)
                bdch = Buf("dch")
                S.dma("sp", lambda e: e.dma_start(out=T0[:], in_=PT[R_DT:R_DT + 8, :]), [B_PT], [b0])
                S.op("dve", lambda e: e.tensor_scalar_add(out=T0[:], in0=T0[:], scalar1=dtp[:, 0:1]), [b0, bdtp], [b0])
                S.op("act", lambda e: e.activation(out=T1[:], in_=T0[:], func=AF.Abs), [b0], [b1])
                S.op("act", lambda e: e.activation(out=T1[:], in_=T1[:], func=AF.Exp, scale=-1.0), [b1], [b1])
                S.op("act", lambda e: e.activation(out=T1[:], in_=T1[:], func=AF.Ln, bias=1.0, scale=1.0), [b1], [b1])
                S.op("dve", lambda e: e.tensor_scalar_max(out=T0[:], in0=T0[:], scalar1=0.0), [b0], [b0])
                S.op("dve", lambda e: e.tensor_tensor(out=T0[:], in0=T0[:], in1=T1[:], op=ALU.add), [b0, b1], [b0])
                S.op("dve", lambda e: e.tensor_scalar_mul(out=T1[:], in0=T0[:], scalar1=negA[:, 0:1]), [b0, bdtp], [b1])
                S.op("dve", lambda e: e.tensor_tensor_scan(out=T2[:], data0=m01[:], data1=T1[:], initial=0.0, op0=ALU.mult, op1=ALU.add),
                     [bm01, b1], [b2])
                totb = v3(T2[:])[:, :, 63:64].to_broadcast([8, NCH, 64])
                S.op("dve", lambda e: e.tensor_tensor(out=v3(T3[:]), in0=totb, in1=v3(T2[:]), op=ALU.subtract), [b2], [b3])
                S.op("dve", lambda e: e.tensor_tensor(out=T3[:], in0=T3[:], in1=T1[:], op=ALU.add), [b3, b1], [b3])
                S.op("dve", lambda e: e.tensor_scalar_mul(out=cumrow[:], in0=T2[:], scalar1=dtp[:, 2:3]), [b2, bdtp], [bcum])
                S.op("dve", lambda e: e.tensor_scalar_mul(out=T3[:], in0=T3[:], scalar1=dtp[:, 3:4]), [b3, bdtp], [b3])
                S.op("dve", lambda e: e.tensor_tensor(out=cumrow[:], in0=cumrow[:], in1=T3[:], op=ALU.add), [b3, bcum], [bcum])
                S.op("act", lambda e: e.activation(out=dch[:], in_=v3(T2[:])[:, :, 63], func=AF.Exp), [b2], [bdch])
                S.op("act", lambda e: e.activation(out=T1[:], in_=cumrow[:], func=AF.Exp), [bcum, b1], [b1])
                S.op("dve", lambda e: e.tensor_tensor(out=v3(T3[:]), in0=totb, in1=v3(cumrow[:]), op=ALU.subtract), [b2, bcum, b3], [b3])
                S.op("act", lambda e: e.activation(out=T3[:], in_=T3[:], func=AF.Exp), [b3], [b3])
                S.op("dve", lambda e: e.tensor_tensor(out=T3[:], in0=T3[:], in1=T0[:], op=ALU.mult), [b3, b0], [b3])
                S.op("dve", lambda e: e.tensor_scalar_mul(out=T2[:], in0=cumrow[:], scalar1=-1.0), [bcum, b2, b3], [b2])
                for r in range(8):
                    pb = nb()
                    S.op("pe", lambda e, pb=pb, r=r: e.matmul(psum[pb][:, 0:NCH], lhsT=selr[:, r, :], rhs=dch[:], start=True, stop=True),
                         [bselr, bdch], [B_ps[pb]])
                    S.op("act", lambda e, pb=pb, r=r: e.copy(out=ddb[:, r, :], in_=psum[pb][:, 0:NCH]), [B_ps[pb]], [bddb])
                for tt in range(NTILE):
                    pb = nb()
                    for qi, (T, bT) in enumerate(((T0, b0), (T2, b2), (T1, b1), (T3, b3))):
                        S.op("pe", lambda e, pb=pb, qi=qi, T=T, tt=tt: e.transpose(
                            out=psum[pb][:, qi * 8:(qi + 1) * 8], in_=T[0:8, tt * 128:(tt + 1) * 128], identity=ident_f[0:8, 0:8]),
                            [bT, B_identf], [B_ps[pb]])
                    S.op("dve", lambda e, pb=pb, tt=tt: e.tensor_copy(out=tokq[:, tt, :, :].rearrange("p a b -> p (a b)"),
                                                                      in_=psum[pb][:, 0:32]), [B_ps[pb]], [btokq])
            dbg_dump("DBG_tokq", tokq[:].rearrange("p a b c -> p (a b c)"), btokq)
            dbg_dump("DBG_cum", cumrow[:], bcum)
            dbg_dump("DBG_ddb", ddb[:].rearrange("p a b -> p (a b)"), bddb)
            for dr in range(2):
                with Scope():
                    Sf = sb("sSf", [128, 256], F32)
                    bSf = Buf("sSf")
                    S.op("pool", lambda e: e.memset(Sf[:], 0.0), [], [bSf])
                    xst = [sb("sxs%d" % i, [128, 256], F32) for i in range(2)]
                    bxst = [Buf("sxs") for i in range(2)]
                    btt = [sb("sbt%d" % i, [128, 256], BF16) for i in range(2)]
                    bbtt = [Buf("sbt") for i in range(2)]
                    vw = [sb("svw%d" % i, [128, 2, 256], BF16) for i in range(2)]
                    bvw = [Buf("svw") for i in range(2)]
                    for i in range(2):
                        S.op("pool", lambda e, i=i: e.memset(vw[i][:], 0.0), [], [bvw[i]])
                    order = list(range(NTILE)) if dr == 0 else [1, 0] + list(range(NTILE - 1, 1, -1))
                    for n, tt in enumerate(order):
                        rows = slice(tt * 128, (tt + 1) * 128)
                        x_, bx_, t_, bt_, w_, bw_ = xst[n % 2], bxst[n % 2], btt[n % 2], bbtt[n % 2], vw[n % 2], bvw[n % 2]
                        S.dma("sp", lambda e, x_=x_, rows=rows: e.dma_start(out=x_[:], in_=XS[rows, :]), [B_XS], [bx_])
                        S.dma("sp", lambda e, t_=t_, rows=rows: e.dma_start(out=t_[:], in_=BTOK[rows, :]), [B_BTOK], [bt_])
                        for half in range(2):
                            pr = slice(half * 64, (half + 1) * 64)
                            S.op("dve", lambda e, x_=x_, w_=w_, tt=tt, half=half, pr=pr: e.tensor_tensor(
                                out=w_[pr, half, :].rearrange("p (h d) -> p h d", d=64), in0=x_[pr, :].rearrange("p (h d) -> p h d", d=64),
                                in1=tokq[pr, tt, 3, dr * 4:(dr + 1) * 4].unsqueeze(2).to_broadcast([64, 4, 64]), op=ALU.mult),
                                [bx_, btokq], [bw_])
                        for half in ((0, 1) if dr == 0 else (1, 0)):
                            c = tt * 2 + half
                            S.op("pool", lambda e, c=c: e.tensor_copy(out=Sbf[dr][:, c, :], in_=Sf[:]), [bSf], [bSbf[dr]])
                            pu = nb()
                            for h in range(4):
                                gq = h // 2
                                S.op("pe", lambda e, pu=pu, h=h, gq=gq, half=half, t_=t_, w_=w_: e.matmul(
                                    psum[pu][:, h * 64:(h + 1) * 64], lhsT=t_[:, gq * 128:(gq + 1) * 128],
                                    rhs=w_[:, half, h * 64:(h + 1) * 64], start=True, stop=True),
                                    [bt_, bw_], [B_ps[pu]])
                            S.op("dve", lambda e, c=c: e.tensor_tensor(
                                out=Sf[:].rearrange("p (h d) -> p h d", d=64), in0=Sf[:].rearrange("p (h d) -> p h d", d=64),
                                in1=ddb[:, dr * 4:(dr + 1) * 4, c].unsqueeze(2).to_broadcast([128, 4, 64]), op=ALU.mult),
                                [bSf, bddb, bSbf[dr]], [bSf])
                            S.op("dve", lambda e, pu=pu: e.tensor_tensor(out=Sf[:], in0=Sf[:], in1=psum[pu][:, 0:256], op=ALU.add),
                                 [bSf, B_ps[pu]], [bSf])
            with Scope():
                negm = sb("snegm", [128, 2, 128], F32)
                bnegm = Buf("snegm")
                S.dma("sp", lambda e: e.dma_start(out=negm[:], in_=negm_in.rearrange("d p t -> p d t")), [B_const], [bnegm])
                dskb, bdskb = load_bcast("dskb", dsk_in[l, :], 256, B_const)
                snw, bsnw = load_bcast("snw", snorm_in[l, :], 256, B_const)
                NB = 2
                bT = [sb("sbT%d" % i, [128, 2, 128], BF16) for i in range(NB)]
                bbT = [Buf("sbT") for i in range(NB)]
                cT = [sb("scT%d" % i, [128, 2, 128], BF16) for i in range(NB)]
                bcT = [Buf("scT") for i in range(NB)]
                cA = [sb("scA%d" % i, [128, 2, 128], BF16) for i in range(NB)]
                bcA = [Buf("scA") for i in range(NB)]
                cB = [sb("scB%d" % i, [128, 2, 128], BF16) for i in range(NB)]
                bcB = [Buf("scB") for i in range(NB)]
                for i in range(NB):
                    S.op("pool", lambda e, i=i: e.memset(cA[i][:], 0.0), [], [bcA[i]])
                    S.op("pool", lambda e, i=i: e.memset(cB[i][:], 0.0), [], [bcB[i]])
                xst = [sb("gxs%d" % i, [128, 256], F32) for i in range(NB)]
                bxst = [Buf("gxs") for i in range(NB)]
                zt = [sb("gz%d" % i, [128, 256], F32) for i in range(NB)]
                bzt = [Buf("gz") for i in range(NB)]
                vd = sb("gvd", [128, 2, 256], BF16)
                bvd = Buf("gvd")
                Lt = [sb("gL%d" % i, [128, 128], F32) for i in range(3)]
                bLt = [Buf("gL") for i in range(3)]
                Wt = [sb("gW%d" % i, [128, 128], BF16) for i in range(3)]
                bWt = [Buf("gW") for i in range(3)]
                acc = sb("gacc", [128, 256], F32)
                bacc = Buf("gacc")
                tmp = sb("gtmp", [128, 256], F32)
                btmp = Buf("gtmp")
                ssq = sb("gss", [128, 1], F32)
                bssq = Buf("gss")
                XCTv = XCT.rearrange("(a g p) t -> a p g t", a=2, g=2)
                kL = 0
                for n, tt in enumerate(range(NTILE) if ctx_out else range(2, NTILE)):
                    rows = slice(tt * 128, (tt + 1) * 128)
                    tl = slice(tt * 128, (tt + 1) * 128)
                    i = n % NB
                    S.dma("sp", lambda e, i=i, tl=tl: e.dma_start(out=bT[i][:], in_=XCTv[0][:, :, tl]), [B_XCT], [bbT[i]])
                    S.dma("sp", lambda e, i=i, tl=tl: e.dma_start(out=cT[i][:], in_=XCTv[1][:, :, tl]), [B_XCT], [bcT[i]])
                    S.dma("sp", lambda e, i=i, tt=tt: e.dma_start(out=cA[i][:, :, 0:64], in_=XCTv[1][:, :, tt * 128:tt * 128 + 64]),
                          [B_XCT], [bcA[i]])
                    S.dma("sp", lambda e, i=i, tt=tt: e.dma_start(out=cB[i][:, :, 64:128], in_=XCTv[1][:, :, tt * 128 + 64:tt * 128 + 128]),
                          [B_XCT], [bcB[i]])
                    S.dma("sp", lambda e, i=i, rows=rows: e.dma_start(out=xst[i][:], in_=XS[rows, :]), [B_XS], [bxst[i]])
                    S.dma("sp", lambda e, i=i, rows=rows: e.dma_start(out=zt[i][:], in_=PTOK[rows, T_BZ:T_BZ + 256]), [B_PTOK], [bzt[i]])
                    for dr in range(2):
                        S.op("dve", lambda e, i=i, dr=dr, tt=tt: e.tensor_tensor(
                            out=vd[:, dr, :].rearrange("p (h d) -> p h d", d=64), in0=xst[i][:].rearrange("p (h d) -> p h d", d=64),
                            in1=tokq[:, tt, 0, dr * 4:(dr + 1) * 4].unsqueeze(2).to_broadcast([128, 4, 64]), op=ALU.mult),
                            [bxst[i], btokq], [bvd])
                    for gq in range(2):
                        S.op("pe", lambda e, i=i, gq=gq: e.matmul(psum[0][:, gq * 128:(gq + 1) * 128], lhsT=bT[i][:, gq, :], rhs=cT[i][:, gq, :],
                                                                  start=True, stop=True), [bbT[i], bcT[i]], [B_ps[0]])
                    for h in range(4):
                        for dr in range(2):
                            r = dr * 4 + h
                            gq = h // 2
                            pl = nb(4, 8)
                            S.op("pe", lambda e, pl=pl, r=r, tl=tl: e.matmul(psum[pl][:, 0:128], lhsT=selr[:, r, :], rhs=cumrow[0:8, tl],
                                                                            start=True, stop=False), [bselr, bcum], [B_ps[pl]])
                            S.op("pe", lambda e, pl=pl, dr=dr: e.matmul(psum[pl][:, 0:128], lhsT=ident_f[:], rhs=negm[:, dr, :],
                                                                        start=False, stop=True), [B_identf, bnegm], [B_ps[pl]])
                            L_, bL_, W_, bW_ = Lt[kL % 3], bLt[kL % 3], Wt[kL % 3], bWt[kL % 3]
                            kL += 1
                            S.op("act", lambda e, pl=pl, L_=L_, tt=tt, r=r: e.activation(
                                out=L_[:], in_=psum[pl][:, 0:128], func=AF.Exp, bias=tokq[:, tt, 1, r:r + 1], scale=1.0),
                                [B_ps[pl], btokq], [bL_])
                            S.op("dve", lambda e, L_=L_, W_=W_, gq=gq: e.tensor_tensor(out=W_[:], in0=psum[0][:, gq * 128:(gq + 1) * 128],
                                                                                      in1=L_[:], op=ALU.mult), [B_ps[0], bL_], [bW_])
                            S.op("pe", lambda e, W_=W_, dr=dr, h=h: e.matmul(psum[1][:, h * 64:(h + 1) * 64], lhsT=W_[:],
                                                                             rhs=vd[:, dr, h * 64:(h + 1) * 64], start=(dr == 0), stop=(dr == 1)),
                                 [bW_, bvd], [B_ps[1]])
                            S.op("pe", lambda e, i=i, dr=dr, h=h, gq=gq, tt=tt: e.matmul(
                                psum[2 + dr][:, h * 64:(h + 1) * 64], lhsT=cA[i][:, gq, :], rhs=Sbf[dr][:, 2 * tt, h * 64:(h + 1) * 64],
                                start=True, stop=False), [bcA[i], bSbf[dr]], [B_ps[2 + dr]])
                            S.op("pe", lambda e, i=i, dr=dr, h=h, gq=gq, tt=tt: e.matmul(
                                psum[2 + dr][:, h * 64:(h + 1) * 64], lhsT=cB[i][:, gq, :], rhs=Sbf[dr][:, 2 * tt + 1, h * 64:(h + 1) * 64],
                                start=False, stop=True), [bcB[i], bSbf[dr]], [B_ps[2 + dr]])
                    S.op("act", lambda e: e.copy(out=acc[:], in_=psum[1][:, 0:256]), [B_ps[1]], [bacc])
                    for dr in range(2):
                        S.op("dve", lambda e, dr=dr, tt=tt: e.tensor_tensor(
                            out=tmp[:].rearrange("p (h d) -> p h d", d=64), in0=psum[2 + dr][:, 0:256].rearrange("p (h d) -> p h d", d=64),
                            in1=tokq[:, tt, 2, dr * 4:(dr + 1) * 4].unsqueeze(2).to_broadcast([128, 4, 64]), op=ALU.mult),
                            [B_ps[2 + dr], btokq], [btmp])
                        S.op("pool", lambda e: e.tensor_tensor(out=acc[:], in0=acc[:], in1=tmp[:], op=ALU.add), [bacc, btmp], [bacc])
                    S.op("dve", lambda e, i=i: e.tensor_tensor(out=tmp[:], in0=xst[i][:], in1=dskb[:], op=ALU.mult), [bxst[i], bdskb], [btmp])
                    S.op("pool", lambda e: e.tensor_tensor(out=acc[:], in0=acc[:], in1=tmp[:], op=ALU.add), [bacc, btmp], [bacc])
                    S.op("act", lambda e, i=i: e.activation(out=zt[i][:], in_=zt[i][:], func=AF.Silu), [bzt[i]], [bzt[i]])
                    S.op("pool", lambda e, i=i: e.tensor_tensor(out=acc[:], in0=acc[:], in1=zt[i][:], op=ALU.mult), [bacc, bzt[i]], [bacc])
                    S.op("dve", lambda e: e.tensor_tensor(out=tmp[:], in0=acc[:], in1=acc[:], op=ALU.mult), [bacc], [btmp])
                    S.op("dve", lambda e: e.tensor_reduce(out=ssq[:], in_=tmp[:], axis=AX.X, op=ALU.add), [btmp], [bssq])
                    S.op("act", lambda e: e.activation(out=ssq[:], in_=ssq[:], func=AF.Sqrt, bias=eps_t[:], scale=1.0 / 256), [bssq, B_eps], [bssq])
                    S.op("dve", lambda e: e.reciprocal(out=ssq[:], in_=ssq[:]), [bssq], [bssq])
                    S.op("dve", lambda e: e.tensor_scalar(out=tmp[:], in0=acc[:], scalar1=ssq[:], scalar2=None, op0=ALU.mult), [bacc, bssq], [btmp])
                    S.op("pool", lambda e: e.tensor_tensor(out=tmp[:], in0=tmp[:], in1=snw[:], op=ALU.mult), [btmp, bsnw], [btmp])
                    S.dma("pool", lambda e, rows=rows: e.dma_start(out=BR[rows, 256:512], in_=tmp[:]), [btmp], [B_BR])

        def run_stage(name, fn):
            with Scope():
                fn()
            return stop_after == name

        if "br_init" in debug:
            br_init = dram_in("br_init", [NT, 512])
            S.dma("sp", lambda e: e.dma_start(out=BR[:, 0:512], in_=br_init), [B_const], [B_BR])
        for l in range(n_layers):
            last = (l == DEPTH - 1)
            ctx_out = not last
            tiles = list(range(NTILE)) if ctx_out else list(range(2, NTILE))
            xsrc, bxsrc = (xin, B_xin) if l == 0 else (XRES, B_XRES)
            if run_stage("mod", lambda: stage_mod(l)):
                break
            if "inproj" not in skip:
                mark = sbtop[0]
                hT = sb("hT", [128, 8, NT], BF16)
                B_hT = [Buf("hT%d" % i) for i in range(NTILE)]
                with Scope():
                    stage_ln_mod(l, xsrc, bxsrc, 0, 1, hT, B_hT, list(range(NTILE)))
                with Scope():
                    stage_inproj(l, hT, B_hT)
                sbtop[0] = mark
            if stop_after == "inproj":
                break
            if "gqa" not in skip and run_stage("gqa", lambda: stage_gqa(l, ctx_out)):
                break
            if "na" not in skip and run_stage("na", lambda: stage_na(l, ctx_out)):
                break
            if "hgrn" not in skip and run_stage("hgrn", lambda: stage_hgrn(l, ctx_out)):
                break
            if "ssd" not in skip:
                with Scope():
                    stage_ssd(l, ctx_out)
                if stop_after == "ssd":
                    break
            if stop_after == "mixers":
                break
            if run_stage("merge", lambda: stage_merge(l, tiles, xsrc, bxsrc)):
                break
            mark = sbtop[0]
            hT = sb("h2T", [128, 8, NT], BF16)
            B_hT = [Buf("h2T%d" % i) for i in range(NTILE)]
            with Scope():
                stage_ln_mod(l, X1, B_X1, 3, 4, hT, B_hT, tiles)
            with Scope():
                stage_ffn_up(l, hT, B_hT, tiles[0] * 128)
            sbtop[0] = mark
            if run_stage("ffn", lambda: stage_ffn_down(l, tiles, last)):
                break
        dbg_bufs = {"MODD": B_MODD, "PT": B_PT, "PTOK": B_PTOK, "XRES": B_XRES, "X1": B_X1, "BR": B_BR,
                    "XS": B_XS, "XCT": B_XCT, "BTOK": B_BTOK, "ACTT": B_ACTT}
        for nm, (ap_, b_) in DBG.items():
            dbg_bufs[nm] = b_
        wait_bufs = [dbg_bufs[n] for n in debug if n in dbg_bufs] + [B_out]
        S.final_wait("sp", [b for b in wait_bufs if b.w is not None])
        if debug:
            print('instr counts', {e: len(v) for e, v in S.lists.items()}, 'dma sems', S.nsem, 'max dma sem val', max([t[1] for t in S.dsems.values()] + [0]), sorted([t[1] for t in S.dsems.values()])[-5:])
        S.emit()
    return nc


def build_natab(rpb):
    tab = np.full((128, NA_NBLK, 4, 64), NEG, np.float32)
    qc = np.arange(64)
    win0 = np.clip(qc - 8, 0, 48)
    kc = np.arange(64)
    inwin = (kc[:, None] >= win0[None, :]) & (kc[:, None] < win0[None, :] + 16)
    dcol = np.clip(kc[:, None] - qc[None, :] + 15, 0, 30)
    for key, bi in NA_BLOCKS.items():
        for half in range(2):
            valid, dr = key[half]
            if not valid:
                continue
            vals = rpb[:, dr + 7, :][:, dcol]
            blk = np.where(inwin[None], vals, np.float32(NEG))
            tab[half * 64:(half + 1) * 64, bi] = blk.transpose(1, 0, 2)
    return tab.reshape(128, NA_NBLK * 256)


def rope_tables():
    t = np.arange(NLAT)
    nf = 16
    inv_freq = (10000.0 ** (-np.arange(nf, dtype=np.float32) / nf)).astype(np.float32)
    row = (t // 64).astype(np.float32)[:, None] * inv_freq
    col = (t % 64).astype(np.float32)[:, None] * inv_freq
    C = np.concatenate([np.cos(row), np.cos(row), np.cos(col), np.cos(col)], axis=1)
    Sn = np.concatenate([-np.sin(row), np.sin(row), -np.sin(col), np.sin(col)], axis=1)
    cs = np.zeros((NT, 128), np.float32)
    cs[:NCTX, 0:64] = 1.0
    cs[NCTX:, 0:64] = C
    cs[NCTX:, 64:128] = Sn
    return cs


def make_consts():
    si = np.arange(128)[:, None]
    ti = np.arange(128)[None, :]
    same = (si // 64) == (ti // 64)
    same32 = (si // 32) == (ti // 32)
    hmask2 = np.stack([same & (si <= ti), same & (si >= ti)]).astype(np.float32)
    hmask = np.stack([same32 & (si <= ti), same32 & (si >= ti),
                      same & (si % 64 < 32) & (ti % 64 >= 32), same & (si % 64 >= 32) & (ti % 64 < 32)]).astype(np.float32)
    selr = np.zeros((8, 8, 128), np.float32)
    for r in range(8):
        selr[r, r, :] = 1.0
    negm = np.where(hmask2 > 0, 0.0, NEG).astype(np.float32)
    return {"ident": np.eye(128, dtype=np.float32), "rope_cs": rope_tables(), "hmask": hmask, "selr": selr, "negm": negm}


def prep_inputs(inputs):
    x, c, ctx, c_ctx = inputs["x"], inputs["c"], inputs["ctx"], inputs["c_ctx"]
    w_in = inputs["w_in"]
    fm_idx = np.concatenate([np.arange(O_AQ, O_AQ + 768), np.arange(O_XBC, O_XBC + 768),
                             np.arange(O_CQ, O_CQ + 512), np.arange(O_DTF, O_DTF + 8)])
    w_fm = np.zeros((DEPTH, D, FM_COLS + 128), np.float32)
    w_fm[:, :, :FM_COLS + 8] = w_in[:, :, fm_idx]
    tm_idx = np.concatenate([np.arange(O_AV, O_AV + 512), np.arange(O_BZ, O_BZ + 256), np.arange(O_CV, O_CV + 256),
                             np.arange(O_DQ, O_DQ + 512), np.arange(O_GATE, O_GATE + 4096)])
    w_tm = np.ascontiguousarray(w_in[:, :, tm_idx])
    wu = inputs["ffn_w_up"]
    w_up = np.stack([wu[:, :, :FH].reshape(DEPTH, D, FH // 128, 128), wu[:, :, FH:].reshape(DEPTH, D, FH // 128, 128)],
                    axis=3).reshape(DEPTH, D, 2 * FH)
    lnp = np.stack([inputs["ln1_g"], inputs["ln1_b"], inputs["ln2_g"], inputs["ln2_b"]], axis=1)
    qkw = np.concatenate([np.tile(inputs["q_norm"], (1, 4)), np.tile(inputs["k_norm"], (1, 2))], axis=1)
    natab = np.stack([build_natab(inputs["na_rpb"][l]) for l in range(DEPTH)], axis=0)
    shared = dict(ada_w=np.ascontiguousarray(inputs["ada_w"]), ada_b=np.ascontiguousarray(inputs["ada_b"]),
                  w_fm=w_fm, w_tm=w_tm, w_br=np.ascontiguousarray(inputs["w_branch"].reshape(DEPTH, D, D)),
                  w_o=np.ascontiguousarray(inputs["w_out"]), w_up=np.ascontiguousarray(w_up),
                  w_dn=np.ascontiguousarray(inputs["ffn_w_down"]), lnp=np.ascontiguousarray(lnp.astype(np.float32)),
                  qkw=np.ascontiguousarray(qkw.astype(np.float32)), natab=np.ascontiguousarray(natab),
                  lbp=np.ascontiguousarray(inputs["hgrn_lb"].reshape(2, DEPTH, 2, 128).transpose(3, 0, 1, 2).reshape(128, 8).astype(np.float32)),
                  hnorm=np.ascontiguousarray(inputs["hgrn_norm"].astype(np.float32)),
                  convp=np.ascontiguousarray(np.concatenate([inputs["ssd_conv_w"], inputs["ssd_conv_b"][:, None, :]], axis=1)
                                             .reshape(DEPTH, 6, 6, 128).transpose(0, 3, 2, 1).astype(np.float32)),
                  dtp=np.ascontiguousarray(np.stack([inputs["ssd_dt_bias"].reshape(DEPTH, 8), inputs["ssd_a_log"].reshape(DEPTH, 8),
                                                     np.tile(np.array([1, 1, 1, 1, 0, 0, 0, 0], np.float32), (DEPTH, 1)),
                                                     np.tile(np.array([0, 0, 0, 0, 1, 1, 1, 1], np.float32), (DEPTH, 1))], axis=2).astype(np.float32)),
                  dsk=np.ascontiguousarray(np.repeat(inputs["ssd_d"], 64, axis=1).astype(np.float32)),
                  snorm=np.ascontiguousarray(inputs["ssd_norm"].astype(np.float32)))
    shared.update(make_consts())
    in_maps = []
    for b in range(x.shape[0]):
        m = dict(shared)
        m["xin"] = np.ascontiguousarray(np.concatenate([ctx[b], x[b]], axis=0))
        cv = np.concatenate([c[b].reshape(8, 128).T, c_ctx.reshape(8, 128).T], axis=1)
        m["cvec"] = np.ascontiguousarray(cv.astype(np.float32))
        in_maps.append(m)
    return in_maps


def kernel(**inputs):
    inputs = {k: np.asarray(v) for k, v in inputs.items()}
    in_maps = prep_inputs(inputs)
    nc = build_program()
    res = run_bass_kernel_spmd(nc, in_maps, core_ids=list(range(len(in_maps))))
    return np.stack([r["out"] for r in res.results], axis=0)
```

```python
import types
import numpy as np
from contextlib import ExitStack
import concourse.bass as bass
import concourse.mybir as mybir
from concourse.bass_utils import run_bass_kernel_spmd

F32 = mybir.dt.float32
BF16 = mybir.dt.bfloat16
AF = mybir.ActivationFunctionType
ALU = mybir.AluOpType
AX = mybir.AxisListType

D = 1024
NCTX = 256
NLAT = 4096
NT = NCTX + NLAT
NTILE = NT // 128
DEPTH = 2
EPS = 1e-6
ALPHA = (2.0 * DEPTH) ** 0.25
FH = 2816
FM_COLS = 256 * 3 + 768 + 256 + 256
FM_DT = 8
TM_COLS = 256 * 5 + 128 + 128 + 4096
O_AQ, O_AFF, O_AFB, O_AV, O_AG = 0, 256, 512, 768, 1024
O_BZ, O_XBC, O_DTF, O_DTB = 1280, 1536, 2304, 2308
O_CQ, O_CK, O_CV = 2312, 2568, 2824
O_DQ, O_DK, O_DV = 3080, 3336, 3464
O_GATE = 3592
T_AV, T_AG, T_BZ, T_CV, T_DQ, T_DK, T_DV, T_GATE = 0, 256, 512, 768, 1024, 1280, 1408, 1536
R_AQ, R_AFF, R_AFB, R_XBC, R_CQ, R_CK, R_DT = 0, 256, 512, 768, 1536, 1792, 2048


def na_plan_and_blocks():
    blocks = {}
    plan = []
    for i in range(32):
        starts = [min(max(2 * i + a - 4, 0), 56) for a in range(2)]
        lo = min(starts) // 2
        hi = (max(starts) + 7) // 2
        lst = []
        for j in range(lo, hi + 1):
            ids = []
            for a in range(2):
                qr = 2 * i + a
                key = []
                for half in range(2):
                    kr = 2 * j + half
                    valid = starts[a] <= kr < starts[a] + 8
                    key.append((valid, kr - qr if valid else 0))
                key = tuple(key)
                if key not in blocks:
                    blocks[key] = len(blocks)
                ids.append(blocks[key])
            lst.append((j, ids[0], ids[1]))
        plan.append(lst)
    return plan, blocks


NA_PLAN, NA_BLOCKS = na_plan_and_blocks()
NA_NBLK = len(NA_BLOCKS)
NEG = -30000.0


class Buf:
    __slots__ = ("name", "w", "r", "sem", "dcount", "kind")

    def __init__(self, name):
        self.name = name
        self.w = None
        self.r = {}
        self.sem = None
        self.dcount = 0
        self.kind = None


def _freeze(f):
    if f is None or f.__closure__ is None:
        return f
    cells = []
    for c in f.__closure__:
        try:
            cells.append(types.CellType(c.cell_contents))
        except ValueError:
            cells.append(c)
    return types.FunctionType(f.__code__, f.__globals__, f.__name__, f.__defaults__, tuple(cells))


class Sched:
    ENGS = ("pe", "act", "dve", "pool", "sp")

    def __init__(self, nc, es):
        self.nc = nc
        self.es = es
        self.lists = {e: [] for e in self.ENGS}
        self.esem = {e: es.enter_context(nc.semaphore("sem_" + e)) for e in ("pe", "act", "dve", "pool")}
        self.ecount = {e: 0 for e in ("pe", "act", "dve", "pool")}
        self.waited = {e: {} for e in self.ENGS}
        self.nsem = 0
        self.nrot = 0
        self.free = {"sw": [], "hw": []}
        self.sem_bufs = []
        self.dsems = {}
        self.sem_owner = {}
        for e, s in self.esem.items():
            self.sem_owner[id(s)] = e

    def _deps(self, eng, reads, writes):
        need = {}

        def add(tok):
            if tok is None:
                return
            sem, val = tok
            k = id(sem)
            if k not in need or need[k][1] < val:
                need[k] = (sem, val)

        for b in reads:
            add(b.w)
        for b in writes:
            add(b.w)
            for t in b.r.values():
                add(t)
        waits = []
        for k, (sem, val) in need.items():
            if eng == "pe" and self.sem_owner.get(k) == "pe":
                continue
            if self.waited[eng].get(k, 0) >= val:
                continue
            self.waited[eng][k] = val
            waits.append((sem, val))
        return waits

    def _mark(self, tok, reads, writes):
        k = id(tok[0])
        for b in reads:
            b.r[k] = tok
        for b in writes:
            b.w = tok
            b.r = {}

    def op(self, eng, thunk, reads=(), writes=()):
        waits = self._deps(eng, reads, writes)
        self.ecount[eng] += 1
        tok = (self.esem[eng], self.ecount[eng])
        self.lists[eng].append((waits, _freeze(thunk), tok[0], 1))
        self._mark(tok, reads, writes)

    def dma(self, queue, thunk, reads, writes, sembuf=None):
        waits = self._deps(queue, reads, writes)
        sb = sembuf if sembuf is not None else writes[0]
        kind = "sw" if queue == "pool" else "hw"
        if sb.sem is not None and sb.kind != kind:
            raise AssertionError("buffer %s written by both DMA queue kinds" % sb.name)
        if sb.sem is None:
            sb.kind = kind
            if self.free[kind]:
                sb.sem, sb.dcount = self.free[kind].pop()
            else:
                sb.sem = self.es.enter_context(self.nc.semaphore("dsem%d" % self.nsem))
                sb.dcount = 0
                self.nsem += 1
            self.sem_bufs.append(sb)
        sb.dcount += 16
        tok = (sb.sem, sb.dcount)
        self.dsems[id(sb.sem)] = tok
        self.lists[queue].append((waits, _freeze(thunk), tok[0], 16))
        self._mark(tok, reads, writes)

    def barrier(self):
        toks = [(self.esem[e], self.ecount[e]) for e in self.ecount if self.ecount[e] > 0]
        toks += list(self.dsems.values())
        for eng in self.ENGS:
            waits = []
            for (sem, val) in toks:
                k = id(sem)
                if eng == "pe" and self.sem_owner.get(k) == "pe":
                    continue
                if self.waited[eng].get(k, 0) >= val:
                    continue
                self.waited[eng][k] = val
                waits.append((sem, val))
            if waits:
                self.lists[eng].append((waits, None, None, 0))
        for b in self.sem_bufs:
            self.free[b.kind].append((b.sem, b.dcount))
            b.sem = None
        self.sem_bufs = []
        for e in list(self.ecount):
            if self.ecount[e] > 12000:
                ns = self.es.enter_context(self.nc.semaphore("sem_%s_%d" % (e, self.nrot)))
                self.nrot += 1
                self.esem[e] = ns
                self.ecount[e] = 0
                self.sem_owner[id(ns)] = e

    def final_wait(self, eng, bufs):
        waits = self._deps(eng, bufs, ())
        self.lists[eng].append((waits, None, None, 0))

    def emit(self):
        nc = self.nc
        lists = self.lists

        def run(engname, e):
            for waits, thunk, sem, inc in lists[engname]:
                for (s, v) in waits:
                    e.wait_ge(s, v)
                if thunk is not None:
                    ins = thunk(e)
                    ins.then_inc(sem, inc)

        with nc.Block() as block:
            @block.tensor
            def _(e):
                run("pe", e)

            @block.vector
            def _(e):
                run("dve", e)

            @block.scalar
            def _(e):
                run("act", e)

            @block.gpsimd
            def _(e):
                run("pool", e)

            @block.sync
            def _(e):
                run("sp", e)


class Ctx:
    pass


def build_program(n_layers=DEPTH, stop_after=None, debug=(), skip=()):
    nc = bass.Bass("TRN2", target_bir_lowering=False)
    es = ExitStack()
    with es:
        S = Sched(nc, es)
        g = Ctx()
        g.nc, g.S, g.es = nc, S, es

        def dram_in(name, shape, dt=F32):
            return nc.dram_tensor(name, list(shape), dt, kind="ExternalInput").ap()

        def dram_scratch(name, shape, dt=F32):
            kind = "ExternalOutput" if name in debug else "Internal"
            return nc.dram_tensor(name, list(shape), dt, kind=kind).ap()

        SB_WORDS = 53000
        big = es.enter_context(nc.sbuf_tensor("bigsb", [128, SB_WORDS], F32))
        sbtop = [0]

        def sb(name, shape, dt=F32):
            shape = list(shape)
            nel = 1
            for s_ in shape[1:]:
                nel *= s_
            esz = 4 if dt == F32 else 2
            nw = (nel * esz + 3) // 4
            nw = (nw + 7) // 8 * 8
            off = sbtop[0]
            assert off + nw <= SB_WORDS, "SBUF overflow at %s: %d + %d" % (name, off, nw)
            sbtop[0] = off + nw
            ap = big[0:shape[0], off:off + nw]
            if dt != F32:
                ap = ap.bitcast(dt)
            ap = ap[:, 0:nel]
            if len(shape) == 3:
                ap = ap.rearrange("p (a b) -> p a b", a=shape[1])
            elif len(shape) == 4:
                ap = ap.rearrange("p (a b c) -> p a b c", a=shape[1], b=shape[2])
            return ap

        class Scope:
            def __enter__(self_):
                self_.mark = sbtop[0]
                return self_

            def __exit__(self_, *a):
                S.barrier()
                sbtop[0] = self_.mark
                return False

        def ps(name, shape, dt=F32):
            return es.enter_context(nc.psum_tensor(name, list(shape), dt))

        xin = dram_in("xin", [NT, D])
        cvec = dram_in("cvec", [128, 16])
        ada_w = dram_in("ada_w", [DEPTH, D, 6 * D])
        ada_b = dram_in("ada_b", [DEPTH, 6 * D])
        w_fm = dram_in("w_fm", [DEPTH, D, FM_COLS + 128])
        w_tm = dram_in("w_tm", [DEPTH, D, TM_COLS])
        w_br = dram_in("w_br", [DEPTH, D, D])
        w_o = dram_in("w_o", [DEPTH, D, D])
        w_up = dram_in("w_up", [DEPTH, D, 2 * FH])
        w_dn = dram_in("w_dn", [DEPTH, FH, D])
        lnp = dram_in("lnp", [DEPTH, 4, D])
        ident_in = dram_in("ident", [128, 128])
        rope_cs = dram_in("rope_cs", [NT, 128])
        qkw = dram_in("qkw", [DEPTH, 384])
        natab_in = dram_in("natab", [DEPTH, 128, NA_NBLK * 256])
        hmask_in = dram_in("hmask", [4, 128, 128])
        lbp_in = dram_in("lbp", [128, 8])
        hnorm_in = dram_in("hnorm", [DEPTH, 256])
        convp_in = dram_in("convp", [DEPTH, 128, 6, 6])
        dtp_in = dram_in("dtp", [DEPTH, 8, 4])
        dsk_in = dram_in("dsk", [DEPTH, 256])
        snorm_in = dram_in("snorm", [DEPTH, 256])
        selr_in = dram_in("selr", [8, 8, 128])
        negm_in = dram_in("negm", [2, 128, 128])
        out = nc.dram_tensor("out", [NLAT, D], F32, kind="ExternalOutput").ap()

        XRES = dram_scratch("XRES", [NT, D])
        X1 = dram_scratch("X1", [NT, D])
        MODD = dram_scratch("MODD", [2, 6 * D])
        PT = dram_scratch("PT", [FM_COLS + 128, NT])
        PTOK = dram_scratch("PTOK", [NT, TM_COLS])
        BR = dram_scratch("BR", [NT, D])
        ACTT = dram_scratch("ACTT", [FH, NT], BF16)
        XS = dram_scratch("XS", [NT, 256])
        XCT = dram_scratch("XCT", [512, NT], BF16)
        BTOK = dram_scratch("BTOK", [NT, 256], BF16)
        B_XS, B_XCT, B_BTOK = Buf("XS"), Buf("XCT"), Buf("BTOK")
        OA_dr = [dram_scratch("OA0", [NT, 256]), dram_scratch("OA1", [NT, 256])]
        B_OA = Buf("OA")
        DBG = {}
        for nm, shp in (("DBG_tokq", [128, NTILE * 32]), ("DBG_cum", [8, NT]), ("DBG_ddb", [128, 8 * (NT // 64)]),
                        ("DBG_oacc", [128, NTILE * 256]), ("DBG_h1", [128, 5 * NT]), ("DBG_dd", [128, NT // 64])):
            if nm in debug:
                DBG[nm] = (nc.dram_tensor(nm, shp, F32, kind="ExternalOutput").ap(), Buf(nm))

        def dbg_dump(nm, src_ap, bsrc, col0=0, ncol=None, eng="sp"):
            if nm not in DBG:
                return
            dst, bd = DBG[nm]
            n = ncol if ncol is not None else dst.shape[1]
            S.dma(eng, lambda e: e.dma_start(out=dst[0:src_ap.shape[0], col0:col0 + n], in_=src_ap), [bsrc], [bd])
        B_xin, B_XRES, B_X1, B_MODD, B_PT, B_PTOK, B_BR, B_ACTT, B_out = (
            Buf(n) for n in ("xin", "XRES", "X1", "MODD", "PT", "PTOK", "BR", "ACTT", "out"))
        B_const = Buf("constin")

        PS = ps("PS", [128, 4096], F32)
        psum = [PS[:, i * 512:(i + 1) * 512] for i in range(8)]
        B_ps = [Buf("ps%d" % i) for i in range(8)]
        rr = [0, 0]

        def nb(lo=0, hi=8):
            rr[0] += 1
            return lo + rr[0] % (hi - lo)

        def nb2(lo=0, hi=8):
            rr[1] += 1
            return lo + 2 * (rr[1] % ((hi - lo) // 2))

        ident_f = sb("ident_f", [128, 128], F32)
        ident_b = sb("ident_b", [128, 128], BF16)
        B_identf, B_identb = Buf("identf"), Buf("identb")
        S.dma("sp", lambda e: e.dma_start(out=ident_f[:], in_=ident_in), [B_const], [B_identf])
        S.op("dve", lambda e: e.tensor_copy(out=ident_b[:], in_=ident_f[:]), [B_identf], [B_identb])
        eps_t = sb("eps_t", [128, 1], F32)
        B_eps = Buf("eps")
        S.op("pool", lambda e: e.memset(eps_t[:], EPS), [], [B_eps])

        def load_bcast(name, src_row_ap, n, bsrc):
            t = sb(name, [128, n], F32)
            b = Buf(name)
            S.dma("sp", lambda e: e.dma_start(out=t[:], in_=src_row_ap.partition_broadcast(128)), [bsrc], [b])
            return t, b

        def load_weight_bf16(name, src3, kc, ncols, stg, bstg, chunk=512):
            wt = sb(name, [128, kc, ncols], BF16)
            bw = Buf(name)
            for ci, c0 in enumerate(range(0, ncols, chunk)):
                cn = min(chunk, ncols - c0)
                st, bs = stg[ci % len(stg)], bstg[ci % len(stg)]
                stv = st[:, 0:kc * cn].rearrange("p (k n) -> p k n", k=kc)
                S.dma("sp", lambda e, stv=stv, c0=c0, cn=cn: e.dma_start(out=stv, in_=src3[:, :, c0:c0 + cn]), [B_const], [bs])
                S.op("pool", lambda e, stv=stv, c0=c0, cn=cn: e.tensor_copy(out=wt[:, :, c0:c0 + cn], in_=stv), [bs], [bw])
            return wt, bw

        def stage_mod(l):
            csb = sb("csb", [128, 16], F32)
            sil = sb("sil", [128, 16], F32)
            B_c, B_sil = Buf("c"), Buf("sil")
            S.dma("sp", lambda e: e.dma_start(out=csb[:], in_=cvec), [B_const], [B_c])
            S.op("act", lambda e: e.activation(out=sil[:], in_=csb[:], func=AF.Silu), [B_c], [B_sil])
            adab = sb("adab", [1, 6 * D], F32)
            B_adab = Buf("adab")
            S.dma("sp", lambda e: e.dma_start(out=adab[:], in_=ada_b[l:l + 1, :]), [B_const], [B_adab])
            wst = [sb("adaw%d" % i, [128, 8, 512], F32) for i in range(2)]
            B_wst = [Buf("adaw%d" % i) for i in range(2)]
            modrow = sb("modrow", [1, 2, 6 * D], F32)
            B_modrow = Buf("modrow")
            for gi in range(12):
                w = wst[gi % 2]
                bw = B_wst[gi % 2]
                src = ada_w[l].rearrange("(kc p) n -> p kc n", p=128)[:, :, gi * 512:(gi + 1) * 512]
                S.dma("sp", lambda e, w=w, src=src: e.dma_start(out=w[:], in_=src), [B_const], [bw])
                for which in range(2):
                    pb = nb()
                    for kc in range(8):
                        S.op("pe", lambda e, pb=pb, kc=kc, w=w, which=which: e.matmul(
                            psum[pb][0:1, :], lhsT=sil[:, which * 8 + kc: which * 8 + kc + 1], rhs=w[:, kc, :],
                            start=(kc == 0), stop=(kc == 7)), [B_sil, bw], [B_ps[pb]])
                    S.op("dve", lambda e, pb=pb, which=which, gi=gi: e.tensor_tensor(
                        out=modrow[0:1, which, gi * 512:(gi + 1) * 512], in0=psum[pb][0:1, :],
                        in1=adab[0:1, gi * 512:(gi + 1) * 512], op=ALU.add), [B_ps[pb], B_adab], [B_modrow])
            S.dma("sp", lambda e: e.dma_start(out=MODD.rearrange("(o a) n -> o a n", o=1), in_=modrow[:]),
                  [B_modrow], [B_MODD])

        def load_mod_pair(tag, idx, plus_one=False):
            res = []
            for which in range(2):
                t, b = load_bcast("modb_%s_%d" % (tag, which), MODD[which, idx * D:(idx + 1) * D], D, B_MODD)
                if plus_one:
                    S.op("pool", lambda e, t=t: e.tensor_scalar_add(out=t[:], in0=t[:], scalar1=1.0), [b], [b])
                res.append((t, b))
            return res

        def make_ln_scr(tag):
            return dict(st=sb(tag + "st", [128, 2, 6], F32), mv=sb(tag + "mv", [128, 2], F32),
                        rstd=sb(tag + "rs", [128, 1], F32), bst=Buf("st"), bmv=Buf("mv"), brs=Buf("rs"))

        def ln_stats(xt, bx, scr):
            st, mv, rstd = scr["st"], scr["mv"], scr["rstd"]
            bst, bmv, brs = scr["bst"], scr["bmv"], scr["brs"]
            for hlf in range(2):
                S.op("dve", lambda e, hlf=hlf: e.bn_stats(out=st[:, hlf, :], in_=xt[:, hlf * 512:(hlf + 1) * 512]),
                     [bx], [bst])
            S.op("dve", lambda e: e.bn_aggr(out=mv[:], in_=st[:]), [bst], [bmv])
            S.op("act", lambda e: e.activation(out=rstd[:], in_=mv[:, 1:2], func=AF.Sqrt, bias=eps_t[:], scale=1.0),
                 [bmv, B_eps], [brs])
            S.op("dve", lambda e: e.reciprocal(out=rstd[:], in_=rstd[:]), [brs], [brs])

        def transpose8(src_bf, bsrc, dst3, bdst, eng="act"):
            pb = nb()
            pt = psum[pb].bitcast(BF16)
            for kc in range(8):
                S.op("pe", lambda e, kc=kc, pt=pt: e.transpose(
                    out=pt[:, kc * 128:(kc + 1) * 128], in_=src_bf[:, kc * 128:(kc + 1) * 128], identity=ident_b[:]),
                    [bsrc, B_identb], [B_ps[pb]])
            if eng == "act":
                S.op("act", lambda e, pt=pt: e.copy(out=dst3, in_=pt[:, 0:1024].rearrange("p (k t) -> p k t", k=8)),
                     [B_ps[pb]], [bdst])
            else:
                S.op("dve", lambda e, pt=pt: e.tensor_copy(out=dst3, in_=pt[:, 0:1024].rearrange("p (k t) -> p k t", k=8)),
                     [B_ps[pb]], [bdst])

        def stage_ln_mod(l, src, bsrc, shift_idx, scale_idx, hT, B_hT, tiles):
            mods = load_mod_pair("sh", shift_idx)
            modsc = load_mod_pair("sc", scale_idx, plus_one=True)
            NB = 2
            xts = [sb("lnx%d" % i, [128, D], F32) for i in range(NB)]
            bxs = [Buf("lnx") for i in range(NB)]
            hts = [sb("lnh%d" % i, [128, D], BF16) for i in range(NB)]
            bhs = [Buf("lnh") for i in range(NB)]
            scr = make_ln_scr("ln")
            xn = sb("lnxn", [128, D], F32)
            bxn = Buf("xn")
            for n, tt in enumerate(tiles):
                xt, bx, ht, bh = xts[n % NB], bxs[n % NB], hts[n % NB], bhs[n % NB]
                which = 1 if tt < 2 else 0
                S.dma("sp", lambda e, xt=xt, tt=tt: e.dma_start(out=xt[:], in_=src[tt * 128:(tt + 1) * 128, :]),
                      [bsrc], [bx])
                ln_stats(xt, bx, scr)
                S.op("dve", lambda e, xt=xt: e.tensor_scalar(out=xn[:], in0=xt[:], scalar1=scr["mv"][:, 0:1],
                                                             scalar2=scr["rstd"][:], op0=ALU.subtract, op1=ALU.mult),
                     [bx, scr["bmv"], scr["brs"]], [bxn])
                S.op("pool", lambda e, which=which: e.tensor_tensor(out=xn[:], in0=xn[:], in1=modsc[which][0][:],
                                                                    op=ALU.mult), [bxn, modsc[which][1]], [bxn])
                S.op("pool", lambda e, which=which, ht=ht: e.tensor_tensor(out=ht[:], in0=xn[:], in1=mods[which][0][:],
                                                                           op=ALU.add), [bxn, mods[which][1]], [bh])
                transpose8(ht, bh, hT[:, :, tt * 128:(tt + 1) * 128], B_hT[tt])

        def stage_inproj(l, hT, B_hT):
            NB = 2
            wst = [sb("wst%d" % i, [128, 8, 512], F32) for i in range(NB)]
            bwst = [Buf("wst") for i in range(NB)]
            wbf = [sb("wbf%d" % i, [128, 8, 512], BF16) for i in range(NB)]
            bwbf = [Buf("wbf") for i in range(NB)]
            NE = 6
            ev = [sb("ipev%d" % i, [128, 512], F32) for i in range(NE)]
            bev = [Buf("ipev") for i in range(NE)]
            cnt = [0]
            for gi in range(TM_COLS // 512):
                w, bw, wb, bwb = wst[gi % NB], bwst[gi % NB], wbf[gi % NB], bwbf[gi % NB]
                src = w_tm[l].rearrange("(kc p) n -> p kc n", p=128)[:, :, gi * 512:(gi + 1) * 512]
                S.dma("sp", lambda e, w=w, src=src: e.dma_start(out=w[:], in_=src), [B_const], [bw])
                S.op("pool", lambda e, w=w, wb=wb: e.tensor_copy(out=wb[:], in_=w[:]), [bw], [bwb])
                is_gate = gi * 512 >= T_GATE
                for tt in range(NTILE):
                    i = cnt[0]
                    cnt[0] += 1
                    pb = nb()
                    for kc in range(8):
                        S.op("pe", lambda e, pb=pb, kc=kc, wb=wb, tt=tt: e.matmul(
                            psum[pb], lhsT=hT[:, kc, tt * 128:(tt + 1) * 128], rhs=wb[:, kc, :],
                            start=(kc == 0), stop=(kc == 7)), [B_hT[tt], bwb], [B_ps[pb]])
                    evt, bevt = ev[i % NE], bev[i % NE]
                    if is_gate:
                        S.op("act", lambda e, pb=pb, evt=evt: e.activation(out=evt[:], in_=psum[pb], func=AF.Sigmoid),
                             [B_ps[pb]], [bevt])
                    elif i % 2 == 0:
                        S.op("dve", lambda e, pb=pb, evt=evt: e.tensor_copy(out=evt[:], in_=psum[pb]),
                             [B_ps[pb]], [bevt])
                    else:
                        S.op("act", lambda e, pb=pb, evt=evt: e.copy(out=evt[:], in_=psum[pb]),
                             [B_ps[pb]], [bevt])
                    S.dma("pool", lambda e, evt=evt, tt=tt, gi=gi: e.dma_start(
                        out=PTOK[tt * 128:(tt + 1) * 128, gi * 512:(gi + 1) * 512], in_=evt[:]), [bevt], [B_PTOK])
            TG = [(i * 512, 512) for i in range(NT // 512)] + ([(NT // 512 * 512, NT % 512)] if NT % 512 else [])
            ngrp = FM_COLS // 512 + 1
            for gi in range(ngrp):
                ncol = 512 if gi < FM_COLS // 512 else 128
                w, bw, wb, bwb = wst[gi % NB], bwst[gi % NB], wbf[gi % NB], bwbf[gi % NB]
                src = w_fm[l].rearrange("(kc p) n -> p kc n", p=128)[:, :, gi * 512:gi * 512 + ncol]
                S.dma("sp", lambda e, w=w, src=src, ncol=ncol: e.dma_start(out=w[:, :, 0:ncol], in_=src), [B_const], [bw])
                S.op("pool", lambda e, w=w, wb=wb, ncol=ncol: e.tensor_copy(out=wb[:, :, 0:ncol], in_=w[:, :, 0:ncol]),
                     [bw], [bwb])
                for mi in range(ncol // 128):
                    for (t0, tn) in TG:
                        i = cnt[0]
                        cnt[0] += 1
                        pb = nb()
                        tts = list(range(t0 // 128, (t0 + tn) // 128))
                        for kc in range(8):
                            S.op("pe", lambda e, pb=pb, kc=kc, wb=wb, mi=mi, t0=t0, tn=tn: e.matmul(
                                psum[pb][:, 0:tn], lhsT=wb[:, kc, mi * 128:(mi + 1) * 128], rhs=hT[:, kc, t0:t0 + tn],
                                start=(kc == 0), stop=(kc == 7)), [B_hT[t] for t in tts] + [bwb], [B_ps[pb]])
                        evt, bevt = ev[i % NE], bev[i % NE]
                        if i % 2 == 0:
                            S.op("dve", lambda e, pb=pb, evt=evt, tn=tn: e.tensor_copy(out=evt[:, 0:tn], in_=psum[pb][:, 0:tn]),
                                 [B_ps[pb]], [bevt])
                        else:
                            S.op("act", lambda e, pb=pb, evt=evt, tn=tn: e.copy(out=evt[:, 0:tn], in_=psum[pb][:, 0:tn]),
                                 [B_ps[pb]], [bevt])
                        r0 = gi * 512 + mi * 128
                        S.dma("pool", lambda e, evt=evt, r0=r0, t0=t0, tn=tn: e.dma_start(
                            out=PT[r0:r0 + 128, t0:t0 + tn], in_=evt[:, 0:tn]), [bevt], [B_PT])

        def post_norm_tile(xsrc_rows, bxsrc, ypb, gate_t, gate_b, g_t, g_b, b_t, b_b, dst_rows, bdst, tiles):
            xt, bx, tt_, bt, scr = tiles["xt"], tiles["bx"], tiles["t"], tiles["bt"], tiles["scr"]
            S.dma("sp", lambda e: e.dma_start(out=xt[:], in_=xsrc_rows), [bxsrc], [bx])
            yv = PS[:, ypb * 512:(ypb + 2) * 512]
            S.op("dve", lambda e: e.tensor_tensor(out=tt_[:], in0=yv, in1=gate_t[:], op=ALU.mult),
                 [B_ps[ypb], B_ps[ypb + 1], gate_b], [bt])
            S.op("act", lambda e: e.activation(out=xt[:], in_=xt[:], func=AF.Copy, scale=ALPHA), [bx], [bx])
            S.op("dve", lambda e: e.tensor_tensor(out=tt_[:], in0=tt_[:], in1=xt[:], op=ALU.add), [bx, bt], [bt])
            ln_stats(tt_, bt, scr)
            S.op("dve", lambda e: e.tensor_scalar(out=tt_[:], in0=tt_[:], scalar1=scr["mv"][:, 0:1],
                                                  scalar2=scr["rstd"][:], op0=ALU.subtract, op1=ALU.mult),
                 [bt, scr["bmv"], scr["brs"]], [bt])
            S.op("pool", lambda e: e.tensor_tensor(out=tt_[:], in0=tt_[:], in1=g_t[:], op=ALU.mult), [bt, g_b], [bt])
            S.op("pool", lambda e: e.tensor_tensor(out=xt[:], in0=tt_[:], in1=b_t[:], op=ALU.add), [bt, b_b], [bx])
            S.dma("pool", lambda e: e.dma_start(out=dst_rows, in_=xt[:]), [bx], [bdst])

        def stage_merge(l, tiles, xsrc, bxsrc):
            stg = [sb("mstg%d" % i, [128, 8 * 512], F32) for i in range(2)]
            bstg = [Buf("mstg") for i in range(2)]
            wbr, bwbr = load_weight_bf16("wbr", w_br[l].rearrange("(kc p) n -> p kc n", p=128), 8, D, stg, bstg)
            wo, bwo = load_weight_bf16("wo", w_o[l].rearrange("(kc p) n -> p kc n", p=128), 8, D, stg, bstg)
            gate1 = load_mod_pair("g1", 2)
            lg, blg = load_bcast("ln1g", lnp[l, 0, :], D, B_const)
            lb, blb = load_bcast("ln1b", lnp[l, 1, :], D, B_const)
            NB = 2
            brt = [sb("brt%d" % i, [128, D], F32) for i in range(NB)]
            bbrt = [Buf("brt") for i in range(NB)]
            gt = [sb("gt%d" % i, [128, 4 * D], F32) for i in range(NB)]
            bgt = [Buf("gt") for i in range(NB)]
            brb = sb("brb", [128, D], BF16)
            bbrb = Buf("brb")
            brT = sb("brT", [128, 8, 128], BF16)
            bbrT = Buf("brT")
            mg = sb("mg", [128, D], F32)
            bmg = Buf("mg")
            tmp = [sb("mtmp%d" % i, [128, 512], F32) for i in range(2)]
            btmp = [Buf("mtmp") for i in range(2)]
            mb = sb("mb", [128, D], BF16)
            bmb = Buf("mb")
            mT = sb("mT", [128, 8, 128], BF16)
            bmT = Buf("mT")
            pn = dict(xt=sb("pnx", [128, D], F32), bx=Buf("pnx"), t=sb("pnt", [128, D], F32), bt=Buf("pnt"),
                      scr=make_ln_scr("pn"))
            for n, tt in enumerate(tiles):
                rows = slice(tt * 128, (tt + 1) * 128)
                which = 1 if tt < 2 else 0
                b_, bb_, g_, bg_ = brt[n % NB], bbrt[n % NB], gt[n % NB], bgt[n % NB]
                S.dma("sp", lambda e, b_=b_, rows=rows: e.dma_start(out=b_[:], in_=BR[rows, :]), [B_BR], [bb_])
                S.dma("sp", lambda e, g_=g_, rows=rows: e.dma_start(out=g_[:], in_=PTOK[rows, T_GATE:T_GATE + 4 * D]),
                      [B_PTOK], [bg_])
                S.op("pool", lambda e, b_=b_: e.tensor_copy(out=brb[:], in_=b_[:]), [bb_], [bbrb])
                transpose8(brb, bbrb, brT[:], bbrT)
                k = 0
                for nbr in range(4):
                    for cg in range(2):
                        pb = nb()
                        for kh in range(2):
                            S.op("pe", lambda e, pb=pb, nbr=nbr, kh=kh, cg=cg: e.matmul(
                                psum[pb], lhsT=brT[:, 2 * nbr + kh, :], rhs=wbr[:, 2 * nbr + kh, cg * 512:(cg + 1) * 512],
                                start=(kh == 0), stop=(kh == 1)), [bbrT, bwbr], [B_ps[pb]])
                        gsl = g_[:, nbr * D + cg * 512: nbr * D + (cg + 1) * 512]
                        if nbr == 0:
                            S.op("dve", lambda e, pb=pb, cg=cg, gsl=gsl: e.tensor_tensor(
                                out=mg[:, cg * 512:(cg + 1) * 512], in0=psum[pb], in1=gsl, op=ALU.mult),
                                [B_ps[pb], bg_], [bmg])
                        else:
                            tm_, btm_ = tmp[k % 2], btmp[k % 2]
                            k += 1
                            S.op("dve", lambda e, pb=pb, gsl=gsl, tm_=tm_: e.tensor_tensor(
                                out=tm_[:], in0=psum[pb], in1=gsl, op=ALU.mult), [B_ps[pb], bg_], [btm_])
                            S.op("pool", lambda e, cg=cg, tm_=tm_: e.tensor_tensor(
                                out=mg[:, cg * 512:(cg + 1) * 512], in0=mg[:, cg * 512:(cg + 1) * 512], in1=tm_[:],
                                op=ALU.add), [btm_, bmg], [bmg])
                S.op("act", lambda e: e.copy(out=mb[:], in_=mg[:]), [bmg], [bmb])
                transpose8(mb, bmb, mT[:], bmT, eng="dve")
                ypb = nb2()
                for cg in range(2):
                    for kc in range(8):
                        S.op("pe", lambda e, ypb=ypb, cg=cg, kc=kc: e.matmul(
                            psum[ypb + cg], lhsT=mT[:, kc, :], rhs=wo[:, kc, cg * 512:(cg + 1) * 512],
                            start=(kc == 0), stop=(kc == 7)), [bmT, bwo], [B_ps[ypb + cg]])
                post_norm_tile(xsrc[rows, :], bxsrc, ypb, gate1[which][0], gate1[which][1], lg, blg, lb, blb,
                               X1[rows, :], B_X1, pn)

        def stage_ffn_up(l, hT, B_hT, t_start):
            NB = 2
            wst = [sb("fwst%d" % i, [128, 8, 512], F32) for i in range(NB)]
            bwst = [Buf("fwst") for i in range(NB)]
            wbf = [sb("fwbf%d" % i, [128, 8, 512], BF16) for i in range(NB)]
            bwbf = [Buf("fwbf") for i in range(NB)]
            sg = [sb("fsg%d" % i, [128, 512], F32) for i in range(2)]
            bsg = [Buf("fsg") for i in range(2)]
            av = [sb("fav%d" % i, [128, 512], BF16) for i in range(3)]
            bav = [Buf("fav") for i in range(3)]
            TG = []
            t0 = t_start
            while t0 < NT:
                tn = min(512, NT - t0)
                TG.append((t0, tn))
                t0 += tn
            cnt = 0
            for gi in range(2 * FH // 512):
                w, bw, wb, bwb = wst[gi % NB], bwst[gi % NB], wbf[gi % NB], bwbf[gi % NB]
                src = w_up[l].rearrange("(kc p) n -> p kc n", p=128)[:, :, gi * 512:(gi + 1) * 512]
                S.dma("sp", lambda e, w=w, src=src: e.dma_start(out=w[:], in_=src), [B_const], [bw])
                S.op("pool", lambda e, w=w, wb=wb: e.tensor_copy(out=wb[:], in_=w[:]), [bw], [bwb])
                for mm in range(2):
                    m = gi * 2 + mm
                    for (t0, tn) in TG:
                        tts = list(range(t0 // 128, (t0 + tn) // 128))
                        pg, pu = nb(), nb()
                        for which, pb in ((0, pg), (1, pu)):
                            c0 = mm * 256 + which * 128
                            for kc in range(8):
                                S.op("pe", lambda e, pb=pb, kc=kc, wb=wb, c0=c0, t0=t0, tn=tn: e.matmul(
                                    psum[pb][:, 0:tn], lhsT=wb[:, kc, c0:c0 + 128], rhs=hT[:, kc, t0:t0 + tn],
                                    start=(kc == 0), stop=(kc == 7)), [B_hT[t] for t in tts] + [bwb], [B_ps[pb]])
                        s_, bs_ = sg[cnt % 2], bsg[cnt % 2]
                        a_, ba_ = av[cnt % 3], bav[cnt % 3]
                        cnt += 1
                        S.op("act", lambda e, pg=pg, s_=s_, tn=tn: e.activation(out=s_[:, 0:tn], in_=psum[pg][:, 0:tn],
                                                                               func=AF.Silu), [B_ps[pg]], [bs_])
                        S.op("dve", lambda e, pu=pu, s_=s_, a_=a_, tn=tn: e.tensor_tensor(
                            out=a_[:, 0:tn], in0=psum[pu][:, 0:tn], in1=s_[:, 0:tn], op=ALU.mult), [B_ps[pu], bs_], [ba_])
                        S.dma("pool", lambda e, a_=a_, m=m, t0=t0, tn=tn: e.dma_start(
                            out=ACTT[m * 128:(m + 1) * 128, t0:t0 + tn], in_=a_[:, 0:tn]), [ba_], [B_ACTT])

        def stage_ffn_down(l, tiles, last):
            KC = FH // 128
            stg = [sb("dstg%d" % i, [128, KC * 256], F32) for i in range(2)]
            bstg = [Buf("dstg") for i in range(2)]
            wd, bwd = load_weight_bf16("wd", w_dn[l].rearrange("(kc p) n -> p kc n", p=128), KC, D, stg, bstg, chunk=256)
            gate2 = load_mod_pair("g2", 5)
            lg, blg = load_bcast("ln2g", lnp[l, 2, :], D, B_const)
            lb, blb = load_bcast("ln2b", lnp[l, 3, :], D, B_const)
            at = [sb("dat%d" % i, [128, KC, 256], BF16) for i in range(2)]
            bat = [Buf("dat") for i in range(2)]
            pn = dict(xt=sb("pnx", [128, D], F32), bx=Buf("pnx"), t=sb("pnt", [128, D], F32), bt=Buf("pnt"),
                      scr=make_ln_scr("pn"))
            for n in range(0, len(tiles), 2):
                tt0 = tiles[n]
                a_, ba_ = at[(n // 2) % 2], bat[(n // 2) % 2]
                for (k0, k1) in ((0, 6), (6, 12), (12, 17), (17, 22)):
                    S.dma("sp", lambda e, a_=a_, tt0=tt0, k0=k0, k1=k1: e.dma_start(
                        out=a_[:, k0:k1, :], in_=ACTT.rearrange("(kc p) t -> p kc t", p=128)[:, k0:k1, tt0 * 128: tt0 * 128 + 256]),
                        [B_ACTT], [ba_])
                for sub in range(2):
                    tt = tt0 + sub
                    rows = slice(tt * 128, (tt + 1) * 128)
                    which = 1 if tt < 2 else 0
                    ypb = nb2()
                    for cg in range(2):
                        for kc in range(KC):
                            S.op("pe", lambda e, ypb=ypb, cg=cg, kc=kc, a_=a_, sub=sub: e.matmul(
                                psum[ypb + cg], lhsT=a_[:, kc, sub * 128:(sub + 1) * 128],
                                rhs=wd[:, kc, cg * 512:(cg + 1) * 512], start=(kc == 0), stop=(kc == KC - 1)),
                                [ba_, bwd], [B_ps[ypb + cg]])
                    if last:
                        dst, bdst = out[(tt - 2) * 128:(tt - 1) * 128, :], B_out
                    else:
                        dst, bdst = XRES[rows, :], B_XRES
                    post_norm_tile(X1[rows, :], B_X1, ypb, gate2[which][0], gate2[which][1], lg, blg, lb, blb,
                                   dst, bdst, pn)
        def stage_gqa(l, ctx_out):
            qT2 = sb("qT2", [128, 2, 2, NT], BF16)
            kT = sb("kT", [128, NT], BF16)
            vall = sb("vall", [128, NTILE, 2, 65], BF16)
            bq, bk, bv = Buf("qT2"), Buf("kT"), Buf("vall")
            S.op("pool", lambda e: e.memset(vall[:], 1.0), [], [bv])
            S.op("pool", lambda e: e.memset(qT2[:], 0.0), [], [bq])
            nw, bnw = load_bcast("qkw", qkw[l, :], 384, B_const)
            NB = 2
            xts = [sb("gx%d" % i, [128, 512], F32) for i in range(NB)]
            bxs = [Buf("gx") for i in range(NB)]
            css = [sb("gcs%d" % i, [128, 128], F32) for i in range(NB)]
            bcs = [Buf("gcs") for i in range(NB)]
            sq = sb("gsq", [128, 384], F32)
            bsq = Buf("gsq")
            ss = sb("gss", [128, 6], F32)
            bss = Buf("gss")
            xn = sb("gxn", [128, 384], F32)
            bxn = Buf("gxn")
            rot = sb("grot", [128, 384], F32)
            brot = Buf("grot")
            t1 = sb("gt1", [128, 384], F32)
            bt1 = Buf("gt1")
            qkb = sb("gqkb", [128, 384], BF16)
            bqkb = Buf("gqkb")
            for tt in range(NTILE):
                rows = slice(tt * 128, (tt + 1) * 128)
                xt, bx, cs, bc = xts[tt % NB], bxs[tt % NB], css[tt % NB], bcs[tt % NB]
                S.dma("sp", lambda e, xt=xt, rows=rows: e.dma_start(out=xt[:], in_=PTOK[rows, T_DQ:T_DQ + 512]), [B_PTOK], [bx])
                S.dma("sp", lambda e, cs=cs, rows=rows: e.dma_start(out=cs[:], in_=rope_cs[rows, :]), [B_const], [bc])
                S.op("dve", lambda e, xt=xt: e.tensor_tensor(out=sq[:], in0=xt[:, 0:384], in1=xt[:, 0:384], op=ALU.mult), [bx], [bsq])
                S.op("dve", lambda e: e.tensor_reduce(out=ss[:], in_=sq[:].rearrange("p (h d) -> p h d", d=64), axis=AX.X, op=ALU.add),
                     [bsq], [bss])
                S.op("act", lambda e: e.activation(out=ss[:], in_=ss[:], func=AF.Sqrt, bias=eps_t[:], scale=1.0 / 64), [bss, B_eps], [bss])
                S.op("dve", lambda e: e.reciprocal(out=ss[:], in_=ss[:]), [bss], [bss])
                S.op("dve", lambda e, xt=xt: e.tensor_tensor(
                    out=xn[:].rearrange("p (h d) -> p h d", d=64), in0=xt[:, 0:384].rearrange("p (h d) -> p h d", d=64),
                    in1=ss[:].unsqueeze(2).to_broadcast([128, 6, 64]), op=ALU.mult), [bx, bss], [bxn])
                S.op("pool", lambda e: e.tensor_tensor(out=xn[:], in0=xn[:], in1=nw[:], op=ALU.mult), [bxn, bnw], [bxn])
                xv = xn[:].rearrange("p (g two x) -> p g two x", two=2, x=16)
                rv = rot[:].rearrange("p (g two x) -> p g two x", two=2, x=16)
                S.op("pool", lambda e, xv=xv, rv=rv: e.tensor_copy(out=rv[:, :, 0, :], in_=xv[:, :, 1, :]), [bxn], [brot])
                S.op("pool", lambda e, xv=xv, rv=rv: e.tensor_copy(out=rv[:, :, 1, :], in_=xv[:, :, 0, :]), [bxn], [brot])
                cb = cs[:, 0:64].unsqueeze(1).to_broadcast([128, 6, 64])
                sbb = cs[:, 64:128].unsqueeze(1).to_broadcast([128, 6, 64])
                S.op("dve", lambda e, cb=cb: e.tensor_tensor(out=t1[:].rearrange("p (h d) -> p h d", d=64),
                                                             in0=xn[:].rearrange("p (h d) -> p h d", d=64), in1=cb, op=ALU.mult),
                     [bxn, bc], [bt1])
                S.op("pool", lambda e, sbb=sbb: e.tensor_tensor(out=rot[:].rearrange("p (h d) -> p h d", d=64),
                                                                in0=rot[:].rearrange("p (h d) -> p h d", d=64), in1=sbb, op=ALU.mult),
                     [brot, bc], [brot])
                S.op("dve", lambda e: e.tensor_tensor(
                    out=qkb[:, 0:256].rearrange("p (j k d) -> p j k d", j=2, k=2),
                    in0=t1[:, 0:256].rearrange("p (k j d) -> p j k d", k=2, j=2),
                    in1=rot[:, 0:256].rearrange("p (k j d) -> p j k d", k=2, j=2), op=ALU.add), [bt1, brot], [bqkb])
                S.op("dve", lambda e: e.tensor_tensor(out=qkb[:, 256:384], in0=t1[:, 256:384], in1=rot[:, 256:384], op=ALU.add),
                     [bt1, brot], [bqkb])
                S.op("act", lambda e, xt=xt, tt=tt: e.copy(out=vall[:, tt, :, 0:64],
                                                           in_=xt[:, 384:512].rearrange("p (k d) -> p k d", k=2)), [bx], [bv])
                pb = nb(4, 8)
                pt = psum[pb].bitcast(BF16)
                for c3 in range(3):
                    S.op("pe", lambda e, c3=c3, pt=pt: e.transpose(out=pt[:, c3 * 128:(c3 + 1) * 128],
                                                                   in_=qkb[:, c3 * 128:(c3 + 1) * 128], identity=ident_b[:]),
                         [bqkb, B_identb], [B_ps[pb]])
                for kh_ in range(2):
                    pr = slice(kh_ * 64, (kh_ + 1) * 64)
                    S.op("act", lambda e, pt=pt, tt=tt, kh_=kh_, pr=pr: e.copy(
                        out=qT2[pr, kh_, :, tt * 128:(tt + 1) * 128], in_=pt[pr, 0:256].rearrange("p (j t) -> p j t", j=2)),
                        [B_ps[pb]], [bq])
                S.op("act", lambda e, pt=pt, tt=tt: e.copy(out=kT[:, tt * 128:(tt + 1) * 128], in_=pt[:, 256:384]), [B_ps[pb]], [bk])
            eb = [sb("geb%d" % i, [128, 512], BF16) for i in range(3)]
            beb = [Buf("geb") for i in range(3)]
            osb = [sb("gosb%d" % i, [128, 2, 256], F32) for i in range(2)]
            bosb = [Buf("gosb") for i in range(2)]
            rc = sb("grc", [128, 1], F32)
            brc = Buf("grc")
            groups = ([(0, [0, 1])] if ctx_out else []) + [(2 + 2 * i, list(range(NTILE))) for i in range(16)]
            cnt = 0
            for gi, (tt0, keys) in enumerate(groups):
                t0 = tt0 * 128
                o_, bo_ = osb[gi % 2], bosb[gi % 2]
                for kh in range(2):
                    for ki, kt in enumerate(keys):
                        spb = nb(4, 8)
                        S.op("pe", lambda e, spb=spb, kh=kh, kt=kt, t0=t0: e.matmul(
                            psum[spb].rearrange("p (j t) -> p j t", j=2), lhsT=kT[:, kt * 128:(kt + 1) * 128],
                            rhs=qT2[:, kh, :, t0:t0 + 256], start=True, stop=True), [bq, bk], [B_ps[spb]])
                        e_, be_ = eb[cnt % 3], beb[cnt % 3]
                        cnt += 1
                        S.op("act", lambda e, spb=spb, e_=e_: e.activation(out=e_[:], in_=psum[spb], func=AF.Exp, scale=0.125),
                             [B_ps[spb]], [be_])
                        for j in range(2):
                            for sub in range(2):
                                ab = j * 2 + sub
                                S.op("pe", lambda e, ab=ab, e_=e_, j=j, sub=sub, kt=kt, kh=kh, ki=ki, nk=len(keys): e.matmul(
                                    psum[ab][:, 0:65], lhsT=e_[:, j * 256 + sub * 128: j * 256 + (sub + 1) * 128],
                                    rhs=vall[:, kt, kh, :], start=(ki == 0), stop=(ki == nk - 1)), [be_, bv], [B_ps[ab]])
                    for j in range(2):
                        for sub in range(2):
                            ab = j * 2 + sub
                            hh = 2 * kh + j
                            S.op("dve", lambda e, ab=ab: e.reciprocal(out=rc[:], in_=psum[ab][:, 64:65]), [B_ps[ab]], [brc])
                            S.op("dve", lambda e, ab=ab, o_=o_, sub=sub, hh=hh: e.tensor_scalar(
                                out=o_[:, sub, hh * 64:(hh + 1) * 64], in0=psum[ab][:, 0:64], scalar1=rc[:], scalar2=None,
                                op0=ALU.mult), [B_ps[ab], brc], [bo_])
                S.dma("pool", lambda e, o_=o_, t0=t0: e.dma_start(
                    out=BR[t0:t0 + 256, 768:1024].rearrange("(s p) c -> p s c", p=128), in_=o_[:]), [bo_], [B_BR])

        def stage_na(l, ctx_out):
            cqT = sb("cqT", [128, 4, NT], BF16)
            ckT = sb("ckT", [128, 2, NT], BF16)
            vna = sb("vna", [128, NTILE, 4, 65], BF16)
            bcq, bck, bvn = Buf("cqT"), Buf("ckT"), Buf("vna")
            S.op("pool", lambda e: e.memset(vna[:], 1.0), [], [bvn])
            S.op("pool", lambda e: e.memset(cqT[:], 0.0), [], [bcq])
            tab = sb("natab", [128, NA_NBLK, 4, 64], F32)
            btab = Buf("natab")
            S.dma("sp", lambda e: e.dma_start(out=tab[:].rearrange("p a b c -> p (a b c)"), in_=natab_in[l]), [B_const], [btab])
            stg = [sb("nstg%d" % i, [128, NT], F32) for i in range(2)]
            bstg = [Buf("nstg") for i in range(2)]
            k = 0
            for (dst, bdst, r0) in ((cqT, bcq, R_CQ), (ckT, bck, R_CK)):
                for hc in range(2):
                    st, bs = stg[k % 2], bstg[k % 2]
                    k += 1
                    S.dma("sp", lambda e, st=st, r0=r0, hc=hc: e.dma_start(out=st[:], in_=PT[r0 + hc * 128: r0 + (hc + 1) * 128, :]),
                          [B_PT], [bs])
                    if r0 == R_CQ:
                        for hh in range(2):
                            S.op("pool", lambda e, st=st, dst=dst, hc=hc, hh=hh: e.tensor_scalar_mul(
                                out=dst[hh * 64:(hh + 1) * 64, 2 * hc + hh, :], in0=st[hh * 64:(hh + 1) * 64, :], scalar1=0.125), [bs], [bdst])
                    else:
                        S.op("pool", lambda e, st=st, dst=dst, hc=hc: e.tensor_copy(out=dst[:, hc, :], in_=st[:]), [bs], [bdst])
            vst = [sb("nvst%d" % i, [128, 256], F32) for i in range(2)]
            bvst = [Buf("nvst") for i in range(2)]
            for tt in range(NTILE):
                v_, bv_ = vst[tt % 2], bvst[tt % 2]
                S.dma("sp", lambda e, v_=v_, tt=tt: e.dma_start(out=v_[:], in_=PTOK[tt * 128:(tt + 1) * 128, T_CV:T_CV + 256]),
                      [B_PTOK], [bv_])
                S.op("act", lambda e, v_=v_, tt=tt: e.copy(out=vna[:, tt, :, 0:64], in_=v_[:].rearrange("p (h d) -> p h d", h=4)),
                     [bv_], [bvn])
            sfp = [sb("nsfp%d" % i, [128, 512], F32) for i in range(2)]
            bsfp = [Buf("nsfp") for i in range(2)]
            eb = [sb("neb%d" % i, [128, 512], BF16) for i in range(3)]
            beb = [Buf("neb") for i in range(3)]
            osb = [sb("nosb%d" % i, [128, 256], F32) for i in range(2)]
            bosb = [Buf("nosb") for i in range(2)]
            rc = sb("nrc", [128, 1], F32)
            brc = Buf("nrc")
            plan = ([(0, [(0, None), (1, None)]), (1, [(0, None), (1, None)])] if ctx_out else [])
            import os as _os
            _mode = _os.environ.get("NA_MODE", "")
            for i in range(32):
                if _mode == "nobias":
                    plan.append((2 + i, [(2 + j, None) for (j, b0, b1) in NA_PLAN[i]] + [(0, None), (1, None)]))
                else:
                    plan.append((2 + i, [(2 + j, (b0, b1)) for (j, b0, b1) in NA_PLAN[i]] + [(0, None), (1, None)]))
            if _mode == "prep":
                plan = []
            if _mode == "ctxonly":
                plan = plan[:2]
            cnt = 0
            for qi, (qt, keylist) in enumerate(plan):
                q0 = qt * 128
                o_, bo_ = osb[qi % 2], bosb[qi % 2]
                for ki, (kt, blk) in enumerate(keylist):
                    spb = nb(4, 8)
                    for h in range(4):
                        hc = h // 2
                        S.op("pe", lambda e, spb=spb, h=h, hc=hc, kt=kt, q0=q0: e.matmul(
                            psum[spb][:, h * 128:(h + 1) * 128], lhsT=ckT[:, hc, kt * 128:(kt + 1) * 128],
                            rhs=cqT[:, h, q0:q0 + 128], start=True, stop=True), [bcq, bck], [B_ps[spb]])
                    e_, be_ = eb[cnt % 3], beb[cnt % 3]
                    if blk is not None:
                        s_, bs_ = sfp[cnt % 2], bsfp[cnt % 2]
                        for a in range(2):
                            for h in range(4):
                                c0 = h * 128 + a * 64
                                S.op("dve", lambda e, spb=spb, s_=s_, a=a, h=h, c0=c0, blk=blk: e.tensor_tensor(
                                    out=s_[:, c0:c0 + 64], in0=psum[spb][:, c0:c0 + 64], in1=tab[:, blk[a], h, :], op=ALU.add),
                                    [B_ps[spb], btab], [bs_])
                        S.op("act", lambda e, s_=s_, e_=e_: e.activation(out=e_[:], in_=s_[:], func=AF.Exp), [bs_], [be_])
                    else:
                        S.op("act", lambda e, spb=spb, e_=e_: e.activation(out=e_[:], in_=psum[spb], func=AF.Exp, scale=1.0),
                             [B_ps[spb]], [be_])
                    cnt += 1
                    for h in range(4):
                        S.op("pe", lambda e, h=h, e_=e_, kt=kt, ki=ki, nk=len(keylist): e.matmul(
                            psum[h][:, 0:65], lhsT=e_[:, h * 128:(h + 1) * 128], rhs=vna[:, kt, h, :],
                            start=(ki == 0), stop=(ki == nk - 1)), [be_, bvn], [B_ps[h]])
                for h in range(4):
                    S.op("dve", lambda e, h=h: e.reciprocal(out=rc[:], in_=psum[h][:, 64:65]), [B_ps[h]], [brc])
                    S.op("dve", lambda e, h=h, o_=o_: e.tensor_scalar(
                        out=o_[:, h * 64:(h + 1) * 64], in0=psum[h][:, 0:64], scalar1=rc[:], scalar2=None, op0=ALU.mult),
                        [B_ps[h], brc], [bo_])
                S.dma("pool", lambda e, o_=o_, q0=q0: e.dma_start(out=BR[q0:q0 + 128, 512:768], in_=o_[:]), [bo_], [B_BR])
        def stage_hgrn(l, ctx_out):
            NCH = NT // 64

            def v3(ap):
                return ap.rearrange("p (c x) -> p c x", x=64)

            def v32(ap):
                return ap.rearrange("p (c x) -> p c x", x=32)

            def v5(ap):
                return ap.rearrange("p (c two x) -> p c two x", two=2, x=32)

            def v4(ap):
                return ap.rearrange("p (t two x) -> p t two x", two=2, x=64)

            cm = sb("hcm", [128, 4, 128], F32)
            bcm = Buf("hcm")
            S.dma("sp", lambda e: e.dma_start(out=cm[:], in_=hmask_in.rearrange("d p t -> p d t")), [B_const], [bcm])
            lbt = sb("hlb", [128, 8], F32)
            lb = sb("hlbv", [128, 4], F32)
            oml = sb("homl", [128, 4], F32)
            blb = Buf("hlb")
            if l > 0:
                S.dma("sp", lambda e: e.dma_start(out=lbt[:], in_=lbp_in), [B_const], [blb])
                lv = lbt[:].rearrange("p (d l h) -> p d l h", d=2, l=2)
                S.op("dve", lambda e: e.tensor_tensor(out=lb[:].rearrange("p (d h) -> p d h", d=2), in0=lv[:, :, 1, :],
                                                      in1=lv[:, :, 0, :], op=ALU.subtract), [blb], [blb])
                S.op("act", lambda e: e.activation(out=lb[:], in_=lb[:], func=AF.Sigmoid), [blb], [blb])
                S.op("dve", lambda e: e.tensor_scalar(out=oml[:], in0=lb[:], scalar1=-1.0, scalar2=1.0, op0=ALU.mult, op1=ALU.add),
                     [blb], [blb])
            for dr in range(2):
                OA = OA_dr[dr]
                for hc in range(2):
                    with Scope():
                        q1 = sb("hq1", [128, 2, NT], BF16)
                        q2 = sb("hq2", [128, 2, NT], BF16)
                        k1 = sb("hk1", [128, NT], BF16)
                        k2 = sb("hk2", [128, NT], BF16)
                        qhA = sb("hqA", [128, 2, NT], BF16)
                        qhB = sb("hqB", [128, 2, NT], BF16)
                        kh = sb("hkh", [128, NT], BF16)
                        bper = Buf("hper")
                        Sbf = sb("hSbf", [128, NCH, 2, 64], BF16)
                        bSbf = Buf("hSbf")
                        dd = sb("hdd", [128, NCH], F32)
                        bdd = Buf("hdd")
                        for t_ in (q1, q2, qhA, qhB):
                            S.op("pool", lambda e, t_=t_: e.memset(t_[:], 0.0), [], [bper])
                        S.op("pool", lambda e: e.memset(k2[:], 0.0), [], [bper])
                        with Scope():
                            A = sb("hA", [128, NT], F32)
                            C = sb("hC", [128, NT], F32)
                            E = sb("hE", [128, NT], F32)
                            Q = sb("hQ", [128, NT], F32)
                            m01 = sb("hm01", [128, NT], BF16)
                            bA, bC, bE, bQ, bm01 = Buf("hA"), Buf("hC"), Buf("hE"), Buf("hQ"), Buf("hm01")
                            S.op("pool", lambda e: e.memset(m01[:], 1.0), [], [bm01])
                            S.op("pool", lambda e: e.memset(v3(m01[:])[:, :, 0:1], 0.0), [], [bm01])
                            RF = R_AFF if dr == 0 else R_AFB
                            S.dma("sp", lambda e: e.dma_start(out=A[:], in_=PT[RF + hc * 128:RF + (hc + 1) * 128, :]), [B_PT], [bA])
                            S.op("act", lambda e: e.activation(out=A[:], in_=A[:], func=AF.Sigmoid), [bA], [bA])
                            if l > 0:
                                ci = dr * 2 + hc
                                S.op("dve", lambda e: e.tensor_scalar(out=A[:], in0=A[:], scalar1=oml[:, ci:ci + 1],
                                                                      scalar2=lb[:, ci:ci + 1], op0=ALU.mult, op1=ALU.add),
                                     [bA, blb], [bA])
                            S.op("act", lambda e: e.activation(out=E[:], in_=A[:], func=AF.Ln), [bA], [bE])
                            S.op("dve", lambda e: e.tensor_scalar(out=A[:], in0=A[:], scalar1=-1.0, scalar2=1.0,
                                                                  op0=ALU.mult, op1=ALU.add), [bA, bE], [bA])
                            S.op("dve", lambda e: e.tensor_tensor_scan(out=C[:], data0=m01[:], data1=E[:], initial=0.0,
                                                                       op0=ALU.mult, op1=ALU.add), [bm01, bE], [bC])
                            if dr == 1:
                                S.op("dve", lambda e: e.tensor_tensor(out=v3(Q[:]), in0=v3(C[:])[:, :, 63:64].to_broadcast([128, NCH, 64]),
                                                                      in1=v3(C[:]), op=ALU.subtract), [bC], [bQ])
                                S.op("dve", lambda e: e.tensor_tensor(out=C[:], in0=Q[:], in1=E[:], op=ALU.add), [bQ, bE], [bC])
                                ti, bi, kv, qv = 0, 32, 1, 0
                            else:
                                ti, bi, kv, qv = 63, 31, 0, 1
                            S.op("act", lambda e: e.activation(out=dd[:], in_=v3(C[:])[:, :, ti], func=AF.Exp), [bC], [bdd])
                            S.op("dve", lambda e: e.tensor_tensor(out=v3(E[:]), in0=v3(C[:])[:, :, ti:ti + 1].to_broadcast([128, NCH, 64]),
                                                                  in1=v3(C[:]), op=ALU.subtract), [bC, bE], [bE])
                            S.op("act", lambda e: e.activation(out=E[:], in_=E[:], func=AF.Exp), [bE], [bE])
                            S.op("dve", lambda e: e.tensor_tensor(out=kh[:], in0=A[:], in1=E[:], op=ALU.mult), [bA, bE], [bper])
                            S.op("dve", lambda e: e.tensor_tensor(out=v32(E[:]), in0=v32(C[:]),
                                                                  in1=v32(C[:])[:, :, 16:17].to_broadcast([128, 2 * NCH, 32]),
                                                                  op=ALU.subtract), [bC, bE, bper], [bE])
                            S.op("act", lambda e: e.activation(out=Q[:], in_=E[:], func=AF.Exp, scale=-1.0), [bE, bQ], [bQ])
                            S.op("dve", lambda e: e.tensor_tensor(out=k1[:], in0=A[:], in1=Q[:], op=ALU.mult), [bA, bQ], [bper])
                            S.op("dve", lambda e: e.tensor_tensor(out=v3(Q[:]), in0=v3(C[:])[:, :, bi:bi + 1].to_broadcast([128, NCH, 64]),
                                                                  in1=v3(C[:]), op=ALU.subtract), [bC, bQ, bper], [bQ])
                            S.op("act", lambda e: e.activation(out=Q[:], in_=Q[:], func=AF.Exp), [bQ], [bQ])
                            S.op("dve", lambda e: e.tensor_tensor(out=v5(k2[:])[:, :, kv, :], in0=v5(A[:])[:, :, kv, :],
                                                                  in1=v5(Q[:])[:, :, kv, :], op=ALU.mult), [bA, bQ], [bper])
                            S.dma("sp", lambda e: e.dma_start(out=Q[:], in_=PT[R_AQ + hc * 128:R_AQ + (hc + 1) * 128, :]), [B_PT, bQ, bper], [bQ])
                            S.op("act", lambda e: e.activation(out=A[:], in_=E[:], func=AF.Exp), [bE, bA, bper], [bA])
                            for hh in range(2):
                                pr = slice(hh * 64, (hh + 1) * 64)
                                S.op("dve", lambda e, pr=pr, hh=hh: e.tensor_tensor(out=q1[pr, hh, :], in0=Q[pr, :], in1=A[pr, :], op=ALU.mult),
                                     [bQ, bA], [bper])
                            S.op("dve", lambda e: e.tensor_tensor(out=v3(E[:]), in0=v3(C[:]),
                                                                  in1=v3(C[:])[:, :, bi:bi + 1].to_broadcast([128, NCH, 64]),
                                                                  op=ALU.subtract), [bC, bE, bper, bA], [bE])
                            S.op("act", lambda e: e.activation(out=A[:], in_=E[:], func=AF.Exp), [bE, bA, bper], [bA])
                            for hh in range(2):
                                pr = slice(hh * 64, (hh + 1) * 64)
                                S.op("dve", lambda e, pr=pr, hh=hh: e.tensor_tensor(
                                    out=v5(q2[pr, hh, :])[:, :, qv, :], in0=v5(Q[pr, :])[:, :, qv, :], in1=v5(A[pr, :])[:, :, qv, :],
                                    op=ALU.mult), [bQ, bA], [bper])
                            S.op("act", lambda e: e.activation(out=E[:], in_=C[:], func=AF.Exp), [bC, bE, bper], [bE])
                            for hh in range(2):
                                pr = slice(hh * 64, (hh + 1) * 64)
                                S.op("dve", lambda e, pr=pr, hh=hh: e.tensor_tensor(
                                    out=v4(qhA[pr, hh, :])[:, :, 0, :], in0=v4(Q[pr, :])[:, :, 0, :], in1=v4(E[pr, :])[:, :, 0, :],
                                    op=ALU.mult), [bQ, bE], [bper])
                                S.op("dve", lambda e, pr=pr, hh=hh: e.tensor_tensor(
                                    out=v4(qhB[pr, hh, :])[:, :, 1, :], in0=v4(Q[pr, :])[:, :, 1, :], in1=v4(E[pr, :])[:, :, 1, :],
                                    op=ALU.mult), [bQ, bE], [bper])
                        with Scope():
                            vall = sb("hv", [128, NTILE, 128], BF16)
                            vz = sb("hvz", [128, NTILE, 2, 128], BF16)
                            bv = Buf("hv")
                            S.op("pool", lambda e: e.memset(vz[:], 0.0), [], [bv])
                            vst = [sb("hvst%d" % i, [128, 128], F32) for i in range(2)]
                            bvst = [Buf("hvst") for i in range(2)]
                            for tt in range(NTILE):
                                v_, bv_ = vst[tt % 2], bvst[tt % 2]
                                S.dma("sp", lambda e, v_=v_, tt=tt: e.dma_start(
                                    out=v_[:], in_=PTOK[tt * 128:(tt + 1) * 128, T_AV + hc * 128:T_AV + (hc + 1) * 128]), [B_PTOK], [bv_])
                                S.op("pool", lambda e, v_=v_, tt=tt: e.tensor_copy(out=vall[:, tt, :], in_=v_[:]), [bv_], [bv])
                                for half in range(2):
                                    pr = slice(half * 64, (half + 1) * 64)
                                    S.op("pool", lambda e, v_=v_, tt=tt, half=half, pr=pr: e.tensor_copy(out=vz[pr, tt, half, :], in_=v_[pr, :]),
                                         [bv_], [bv])
                            Sfs = [sb("hSf%d" % i, [128, 128], F32) for i in range(2)]
                            bSfs = [Buf("hSf%d" % i) for i in range(2)]
                            S.op("pool", lambda e: e.memset(Sfs[0][:], 0.0), [], [bSfs[0]])
                            step = [0]
                            khtok = [sb("hktok%d" % i, [128, 128], BF16) for i in range(2)]
                            bkhtok = [Buf("hktok") for i in range(2)]
                            order = list(range(NTILE)) if dr == 0 else [1, 0] + list(range(NTILE - 1, 1, -1))
                            for n, tt in enumerate(order):
                                pb = nb(4, 8)
                                pt = psum[pb].bitcast(BF16)
                                kk_, bkk_ = khtok[n % 2], bkhtok[n % 2]
                                S.op("pe", lambda e, pt=pt, tt=tt: e.transpose(out=pt[:, 0:128], in_=kh[:, tt * 128:(tt + 1) * 128],
                                                                               identity=ident_b[:]), [bper, B_identb], [B_ps[pb]])
                                S.op("act", lambda e, pt=pt, kk_=kk_: e.copy(out=kk_[:], in_=pt[:, 0:128]), [B_ps[pb]], [bkk_])
                                for half in ((0, 1) if dr == 0 else (1, 0)):
                                    c = tt * 2 + half
                                    Sa, bSa = Sfs[step[0] % 2], bSfs[step[0] % 2]
                                    Sb_, bSb = Sfs[(step[0] + 1) % 2], bSfs[(step[0] + 1) % 2]
                                    step[0] += 1
                                    S.op("pool", lambda e, c=c, Sa=Sa: e.tensor_copy(out=Sbf[:, c, :, :].rearrange("p a b -> p (a b)"), in_=Sa[:]),
                                         [bSa], [bSbf])
                                    pu = nb(4, 8)
                                    for hh in range(2):
                                        S.op("pe", lambda e, pu=pu, hh=hh, half=half, kk_=kk_, tt=tt: e.matmul(
                                            psum[pu][:, hh * 64:(hh + 1) * 64], lhsT=kk_[:],
                                            rhs=vz[:, tt, half, hh * 64:(hh + 1) * 64], start=True, stop=True),
                                            [bkk_, bv], [B_ps[pu]])
                                    S.op("dve", lambda e, c=c, Sa=Sa, Sb_=Sb_: e.tensor_scalar_mul(out=Sb_[:], in0=Sa[:], scalar1=dd[:, c:c + 1]),
                                         [bSa, bdd], [bSb])
                                    S.op("dve", lambda e, pu=pu, Sb_=Sb_: e.tensor_tensor(out=Sb_[:], in0=Sb_[:], in1=psum[pu][:, 0:128], op=ALU.add),
                                         [bSb, B_ps[pu]], [bSb])
                            am = [sb("ham%d" % i, [128, 128], BF16) for i in range(3)]
                            bam = [Buf("ham") for i in range(3)]
                            t1s = [sb("ht1%d" % i, [128, 128], F32) for i in range(2)]
                            bt1s = [Buf("ht1") for i in range(2)]
                            t2s = [sb("ht2%d" % i, [128, 128], F32) for i in range(2)]
                            bt2s = [Buf("ht2") for i in range(2)]
                            ots = [sb("hot%d" % i, [128, 128], F32) for i in range(2)]
                            bots = [Buf("hot") for i in range(2)]
                            k = 0
                            for n, tt in enumerate(range(NTILE) if ctx_out else range(2, NTILE)):
                                tl = slice(tt * 128, (tt + 1) * 128)
                                ot, bot = ots[n % 2], bots[n % 2]
                                for hh in range(2):
                                    po = nb(0, 4)
                                    pa1 = nb(4, 8)
                                    pa2 = nb(4, 8)
                                    S.op("pe", lambda e, pa1=pa1, hh=hh, tl=tl: e.matmul(
                                        psum[pa1][:, 0:128], lhsT=k1[:, tl], rhs=q1[:, hh, tl], start=True, stop=True), [bper], [B_ps[pa1]])
                                    S.op("pe", lambda e, pa2=pa2, hh=hh, tl=tl: e.matmul(
                                        psum[pa2][:, 0:128], lhsT=k2[:, tl], rhs=q2[:, hh, tl], start=True, stop=True), [bper], [B_ps[pa2]])
                                    a_, ba_ = am[k % 3], bam[k % 3]
                                    t1, bt1, t2, bt2 = t1s[k % 2], bt1s[k % 2], t2s[k % 2], bt2s[k % 2]
                                    k += 1
                                    S.op("dve", lambda e, pa1=pa1, t1=t1: e.tensor_tensor(out=t1[:], in0=psum[pa1][:, 0:128], in1=cm[:, dr, :],
                                                                                         op=ALU.mult), [B_ps[pa1], bcm], [bt1])
                                    S.op("dve", lambda e, pa2=pa2, t2=t2: e.tensor_tensor(out=t2[:], in0=psum[pa2][:, 0:128], in1=cm[:, 2 + dr, :],
                                                                                         op=ALU.mult), [B_ps[pa2], bcm], [bt2])
                                    S.op("pool", lambda e, a_=a_, t1=t1, t2=t2: e.tensor_tensor(out=a_[:], in0=t1[:], in1=t2[:], op=ALU.add),
                                         [bt1, bt2], [ba_])
                                    oc = psum[po][:, 0:64]
                                    S.op("pe", lambda e, oc=oc, a_=a_, tt=tt, hh=hh: e.matmul(
                                        oc, lhsT=a_[:], rhs=vall[:, tt, hh * 64:(hh + 1) * 64], start=True, stop=False), [ba_, bv], [B_ps[po]])
                                    S.op("pe", lambda e, oc=oc, tl=tl, tt=tt, hh=hh: e.matmul(
                                        oc, lhsT=qhA[:, hh, tl], rhs=Sbf[:, 2 * tt, hh, :], start=False, stop=False),
                                        [bper, bSbf], [B_ps[po]])
                                    S.op("pe", lambda e, oc=oc, tl=tl, tt=tt, hh=hh: e.matmul(
                                        oc, lhsT=qhB[:, hh, tl], rhs=Sbf[:, 2 * tt + 1, hh, :], start=False, stop=True),
                                        [bper, bSbf], [B_ps[po]])
                                    S.op("act", lambda e, po=po, ot=ot, hh=hh: e.copy(out=ot[:, hh * 64:(hh + 1) * 64], in_=psum[po][:, 0:64]),
                                         [B_ps[po]], [bot])
                                S.dma("pool", lambda e, ot=ot, tl=tl: e.dma_start(out=OA[tl, hc * 128:(hc + 1) * 128], in_=ot[:]), [bot], [B_OA])
            nw, bnw = load_bcast("hnw", hnorm_in[l, :], 256, B_const)
            gts = [sb("hg%d" % i, [128, 256], F32) for i in range(2)]
            bgts = [Buf("hg") for i in range(2)]
            oas = [sb("hoa%d" % i, [128, 2, 256], F32) for i in range(2)]
            boas = [Buf("hoa") for i in range(2)]
            sq = sb("hsq", [128, 256], F32)
            bsq = Buf("hsq")
            ss = sb("hss", [128, 4], F32)
            bss = Buf("hss")
            ob = [sb("hob%d" % i, [128, 256], F32) for i in range(2)]
            bob = [Buf("hob") for i in range(2)]
            for n, tt in enumerate(range(NTILE) if ctx_out else range(2, NTILE)):
                rows = slice(tt * 128, (tt + 1) * 128)
                g_, bg_ = gts[n % 2], bgts[n % 2]
                o_, bo_ = ob[n % 2], bob[n % 2]
                oa2, boa2 = oas[n % 2], boas[n % 2]
                for dr in range(2):
                    S.dma("sp", lambda e, oa2=oa2, dr=dr, rows=rows: e.dma_start(out=oa2[:, dr, :], in_=OA_dr[dr][rows, :]), [B_OA], [boa2])
                oa = oa2[:, 0, :]
                S.op("pool", lambda e, oa2=oa2: e.tensor_tensor(out=oa2[:, 0, :], in0=oa2[:, 0, :], in1=oa2[:, 1, :], op=ALU.add), [boa2], [boa2])
                S.dma("sp", lambda e, g_=g_, rows=rows: e.dma_start(out=g_[:], in_=PTOK[rows, T_AG:T_AG + 256]), [B_PTOK], [bg_])
                S.op("act", lambda e, g_=g_: e.activation(out=g_[:], in_=g_[:], func=AF.Silu), [bg_], [bg_])
                S.op("dve", lambda e, oa=oa: e.tensor_tensor(out=sq[:], in0=oa, in1=oa, op=ALU.mult), [boa2], [bsq])
                S.op("dve", lambda e: e.tensor_reduce(out=ss[:], in_=sq[:].rearrange("p (h d) -> p h d", d=64), axis=AX.X, op=ALU.add),
                     [bsq], [bss])
                S.op("act", lambda e: e.activation(out=ss[:], in_=ss[:], func=AF.Sqrt, bias=eps_t[:], scale=1.0 / 64), [bss, B_eps], [bss])
                S.op("dve", lambda e: e.reciprocal(out=ss[:], in_=ss[:]), [bss], [bss])
                S.op("dve", lambda e, oa=oa, o_=o_: e.tensor_tensor(
                    out=o_[:].rearrange("p (h d) -> p h d", d=64), in0=oa.rearrange("p (h d) -> p h d", d=64),
                    in1=ss[:].unsqueeze(2).to_broadcast([128, 4, 64]), op=ALU.mult), [boa2, bss], [bo_])
                S.op("pool", lambda e, o_=o_: e.tensor_tensor(out=o_[:], in0=o_[:], in1=nw[:], op=ALU.mult), [bo_, bnw], [bo_])
                S.op("pool", lambda e, o_=o_, g_=g_: e.tensor_tensor(out=o_[:], in0=o_[:], in1=g_[:], op=ALU.mult), [bo_, bg_], [bo_])
                S.dma("pool", lambda e, o_=o_, rows=rows: e.dma_start(out=BR[rows, 0:256], in_=o_[:]), [bo_], [B_BR])
        def stage_ssd(l, ctx_out):
            NCH = NT // 64

            def v3(ap):
                return ap.rearrange("p (c x) -> p c x", x=64)

            with Scope():
                cp = sb("cp", [128, 6, 6], F32)
                bcp = Buf("cp")
                S.dma("sp", lambda e: e.dma_start(out=cp[:], in_=convp_in[l]), [B_const], [bcp])
                Xs = [sb("cx%d" % i, [128, NT], F32) for i in range(2)]
                bXs = [Buf("cx") for i in range(2)]
                Ys = [sb("cy%d" % i, [128, NT], F32) for i in range(2)]
                bYs = [Buf("cy") for i in range(2)]
                Yb = [sb("cyb%d" % i, [128, NT], BF16) for i in range(2)]
                bYb = [Buf("cyb") for i in range(2)]
                st32 = [sb("cst%d" % i, [128, 128], F32) for i in range(3)]
                bst32 = [Buf("cst") for i in range(3)]
                st16 = [sb("cstb%d" % i, [128, 128], BF16) for i in range(3)]
                bst16 = [Buf("cstb") for i in range(3)]
                ctmp = sb("ctmp", [128, NT], F32)
                bctmp = Buf("ctmp")
                k3 = 0
                for fc in range(6):
                    X, bX, Y, bY = Xs[fc % 2], bXs[fc % 2], Ys[fc % 2], bYs[fc % 2]
                    S.dma("sp", lambda e, X=X, fc=fc: e.dma_start(out=X[:], in_=PT[R_XBC + fc * 128:R_XBC + (fc + 1) * 128, :]), [B_PT], [bX])
                    S.op("dve", lambda e, X=X, Y=Y, fc=fc: e.tensor_scalar(out=Y[:], in0=X[:], scalar1=cp[:, fc, 2:3], scalar2=cp[:, fc, 5:6],
                                                                           op0=ALU.mult, op1=ALU.add), [bX, bcp], [bY])
                    for (lo, hi) in ((0, NCTX), (NCTX, NT)):
                        for kk in (0, 1, 3, 4):
                            dlt = kk - 2
                            a, b_ = lo + max(0, -dlt), hi - max(0, dlt)
                            S.op("act", lambda e, X=X, fc=fc, kk=kk, a=a, b_=b_, dlt=dlt: e.activation(
                                out=ctmp[:, a:b_], in_=X[:, a + dlt:b_ + dlt], func=AF.Copy, scale=cp[:, fc, kk:kk + 1]), [bX, bcp], [bctmp])
                            S.op("dve", lambda e, Y=Y, a=a, b_=b_: e.tensor_tensor(
                                out=Y[:, a:b_], in0=Y[:, a:b_], in1=ctmp[:, a:b_], op=ALU.add), [bY, bctmp], [bY])
                    S.op("act", lambda e, Y=Y: e.activation(out=Y[:], in_=Y[:], func=AF.Silu), [bY], [bY])
                    if fc < 2:
                        for tt in range(NTILE):
                            pb = nb()
                            s_, bs_ = st32[k3 % 3], bst32[k3 % 3]
                            k3 += 1
                            S.op("pe", lambda e, pb=pb, Y=Y, tt=tt: e.transpose(out=psum[pb][:, 0:128], in_=Y[:, tt * 128:(tt + 1) * 128],
                                                                               identity=ident_f[:]), [bY, B_identf], [B_ps[pb]])
                            if k3 % 2:
                                S.op("act", lambda e, pb=pb, s_=s_: e.copy(out=s_[:], in_=psum[pb][:, 0:128]), [B_ps[pb]], [bs_])
                            else:
                                S.op("dve", lambda e, pb=pb, s_=s_: e.tensor_copy(out=s_[:], in_=psum[pb][:, 0:128]), [B_ps[pb]], [bs_])
                            S.dma("pool", lambda e, s_=s_, tt=tt, fc=fc: e.dma_start(out=XS[tt * 128:(tt + 1) * 128, fc * 128:(fc + 1) * 128],
                                                                                    in_=s_[:]), [bs_], [B_XS])
                    else:
                        yb, byb = Yb[fc % 2], bYb[fc % 2]
                        S.op("pool", lambda e, Y=Y, yb=yb: e.tensor_copy(out=yb[:], in_=Y[:]), [bY], [byb])
                        S.dma("pool", lambda e, yb=yb, fc=fc: e.dma_start(out=XCT[(fc - 2) * 128:(fc - 1) * 128, :], in_=yb[:]), [byb], [B_XCT])
                        if fc < 4:
                            for tt in range(NTILE):
                                pb = nb()
                                pt = psum[pb].bitcast(BF16)
                                s_, bs_ = st16[k3 % 3], bst16[k3 % 3]
                                k3 += 1
                                S.op("pe", lambda e, pt=pt, yb=yb, tt=tt: e.transpose(out=pt[:, 0:128], in_=yb[:, tt * 128:(tt + 1) * 128],
                                                                                     identity=ident_b[:]), [byb, B_identb], [B_ps[pb]])
                                S.op("act", lambda e, pt=pt, s_=s_: e.copy(out=s_[:], in_=pt[:, 0:128]), [B_ps[pb]], [bs_])
                                S.dma("pool", lambda e, s_=s_, tt=tt, fc=fc: e.dma_start(
                                    out=BTOK[tt * 128:(tt + 1) * 128, (fc - 2) * 128:(fc - 1) * 128], in_=s_[:]), [bs_], [B_BTOK])
            tokq = sb("tokq", [128, NTILE, 4, 8], F32)
            btokq = Buf("tokq")
            cumrow = sb("cumrow", [8, NT], F32)
            bcum = Buf("cumrow")
            ddb = sb("ddb", [128, 8, NCH], F32)
            bddb = Buf("ddb")
            selr = sb("selr", [8, 8, 128], F32)
            bselr = Buf("selr")
            S.dma("sp", lambda e: e.dma_start(out=selr[:], in_=selr_in), [B_const], [bselr])
            Sbf = [sb("sSbf%d" % i, [128, NCH, 256], BF16) for i in range(2)]
            bSbf = [Buf("sSbf") for i in range(2)]
            with Scope():
                dtp = sb("dtp", [8, 4], F32)
                bdtp = Buf("dtp")
                S.dma("sp", lambda e: e.dma_start(out=dtp[:], in_=dtp_in[l]), [B_const], [bdtp])
                negA = sb("negA", [8, 1], F32)
                S.op("act", lambda e: e.activation(out=negA[:], in_=dtp[:, 1:2], func=AF.Exp), [bdtp], [bdtp])
                S.op("dve", lambda e: e.tensor_scalar_mul(out=negA[:], in0=negA[:], scalar1=-1.0), [bdtp], [bdtp])
                m01 = sb("sm01", [8, NT], BF16)
                bm01 = Buf("sm01")
                S.op("pool", lambda e: e.memset(m01[:], 1.0), [], [bm01])
                S.op("pool", lambda e: e.memset(v3(m01[:])[:, :, 0:1], 0.0), [], [bm01])
                T0, T1, T2, T3 = (sb("sT%d" % i, [8, NT], F32) for i in range(4))
                b0, b1, b2, b3 = Buf("sT0"), Buf("sT1"), Buf("sT2"), Buf("sT3")
                dch = sb("dch", [8, NCH], F32)
                bdch = Buf("dch")
                S.dma("sp", lambda e: e.dma_start(out=T0[:], in_=PT[R_DT:R_DT + 8, :]), [B_PT], [b0])
                S.op("dve", lambda e: e.tensor_scalar_add(out=T0[:], in0=T0[:], scalar1=dtp[:, 0:1]), [b0, bdtp], [b0])
                S.op("act", lambda e: e.activation(out=T1[:], in_=T0[:], func=AF.Abs), [b0], [b1])
                S.op("act", lambda e: e.activation(out=T1[:], in_=T1[:], func=AF.Exp, scale=-1.0), [b1], [b1])
                S.op("act", lambda e: e.activation(out=T1[:], in_=T1[:], func=AF.Ln, bias=1.0, scale=1.0), [b1], [b1])
                S.op("dve", lambda e: e.tensor_scalar_max(out=T0[:], in0=T0[:], scalar1=0.0), [b0], [b0])
                S.op("dve", lambda e: e.tensor_tensor(out=T0[:], in0=T0[:], in1=T1[:], op=ALU.add), [b0, b1], [b0])
                S.op("dve", lambda e: e.tensor_scalar_mul(out=T1[:], in0=T0[:], scalar1=negA[:, 0:1]), [b0, bdtp], [b1])
                S.op("dve", lambda e: e.tensor_tensor_scan(out=T2[:], data0=m01[:], data1=T1[:], initial=0.0, op0=ALU.mult, op1=ALU.add),
                     [bm01, b1], [b2])
                totb = v3(T2[:])[:, :, 63:64].to_broadcast([8, NCH, 64])
                S.op("dve", lambda e: e.tensor_tensor(out=v3(T3[:]), in0=totb, in1=v3(T2[:]), op=ALU.subtract), [b2], [b3])
                S.op("dve", lambda e: e.tensor_tensor(out=T3[:], in0=T3[:], in1=T1[:], op=ALU.add), [b3, b1], [b3])
                S.op("dve", lambda e: e.tensor_scalar_mul(out=cumrow[:], in0=T2[:], scalar1=dtp[:, 2:3]), [b2, bdtp], [bcum])
                S.op("dve", lambda e: e.tensor_scalar_mul(out=T3[:], in0=T3[:], scalar1=dtp[:, 3:4]), [b3, bdtp], [b3])
                S.op("dve", lambda e: e.tensor_tensor(out=cumrow[:], in0=cumrow[:], in1=T3[:], op=ALU.add), [b3, bcum], [bcum])
                S.op("act", lambda e: e.activation(out=dch[:], in_=v3(T2[:])[:, :, 63], func=AF.Exp), [b2], [bdch])
                S.op("act", lambda e: e.activation(out=T1[:], in_=cumrow[:], func=AF.Exp), [bcum, b1], [b1])
                S.op("dve", lambda e: e.tensor_tensor(out=v3(T3[:]), in0=totb, in1=v3(cumrow[:]), op=ALU.subtract), [b2, bcum, b3], [b3])
                S.op("act", lambda e: e.activation(out=T3[:], in_=T3[:], func=AF.Exp), [b3], [b3])
                S.op("dve", lambda e: e.tensor_tensor(out=T3[:], in0=T3[:], in1=T0[:], op=ALU.mult), [b3, b0], [b3])
                S.op("dve", lambda e: e.tensor_scalar_mul(out=T2[:], in0=cumrow[:], scalar1=-1.0), [bcum, b2, b3], [b2])
                for r in range(8):
                    pb = nb()
                    S.op("pe", lambda e, pb=pb, r=r: e.matmul(psum[pb][:, 0:NCH], lhsT=selr[:, r, :], rhs=dch[:], start=True, stop=True),
                         [bselr, bdch], [B_ps[pb]])
                    S.op("act", lambda e, pb=pb, r=r: e.copy(out=ddb[:, r, :], in_=psum[pb][:, 0:NCH]), [B_ps[pb]], [bddb])
                for tt in range(NTILE):
                    pb = nb()
                    for qi, (T, bT) in enumerate(((T0, b0), (T2, b2), (T1, b1), (T3, b3))):
                        S.op("pe", lambda e, pb=pb, qi=qi, T=T, tt=tt: e.transpose(
                            out=psum[pb][:, qi * 8:(qi + 1) * 8], in_=T[0:8, tt * 128:(tt + 1) * 128], identity=ident_f[0:8, 0:8]),
                            [bT, B_identf], [B_ps[pb]])
                    S.op("dve", lambda e, pb=pb, tt=tt: e.tensor_copy(out=tokq[:, tt, :, :].rearrange("p a b -> p (a b)"),
                                                                      in_=psum[pb][:, 0:32]), [B_ps[pb]], [btokq])
            dbg_dump("DBG_tokq", tokq[:].rearrange("p a b c -> p (a b c)"), btokq)
            dbg_dump("DBG_cum", cumrow[:], bcum)
            dbg_dump("DBG_ddb", ddb[:].rearrange("p a b -> p (a b)"), bddb)
            for dr in range(2):
                with Scope():
                    Sfs = [sb("sSf%d" % i, [128, 256], F32) for i in range(2)]
                    bSfs = [Buf("sSf%d" % i) for i in range(2)]
                    S.op("pool", lambda e: e.memset(Sfs[0][:], 0.0), [], [bSfs[0]])
                    step = [0]
                    xst = [sb("sxs%d" % i, [128, 256], F32) for i in range(2)]
                    bxst = [Buf("sxs") for i in range(2)]
                    btt = [sb("sbt%d" % i, [128, 256], BF16) for i in range(2)]
                    bbtt = [Buf("sbt") for i in range(2)]
                    vw = [sb("svw%d" % i, [128, 2, 256], BF16) for i in range(2)]
                    bvw = [Buf("svw") for i in range(2)]
                    for i in range(2):
                        S.op("pool", lambda e, i=i: e.memset(vw[i][:], 0.0), [], [bvw[i]])
                    order = list(range(NTILE)) if dr == 0 else [1, 0] + list(range(NTILE - 1, 1, -1))
                    for n, tt in enumerate(order):
                        rows = slice(tt * 128, (tt + 1) * 128)
                        x_, bx_, t_, bt_, w_, bw_ = xst[n % 2], bxst[n % 2], btt[n % 2], bbtt[n % 2], vw[n % 2], bvw[n % 2]
                        S.dma("sp", lambda e, x_=x_, rows=rows: e.dma_start(out=x_[:], in_=XS[rows, :]), [B_XS], [bx_])
                        S.dma("sp", lambda e, t_=t_, rows=rows: e.dma_start(out=t_[:], in_=BTOK[rows, :]), [B_BTOK], [bt_])
                        for half in range(2):
                            pr = slice(half * 64, (half + 1) * 64)
                            S.op("dve", lambda e, x_=x_, w_=w_, tt=tt, half=half, pr=pr: e.tensor_tensor(
                                out=w_[pr, half, :].rearrange("p (h d) -> p h d", d=64), in0=x_[pr, :].rearrange("p (h d) -> p h d", d=64),
                                in1=tokq[pr, tt, 3, dr * 4:(dr + 1) * 4].unsqueeze(2).to_broadcast([64, 4, 64]), op=ALU.mult),
                                [bx_, btokq], [bw_])
                        for half in ((0, 1) if dr == 0 else (1, 0)):
                            c = tt * 2 + half
                            Sa, bSa = Sfs[step[0] % 2], bSfs[step[0] % 2]
                            Sb_, bSb = Sfs[(step[0] + 1) % 2], bSfs[(step[0] + 1) % 2]
                            step[0] += 1
                            S.op("pool", lambda e, c=c, Sa=Sa: e.tensor_copy(out=Sbf[dr][:, c, :], in_=Sa[:]), [bSa], [bSbf[dr]])
                            pu = nb()
                            for h in range(4):
                                gq = h // 2
                                S.op("pe", lambda e, pu=pu, h=h, gq=gq, half=half, t_=t_, w_=w_: e.matmul(
                                    psum[pu][:, h * 64:(h + 1) * 64], lhsT=t_[:, gq * 128:(gq + 1) * 128],
                                    rhs=w_[:, half, h * 64:(h + 1) * 64], start=True, stop=True),
                                    [bt_, bw_], [B_ps[pu]])
                            S.op("dve", lambda e, c=c, Sa=Sa, Sb_=Sb_: e.tensor_tensor(
                                out=Sb_[:].rearrange("p (h d) -> p h d", d=64), in0=Sa[:].rearrange("p (h d) -> p h d", d=64),
                                in1=ddb[:, dr * 4:(dr + 1) * 4, c].unsqueeze(2).to_broadcast([128, 4, 64]), op=ALU.mult),
                                [bSa, bddb], [bSb])
                            S.op("dve", lambda e, pu=pu, Sb_=Sb_: e.tensor_tensor(out=Sb_[:], in0=Sb_[:], in1=psum[pu][:, 0:256], op=ALU.add),
                                 [bSb, B_ps[pu]], [bSb])
            with Scope():
                negm = sb("snegm", [128, 2, 128], F32)
                bnegm = Buf("snegm")
                S.dma("sp", lambda e: e.dma_start(out=negm[:], in_=negm_in.rearrange("d p t -> p d t")), [B_const], [bnegm])
                dskb, bdskb = load_bcast("dskb", dsk_in[l, :], 256, B_const)
                snw, bsnw = load_bcast("snw", snorm_in[l, :], 256, B_const)
                NB = 2
                bT = [sb("sbT%d" % i, [128, 2, 128], BF16) for i in range(NB)]
                bbT = [Buf("sbT") for i in range(NB)]
                cT = [sb("scT%d" % i, [128, 2, 128], BF16) for i in range(NB)]
                bcT = [Buf("scT") for i in range(NB)]
                cA = [sb("scA%d" % i, [128, 2, 128], BF16) for i in range(NB)]
                bcA = [Buf("scA") for i in range(NB)]
                cB = [sb("scB%d" % i, [128, 2, 128], BF16) for i in range(NB)]
                bcB = [Buf("scB") for i in range(NB)]
                for i in range(NB):
                    S.op("pool", lambda e, i=i: e.memset(cA[i][:], 0.0), [], [bcA[i]])
                    S.op("pool", lambda e, i=i: e.memset(cB[i][:], 0.0), [], [bcB[i]])
                xst = [sb("gxs%d" % i, [128, 256], F32) for i in range(NB)]
                bxst = [Buf("gxs") for i in range(NB)]
                zt = [sb("gz%d" % i, [128, 256], F32) for i in range(NB)]
                bzt = [Buf("gz") for i in range(NB)]
                vd = sb("gvd", [128, 2, 256], BF16)
                bvd = Buf("gvd")
                Lt = [sb("gL%d" % i, [128, 128], F32) for i in range(3)]
                bLt = [Buf("gL") for i in range(3)]
                Wt = [sb("gW%d" % i, [128, 128], BF16) for i in range(3)]
                bWt = [Buf("gW") for i in range(3)]
                acc = sb("gacc", [128, 256], F32)
                bacc = Buf("gacc")
                tmp = sb("gtmp", [128, 256], F32)
                btmp = Buf("gtmp")
                ssq = sb("gss", [128, 1], F32)
                bssq = Buf("gss")
                XCTv = XCT.rearrange("(a g p) t -> a p g t", a=2, g=2)
                kL = 0
                for n, tt in enumerate(range(NTILE) if ctx_out else range(2, NTILE)):
                    rows = slice(tt * 128, (tt + 1) * 128)
                    tl = slice(tt * 128, (tt + 1) * 128)
                    i = n % NB
                    S.dma("sp", lambda e, i=i, tl=tl: e.dma_start(out=bT[i][:], in_=XCTv[0][:, :, tl]), [B_XCT], [bbT[i]])
                    S.dma("sp", lambda e, i=i, tl=tl: e.dma_start(out=cT[i][:], in_=XCTv[1][:, :, tl]), [B_XCT], [bcT[i]])
                    S.dma("sp", lambda e, i=i, tt=tt: e.dma_start(out=cA[i][:, :, 0:64], in_=XCTv[1][:, :, tt * 128:tt * 128 + 64]),
                          [B_XCT], [bcA[i]])
                    S.dma("sp", lambda e, i=i, tt=tt: e.dma_start(out=cB[i][:, :, 64:128], in_=XCTv[1][:, :, tt * 128 + 64:tt * 128 + 128]),
                          [B_XCT], [bcB[i]])
                    S.dma("sp", lambda e, i=i, rows=rows: e.dma_start(out=xst[i][:], in_=XS[rows, :]), [B_XS], [bxst[i]])
                    S.dma("sp", lambda e, i=i, rows=rows: e.dma_start(out=zt[i][:], in_=PTOK[rows, T_BZ:T_BZ + 256]), [B_PTOK], [bzt[i]])
                    for dr in range(2):
                        S.op("dve", lambda e, i=i, dr=dr, tt=tt: e.tensor_tensor(
                            out=vd[:, dr, :].rearrange("p (h d) -> p h d", d=64), in0=xst[i][:].rearrange("p (h d) -> p h d", d=64),
                            in1=tokq[:, tt, 0, dr * 4:(dr + 1) * 4].unsqueeze(2).to_broadcast([128, 4, 64]), op=ALU.mult),
                            [bxst[i], btokq], [bvd])
                    for gq in range(2):
                        S.op("pe", lambda e, i=i, gq=gq: e.matmul(psum[0][:, gq * 128:(gq + 1) * 128], lhsT=bT[i][:, gq, :], rhs=cT[i][:, gq, :],
                                                                  start=True, stop=True), [bbT[i], bcT[i]], [B_ps[0]])
                    for h in range(4):
                        for dr in range(2):
                            r = dr * 4 + h
                            gq = h // 2
                            pl = nb(4, 8)
                            S.op("pe", lambda e, pl=pl, r=r, tl=tl: e.matmul(psum[pl][:, 0:128], lhsT=selr[:, r, :], rhs=cumrow[0:8, tl],
                                                                            start=True, stop=False), [bselr, bcum], [B_ps[pl]])
                            S.op("pe", lambda e, pl=pl, dr=dr: e.matmul(psum[pl][:, 0:128], lhsT=ident_f[:], rhs=negm[:, dr, :],
                                                                        start=False, stop=True), [B_identf, bnegm], [B_ps[pl]])
                            L_, bL_, W_, bW_ = Lt[kL % 3], bLt[kL % 3], Wt[kL % 3], bWt[kL % 3]
                            kL += 1
                            S.op("act", lambda e, pl=pl, L_=L_, tt=tt, r=r: e.activation(
                                out=L_[:], in_=psum[pl][:, 0:128], func=AF.Exp, bias=tokq[:, tt, 1, r:r + 1], scale=1.0),
                                [B_ps[pl], btokq], [bL_])
                            S.op("dve", lambda e, L_=L_, W_=W_, gq=gq: e.tensor_tensor(out=W_[:], in0=psum[0][:, gq * 128:(gq + 1) * 128],
                                                                                      in1=L_[:], op=ALU.mult), [B_ps[0], bL_], [bW_])
                            S.op("pe", lambda e, W_=W_, dr=dr, h=h: e.matmul(psum[1][:, h * 64:(h + 1) * 64], lhsT=W_[:],
                                                                             rhs=vd[:, dr, h * 64:(h + 1) * 64], start=(dr == 0), stop=(dr == 1)),
                                 [bW_, bvd], [B_ps[1]])
                            S.op("pe", lambda e, i=i, dr=dr, h=h, gq=gq, tt=tt: e.matmul(
                                psum[2 + dr][:, h * 64:(h + 1) * 64], lhsT=cA[i][:, gq, :], rhs=Sbf[dr][:, 2 * tt, h * 64:(h + 1) * 64],
                                start=True, stop=False), [bcA[i], bSbf[dr]], [B_ps[2 + dr]])
                            S.op("pe", lambda e, i=i, dr=dr, h=h, gq=gq, tt=tt: e.matmul(
                                psum[2 + dr][:, h * 64:(h + 1) * 64], lhsT=cB[i][:, gq, :], rhs=Sbf[dr][:, 2 * tt + 1, h * 64:(h + 1) * 64],
                                start=False, stop=True), [bcB[i], bSbf[dr]], [B_ps[2 + dr]])
                    S.op("act", lambda e: e.copy(out=acc[:], in_=psum[1][:, 0:256]), [B_ps[1]], [bacc])
                    for dr in range(2):
                        S.op("dve", lambda e, dr=dr, tt=tt: e.tensor_tensor(
                            out=tmp[:].rearrange("p (h d) -> p h d", d=64), in0=psum[2 + dr][:, 0:256].rearrange("p (h d) -> p h d", d=64),
                            in1=tokq[:, tt, 2, dr * 4:(dr + 1) * 4].unsqueeze(2).to_broadcast([128, 4, 64]), op=ALU.mult),
                            [B_ps[2 + dr], btokq], [btmp])
                        S.op("pool", lambda e: e.tensor_tensor(out=acc[:], in0=acc[:], in1=tmp[:], op=ALU.add), [bacc, btmp], [bacc])
                    S.op("dve", lambda e, i=i: e.tensor_tensor(out=tmp[:], in0=xst[i][:], in1=dskb[:], op=ALU.mult), [bxst[i], bdskb], [btmp])
                    S.op("pool", lambda e: e.tensor_tensor(out=acc[:], in0=acc[:], in1=tmp[:], op=ALU.add), [bacc, btmp], [bacc])
                    S.op("act", lambda e, i=i: e.activation(out=zt[i][:], in_=zt[i][:], func=AF.Silu), [bzt[i]], [bzt[i]])
                    S.op("pool", lambda e, i=i: e.tensor_tensor(out=acc[:], in0=acc[:], in1=zt[i][:], op=ALU.mult), [bacc, bzt[i]], [bacc])
                    S.op("dve", lambda e: e.tensor_tensor(out=tmp[:], in0=acc[:], in1=acc[:], op=ALU.mult), [bacc], [btmp])
                    S.op("dve", lambda e: e.tensor_reduce(out=ssq[:], in_=tmp[:], axis=AX.X, op=ALU.add), [btmp], [bssq])
                    S.op("act", lambda e: e.activation(out=ssq[:], in_=ssq[:], func=AF.Sqrt, bias=eps_t[:], scale=1.0 / 256), [bssq, B_eps], [bssq])
                    S.op("dve", lambda e: e.reciprocal(out=ssq[:], in_=ssq[:]), [bssq], [bssq])
                    S.op("dve", lambda e: e.tensor_scalar(out=tmp[:], in0=acc[:], scalar1=ssq[:], scalar2=None, op0=ALU.mult), [bacc, bssq], [btmp])
                    S.op("pool", lambda e: e.tensor_tensor(out=tmp[:], in0=tmp[:], in1=snw[:], op=ALU.mult), [btmp, bsnw], [btmp])
                    S.dma("pool", lambda e, rows=rows: e.dma_start(out=BR[rows, 256:512], in_=tmp[:]), [btmp], [B_BR])

        def run_stage(name, fn):
            with Scope():
                fn()
            return stop_after == name

        if "br_init" in debug:
            br_init = dram_in("br_init", [NT, 512])
            S.dma("sp", lambda e: e.dma_start(out=BR[:, 0:512], in_=br_init), [B_const], [B_BR])
        for l in range(n_layers):
            last = (l == DEPTH - 1)
            ctx_out = not last
            tiles = list(range(NTILE)) if ctx_out else list(range(2, NTILE))
            xsrc, bxsrc = (xin, B_xin) if l == 0 else (XRES, B_XRES)
            if run_stage("mod", lambda: stage_mod(l)):
                break
            if "inproj" not in skip:
                mark = sbtop[0]
                hT = sb("hT", [128, 8, NT], BF16)
                B_hT = [Buf("hT%d" % i) for i in range(NTILE)]
                with Scope():
                    stage_ln_mod(l, xsrc, bxsrc, 0, 1, hT, B_hT, list(range(NTILE)))
                with Scope():
                    stage_inproj(l, hT, B_hT)
                sbtop[0] = mark
            if stop_after == "inproj":
                break
            if "gqa" not in skip and run_stage("gqa", lambda: stage_gqa(l, ctx_out)):
                break
            if "na" not in skip and run_stage("na", lambda: stage_na(l, ctx_out)):
                break
            if "hgrn" not in skip and run_stage("hgrn", lambda: stage_hgrn(l, ctx_out)):
                break
            if "ssd" not in skip:
                with Scope():
                    stage_ssd(l, ctx_out)
                if stop_after == "ssd":
                    break
            if stop_after == "mixers":
                break
            if run_stage("merge", lambda: stage_merge(l, tiles, xsrc, bxsrc)):
                break
            mark = sbtop[0]
            hT = sb("h2T", [128, 8, NT], BF16)
            B_hT = [Buf("h2T%d" % i) for i in range(NTILE)]
            with Scope():
                stage_ln_mod(l, X1, B_X1, 3, 4, hT, B_hT, tiles)
            with Scope():
                stage_ffn_up(l, hT, B_hT, tiles[0] * 128)
            sbtop[0] = mark
            if run_stage("ffn", lambda: stage_ffn_down(l, tiles, last)):
                break
        dbg_bufs = {"MODD": B_MODD, "PT": B_PT, "PTOK": B_PTOK, "XRES": B_XRES, "X1": B_X1, "BR": B_BR,
                    "XS": B_XS, "XCT": B_XCT, "BTOK": B_BTOK, "ACTT": B_ACTT}
        for nm, (ap_, b_) in DBG.items():
            dbg_bufs[nm] = b_
        wait_bufs = [dbg_bufs[n] for n in debug if n in dbg_bufs] + [B_out]
        S.final_wait("sp", [b for b in wait_bufs if b.w is not None])
        if debug:
            print('instr counts', {e: len(v) for e, v in S.lists.items()}, 'dma sems', S.nsem, 'max dma sem val', max([t[1] for t in S.dsems.values()] + [0]), sorted([t[1] for t in S.dsems.values()])[-5:])
        S.emit()
    return nc


def build_natab(rpb):
    tab = np.full((128, NA_NBLK, 4, 64), NEG, np.float32)
    qc = np.arange(64)
    win0 = np.clip(qc - 8, 0, 48)
    kc = np.arange(64)
    inwin = (kc[:, None] >= win0[None, :]) & (kc[:, None] < win0[None, :] + 16)
    dcol = np.clip(kc[:, None] - qc[None, :] + 15, 0, 30)
    for key, bi in NA_BLOCKS.items():
        for half in range(2):
            valid, dr = key[half]
            if not valid:
                continue
            vals = rpb[:, dr + 7, :][:, dcol]
            blk = np.where(inwin[None], vals, np.float32(NEG))
            tab[half * 64:(half + 1) * 64, bi] = blk.transpose(1, 0, 2)
    return tab.reshape(128, NA_NBLK * 256)


def rope_tables():
    t = np.arange(NLAT)
    nf = 16
    inv_freq = (10000.0 ** (-np.arange(nf, dtype=np.float32) / nf)).astype(np.float32)
    row = (t // 64).astype(np.float32)[:, None] * inv_freq
    col = (t % 64).astype(np.float32)[:, None] * inv_freq
    C = np.concatenate([np.cos(row), np.cos(row), np.cos(col), np.cos(col)], axis=1)
    Sn = np.concatenate([-np.sin(row), np.sin(row), -np.sin(col), np.sin(col)], axis=1)
    cs = np.zeros((NT, 128), np.float32)
    cs[:NCTX, 0:64] = 1.0
    cs[NCTX:, 0:64] = C
    cs[NCTX:, 64:128] = Sn
    return cs


def make_consts():
    si = np.arange(128)[:, None]
    ti = np.arange(128)[None, :]
    same = (si // 64) == (ti // 64)
    same32 = (si // 32) == (ti // 32)
    hmask2 = np.stack([same & (si <= ti), same & (si >= ti)]).astype(np.float32)
    hmask = np.stack([same32 & (si <= ti), same32 & (si >= ti),
                      same & (si % 64 < 32) & (ti % 64 >= 32), same & (si % 64 >= 32) & (ti % 64 < 32)]).astype(np.float32)
    selr = np.zeros((8, 8, 128), np.float32)
    for r in range(8):
        selr[r, r, :] = 1.0
    negm = np.where(hmask2 > 0, 0.0, NEG).astype(np.float32)
    return {"ident": np.eye(128, dtype=np.float32), "rope_cs": rope_tables(), "hmask": hmask, "selr": selr, "negm": negm}


def prep_inputs(inputs):
    x, c, ctx, c_ctx = inputs["x"], inputs["c"], inputs["ctx"], inputs["c_ctx"]
    w_in = inputs["w_in"]
    fm_idx = np.concatenate([np.arange(O_AQ, O_AQ + 768), np.arange(O_XBC, O_XBC + 768),
                             np.arange(O_CQ, O_CQ + 512), np.arange(O_DTF, O_DTF + 8)])
    w_fm = np.zeros((DEPTH, D, FM_COLS + 128), np.float32)
    w_fm[:, :, :FM_COLS + 8] = w_in[:, :, fm_idx]
    tm_idx = np.concatenate([np.arange(O_AV, O_AV + 512), np.arange(O_BZ, O_BZ + 256), np.arange(O_CV, O_CV + 256),
                             np.arange(O_DQ, O_DQ + 512), np.arange(O_GATE, O_GATE + 4096)])
    w_tm = np.ascontiguousarray(w_in[:, :, tm_idx])
    wu = inputs["ffn_w_up"]
    w_up = np.stack([wu[:, :, :FH].reshape(DEPTH, D, FH // 128, 128), wu[:, :, FH:].reshape(DEPTH, D, FH // 128, 128)],
                    axis=3).reshape(DEPTH, D, 2 * FH)
    lnp = np.stack([inputs["ln1_g"], inputs["ln1_b"], inputs["ln2_g"], inputs["ln2_b"]], axis=1)
    qkw = np.concatenate([np.tile(inputs["q_norm"], (1, 4)), np.tile(inputs["k_norm"], (1, 2))], axis=1)
    natab = np.stack([build_natab(inputs["na_rpb"][l]) for l in range(DEPTH)], axis=0)
    shared = dict(ada_w=np.ascontiguousarray(inputs["ada_w"]), ada_b=np.ascontiguousarray(inputs["ada_b"]),
                  w_fm=w_fm, w_tm=w_tm, w_br=np.ascontiguousarray(inputs["w_branch"].reshape(DEPTH, D, D)),
                  w_o=np.ascontiguousarray(inputs["w_out"]), w_up=np.ascontiguousarray(w_up),
                  w_dn=np.ascontiguousarray(inputs["ffn_w_down"]), lnp=np.ascontiguousarray(lnp.astype(np.float32)),
                  qkw=np.ascontiguousarray(qkw.astype(np.float32)), natab=np.ascontiguousarray(natab),
                  lbp=np.ascontiguousarray(inputs["hgrn_lb"].reshape(2, DEPTH, 2, 128).transpose(3, 0, 1, 2).reshape(128, 8).astype(np.float32)),
                  hnorm=np.ascontiguousarray(inputs["hgrn_norm"].astype(np.float32)),
                  convp=np.ascontiguousarray(np.concatenate([inputs["ssd_conv_w"], inputs["ssd_conv_b"][:, None, :]], axis=1)
                                             .reshape(DEPTH, 6, 6, 128).transpose(0, 3, 2, 1).astype(np.float32)),
                  dtp=np.ascontiguousarray(np.stack([inputs["ssd_dt_bias"].reshape(DEPTH, 8), inputs["ssd_a_log"].reshape(DEPTH, 8),
                                                     np.tile(np.array([1, 1, 1, 1, 0, 0, 0, 0], np.float32), (DEPTH, 1)),
                                                     np.tile(np.array([0, 0, 0, 0, 1, 1, 1, 1], np.float32), (DEPTH, 1))], axis=2).astype(np.float32)),
                  dsk=np.ascontiguousarray(np.repeat(inputs["ssd_d"], 64, axis=1).astype(np.float32)),
                  snorm=np.ascontiguousarray(inputs["ssd_norm"].astype(np.float32)))
    shared.update(make_consts())
    in_maps = []
    for b in range(x.shape[0]):
        m = dict(shared)
        m["xin"] = np.ascontiguousarray(np.concatenate([ctx[b], x[b]], axis=0))
        cv = np.concatenate([c[b].reshape(8, 128).T, c_ctx.reshape(8, 128).T], axis=1)
        m["cvec"] = np.ascontiguousarray(cv.astype(np.float32))
        in_maps.append(m)
    return in_maps


def kernel(**inputs):
    inputs = {k: np.asarray(v) for k, v in inputs.items()}
    in_maps = prep_inputs(inputs)
    nc = build_program()
    res = run_bass_kernel_spmd(nc, in_maps, core_ids=list(range(len(in_maps))))
    return np.stack([r["out"] for r in res.results], axis=0)
```
